# Optimizing a Trainium2 kernel written in Bass

```python
import math
import jax, jax.numpy as jnp
from jax import lax
import numpy as np

D_MODEL = 2048
BATCH = 16
SEQ = 256
DEPTH = 1
DEC_BATCH = 4
DEC_SEQ = 1024
PAST_LEN = 256

GRID_W = 64
H_A = 8
QK_A = 64
V_A = 2 * QK_A
H_B = 8
DK_B = 128
DV_B = 128
CONV_W = 3
CHUNK = 64
Q_BLOCK = 128
D_FF = 5632
ROPE_THETA = 10000.0
EPS = 1e-6
ATTN_Q = H_A * 2 * QK_A
ATTN_V = H_A * V_A
DN_QK = H_B * DK_B
DN_V = H_B * DV_B
DN_QKV = 2 * DN_QK + DN_V
MIX_W = ATTN_V + DN_V
N_IN = 2 * ATTN_Q + ATTN_V + DN_QKV + DN_V + 4 * H_B
SPLIT_IDX = [ATTN_Q, 2 * ATTN_Q, 2 * ATTN_Q + ATTN_V, 2 * ATTN_Q + ATTN_V + DN_QKV,
             2 * ATTN_Q + ATTN_V + DN_QKV + DN_V, 2 * ATTN_Q + ATTN_V + DN_QKV + DN_V + 2 * H_B]

kernel_name = "hymba_diffattn_gdn_prefix_dit"


def rmsnorm(x, g):
    xf = x.astype(jnp.float32)
    y = xf * lax.rsqrt(jnp.mean(xf * xf, axis=-1, keepdims=True) + EPS)
    return (y * g.astype(jnp.float32)).astype(x.dtype)


def l2norm(x):
    xf = x.astype(jnp.float32)
    return xf * lax.rsqrt(jnp.sum(xf * xf, axis=-1, keepdims=True) + EPS)


def modulation(cvec, w_mod, b_mod):
    m = jax.nn.silu(cvec) @ w_mod + b_mod
    return jnp.split(m[..., None, :], 6, axis=-1)


def dwconv_centred(x, w):
    k = w.shape[0]
    p = k // 2
    t = x.shape[1]
    xp = jnp.pad(x, ((0, 0), (p, p), (0, 0)))
    return sum(xp[:, i:i + t] * w[i] for i in range(k))


def axial_angles(t):
    rows = t // GRID_W
    r = jnp.repeat(jnp.arange(rows, dtype=jnp.float32), GRID_W)
    col = jnp.tile(jnp.arange(GRID_W, dtype=jnp.float32), rows)
    half = QK_A // 2
    inv = ROPE_THETA ** (-jnp.arange(0, half, 2, dtype=jnp.float32) / half)
    return r[:, None] * inv, col[:, None] * inv


def rotate_half_pairs(x, ang):
    x1, x2 = jnp.split(x, 2, axis=-1)
    cos, sin = jnp.cos(ang), jnp.sin(ang)
    return jnp.concatenate([x1 * cos - x2 * sin, x1 * sin + x2 * cos], axis=-1)


def apply_axial_rope(x, ang_r, ang_c):
    xf = x.astype(jnp.float32)
    xr, xc = jnp.split(xf, 2, axis=-1)
    bc = lambda a: a[None, :, None, None, :]
    return jnp.concatenate([rotate_half_pairs(xr, bc(ang_r)), rotate_half_pairs(xc, bc(ang_c))], axis=-1).astype(x.dtype)


def diff_attention(q1, q2, k1, k2, v, lam):
    b, t, h, d = q1.shape
    nb = t // Q_BLOCK
    scale = d ** -0.5
    k1f, k2f, vf = k1.astype(jnp.float32), k2.astype(jnp.float32), v.astype(jnp.float32)

    def block(qs):
        qa, qb = qs
        s1 = jnp.einsum('bqhd,bkhd->bhqk', qa.astype(jnp.float32), k1f) * scale
        s2 = jnp.einsum('bqhd,bkhd->bhqk', qb.astype(jnp.float32), k2f) * scale
        p = jax.nn.softmax(s1, axis=-1) - lam * jax.nn.softmax(s2, axis=-1)
        return jnp.einsum('bhqk,bkhd->bqhd', p, vf)

    split = lambda x: jnp.moveaxis(x.reshape(b, nb, Q_BLOCK, h, d), 1, 0)
    o = lax.map(block, (split(q1), split(q2)))
    return jnp.moveaxis(o, 0, 1).reshape(b, t, h, v.shape[-1])


def gated_delta_chunked(q, k, v, g, beta, s0):
    b, t, h, dk = q.shape
    dv = v.shape[-1]
    n = t // CHUNK

    def chunks(x):
        return jnp.moveaxis(x.reshape((b, n, CHUNK, h) + x.shape[3:]), 3, 1)

    q, k, v, g, beta = chunks(q), chunks(k), chunks(v), chunks(g), chunks(beta)
    gc = jnp.cumsum(g, axis=-1)
    incl = jnp.tril(jnp.ones((CHUNK, CHUNK), bool))
    strict = jnp.tril(jnp.ones((CHUNK, CHUNK), bool), -1)
    gamma = jnp.exp(jnp.where(incl, gc[..., :, None] - gc[..., None, :], -jnp.inf))
    kb = k * beta[..., None]
    a_mat = jnp.where(strict, jnp.einsum('bhncd,bhnsd->bhncs', kb, k) * gamma, 0.0)
    t_sys = a_mat + jnp.eye(CHUNK, dtype=jnp.float32)
    u = lax.linalg.triangular_solve(t_sys, v * beta[..., None], left_side=True, lower=True, unit_diagonal=True)
    w = lax.linalg.triangular_solve(t_sys, kb * jnp.exp(gc)[..., None], left_side=True, lower=True, unit_diagonal=True)
    qk = jnp.einsum('bhncd,bhnsd->bhncs', q, k) * gamma
    qg = q * jnp.exp(gc)[..., None]
    g_last = gc[..., -1]
    kd = k * jnp.exp(g_last[..., None] - gc)[..., None]
    dl = jnp.exp(g_last)

    def step(s, xs):
        u_n, w_n, qg_n, qk_n, kd_n, dl_n = xs
        v_new = u_n - jnp.einsum('bhcd,bhde->bhce', w_n, s)
        o_n = jnp.einsum('bhcd,bhde->bhce', qg_n, s) + jnp.einsum('bhcs,bhse->bhce', qk_n, v_new)
        s = s * dl_n[..., None, None] + jnp.einsum('bhcd,bhce->bhde', kd_n, v_new)
        return s, o_n

    xs = tuple(jnp.moveaxis(x, 2, 0) for x in (u, w, qg, qk, kd, dl))
    s_final, o = lax.scan(step, s0.astype(jnp.float32), xs)
    o = jnp.moveaxis(jnp.moveaxis(o, 0, 2), 1, 3).reshape(b, t, h, dv)
    return o, s_final


def bidir_delta(q, k, v, g, beta, s_f0, s_b0):
    flip = lambda x: jnp.flip(x, axis=1)
    o_f, s_f = gated_delta_chunked(q, k, v, g[:, :, 0], beta[:, :, 0], s_f0)
    o_b, s_b = gated_delta_chunked(flip(q), flip(k), flip(v), flip(g[:, :, 1]), flip(beta[:, :, 1]), s_b0)
    return o_f + flip(o_b), s_f, s_b


def mixer(h, w_in, conv_qkv_w, lam, lam_init, subln_g, a_log, dt_bias, dn_norm_g, w_out, ctx):
    b, t, _ = h.shape
    z = h @ w_in
    aq, ak, av, dqkv, dg, db, da = jnp.split(z, SPLIT_IDX, axis=-1)
    q = aq.reshape(b, t, H_A, 2, QK_A)
    k = ak.reshape(b, t, H_A, 2, QK_A)
    v = av.reshape(b, t, H_A, V_A)
    if ctx is None:
        keys, vals = k, v
        s_f0 = jnp.zeros((b, H_B, DK_B, DV_B), jnp.float32)
        s_b0 = s_f0
    else:
        ctx_k, ctx_v, s_f0, s_b0 = ctx
        ang_r, ang_c = axial_angles(t)
        q = apply_axial_rope(q, ang_r, ang_c)
        k = apply_axial_rope(k, ang_r, ang_c)
        keys = jnp.concatenate([k, ctx_k.reshape(b, ctx_k.shape[1], H_A, 2, QK_A).astype(k.dtype)], axis=1)
        vals = jnp.concatenate([v, ctx_v.astype(v.dtype)], axis=1)
    o_a = diff_attention(q[..., 0, :], q[..., 1, :], keys[..., 0, :], keys[..., 1, :], vals, lam)
    o_a = rmsnorm(o_a, subln_g) * (1.0 - lam_init)

    qkv = jax.nn.silu(dwconv_centred(dqkv, conv_qkv_w))
    dq, dk_, dv_ = jnp.split(qkv, [DN_QK, 2 * DN_QK], axis=-1)
    qd = l2norm(dq.reshape(b, t, H_B, DK_B)) * (DK_B ** -0.5)
    kd = l2norm(dk_.reshape(b, t, H_B, DK_B))
    vd = dv_.reshape(b, t, H_B, DV_B).astype(jnp.float32)
    beta = jax.nn.sigmoid(db.reshape(b, t, 2, H_B).astype(jnp.float32))
    gdec = -jnp.exp(a_log.astype(jnp.float32)) * jax.nn.softplus(
        da.reshape(b, t, 2, H_B).astype(jnp.float32) + dt_bias.astype(jnp.float32))
    o_d, s_f, s_b = bidir_delta(qd, kd, vd, gdec, beta, s_f0, s_b0)
    o_d = rmsnorm(o_d, dn_norm_g) * jax.nn.silu(dg.reshape(b, t, H_B, DV_B).astype(jnp.float32))

    o = jnp.concatenate([o_a.reshape(b, t, ATTN_V), o_d.reshape(b, t, DN_V)], axis=-1).astype(h.dtype)
    out = o @ w_out
    if ctx is None:
        return out, (ak.reshape(b, t, H_A, 2 * QK_A), v, s_f, s_b)
    return out, None


def conv_ffn(h, w_up, conv_w, w_down):
    u = dwconv_centred(h @ w_up, conv_w)
    gate, val = jnp.split(u, 2, axis=-1)
    return (jax.nn.silu(gate) * val) @ w_down


def trunk_layer(x, cvec, l, mod_w, mod_b, norm_mix_g, w_in, conv_qkv_w, lambda_q1, lambda_k1, lambda_q2,
                lambda_k2, subln_g, a_log, dt_bias, dn_norm_g, w_out, norm_ffn_g, w_up, conv_ffn_w, w_down, ctx):
    sh1, sc1, g1, sh2, sc2, g2 = modulation(cvec, mod_w[l], mod_b[l])
    lam_init = 0.8 - 0.6 * math.exp(-0.3 * l)
    lam = (jnp.exp(jnp.sum(lambda_q1[l].astype(jnp.float32) * lambda_k1[l].astype(jnp.float32)))
           - jnp.exp(jnp.sum(lambda_q2[l].astype(jnp.float32) * lambda_k2[l].astype(jnp.float32))) + lam_init)
    h = rmsnorm(x, norm_mix_g[l]) * (1 + sc1) + sh1
    m, new_ctx = mixer(h, w_in[l], conv_qkv_w[l], lam, lam_init, subln_g[l], a_log[l], dt_bias[l],
                       dn_norm_g[l], w_out[l], ctx)
    x = x + g1 * m
    h = rmsnorm(x, norm_ffn_g[l]) * (1 + sc2) + sh2
    x = x + g2 * conv_ffn(h, w_up[l], conv_ffn_w[l], w_down[l])
    return x, new_ctx


def setup_inputs(seed: int = 0) -> dict:
    key = jax.random.key(seed)
    ks = jax.random.split(key, 32)
    nrm = lambda i, shape, s=1.0: jax.random.normal(ks[i], shape, jnp.float32) * s
    centre = jnp.zeros((CONV_W, 1), jnp.float32).at[CONV_W // 2].set(1.0)
    dt = jnp.exp(jax.random.uniform(ks[26], (DEPTH, 2, H_B), jnp.float32, math.log(1e-3), math.log(1e-1)))
    return {
        "x_prompt": nrm(0, (BATCH, SEQ, D_MODEL)),
        "x_sample": nrm(1, (DEC_BATCH, DEC_SEQ, D_MODEL)),
        "cache_k": nrm(2, (DEC_BATCH, DEPTH, PAST_LEN, H_A, 2 * QK_A)),
        "cache_v": nrm(3, (DEC_BATCH, DEPTH, PAST_LEN, H_A, V_A)),
        "state_fwd": nrm(4, (DEC_BATCH, DEPTH, H_B, DK_B, DV_B), 0.1),
        "state_bwd": nrm(5, (DEC_BATCH, DEPTH, H_B, DK_B, DV_B), 0.1),
        "c": nrm(6, (DEC_BATCH, D_MODEL)),
        "c_ctx": nrm(7, (D_MODEL,)),
        "mod_w": nrm(8, (DEPTH, D_MODEL, 6 * D_MODEL), 0.5 * D_MODEL ** -0.5),
        "mod_b": nrm(9, (DEPTH, 6 * D_MODEL), 0.01),
        "norm_mix_g": 1.0 + nrm(10, (DEPTH, D_MODEL), 0.01),
        "w_in": nrm(11, (DEPTH, D_MODEL, N_IN), D_MODEL ** -0.5),
        "conv_qkv_w": centre + nrm(12, (DEPTH, CONV_W, DN_QKV), 0.2),
        "lambda_q1": nrm(13, (DEPTH, QK_A), 0.1),
        "lambda_k1": nrm(14, (DEPTH, QK_A), 0.1),
        "lambda_q2": nrm(15, (DEPTH, QK_A), 0.1),
        "lambda_k2": nrm(16, (DEPTH, QK_A), 0.1),
        "subln_g": 1.0 + nrm(17, (DEPTH, V_A), 0.01),
        "a_log": jnp.log(jax.random.uniform(ks[18], (DEPTH, 2, H_B), jnp.float32, 1.0, 16.0)),
        "dt_bias": dt + jnp.log(-jnp.expm1(-dt)),
        "dn_norm_g": 1.0 + nrm(19, (DEPTH, DV_B), 0.01),
        "w_out": nrm(20, (DEPTH, MIX_W, D_MODEL), MIX_W ** -0.5),
        "norm_ffn_g": 1.0 + nrm(21, (DEPTH, D_MODEL), 0.01),
        "w_up": nrm(22, (DEPTH, D_MODEL, 2 * D_FF), D_MODEL ** -0.5),
        "conv_ffn_w": centre + nrm(23, (DEPTH, CONV_W, 2 * D_FF), 0.2),
        "w_down": nrm(24, (DEPTH, D_FF, D_MODEL), D_FF ** -0.5),
        "final_g": 1.0 + nrm(25, (D_MODEL,), 0.01),
    }


def reference(x_prompt, x_sample, cache_k, cache_v, state_fwd, state_bwd, c, c_ctx, mod_w, mod_b, norm_mix_g,
              w_in, conv_qkv_w, lambda_q1, lambda_k1, lambda_q2, lambda_k2, subln_g, a_log, dt_bias, dn_norm_g,
              w_out, norm_ffn_g, w_up, conv_ffn_w, w_down, final_g):
    xp, xs = x_prompt, x_sample
    new_k, new_v, new_sf, new_sb = [], [], [], []
    for l in range(DEPTH):
        xp, (k_l, v_l, sf_l, sb_l) = trunk_layer(
            xp, c_ctx, l, mod_w, mod_b, norm_mix_g, w_in, conv_qkv_w, lambda_q1, lambda_k1, lambda_q2, lambda_k2,
            subln_g, a_log, dt_bias, dn_norm_g, w_out, norm_ffn_g, w_up, conv_ffn_w, w_down, None)
        new_k.append(k_l)
        new_v.append(v_l)
        new_sf.append(sf_l)
        new_sb.append(sb_l)
        xs, _ = trunk_layer(
            xs, c, l, mod_w, mod_b, norm_mix_g, w_in, conv_qkv_w, lambda_q1, lambda_k1, lambda_q2, lambda_k2,
            subln_g, a_log, dt_bias, dn_norm_g, w_out, norm_ffn_g, w_up, conv_ffn_w, w_down,
            (cache_k[:, l], cache_v[:, l], state_fwd[:, l], state_bwd[:, l]))
    y_prompt = rmsnorm(xp, final_g)
    y_sample = rmsnorm(xs, final_g)
    new_cache_k = jnp.stack(new_k, axis=1)
    new_cache_v = jnp.stack(new_v, axis=1)
    new_state_fwd = jnp.stack(new_sf, axis=1)
    new_state_bwd = jnp.stack(new_sb, axis=1)
    return (y_prompt, y_sample, new_cache_k, new_cache_v, new_state_fwd, new_state_bwd)
```

```python
import os
import math
import numpy as np
from contextlib import ExitStack
import concourse.bass as bass
import concourse.mybir as mybir
from concourse.bass_utils import run_bass_kernel_spmd

F32 = mybir.dt.float32
BF16 = mybir.dt.bfloat16
ALU = mybir.AluOpType
AF = mybir.ActivationFunctionType

D = 2048
NCORES = 8
EPS = 1e-6
DFF = 5632
LAM_INIT = 0.8 - 0.6 * math.exp(0.0)


class _RecEngine:
    def __getattr__(self, name):
        return lambda *a, **k: (name, a, k)


class Dep:
    __slots__ = ("w", "r", "excl")

    def __init__(self, excl=False):
        self.w = None
        self.r = {}
        self.excl = excl


class Sched:
    NDS = 24

    def __init__(self, nc, stack):
        self.nc = nc
        self.engs = {"pe": nc.tensor, "act": nc.scalar, "dve": nc.vector, "pool": nc.gpsimd, "sp": nc.sync}
        self.sem = {k: stack.enter_context(nc.semaphore("s_" + k)) for k in self.engs}
        self.cnt = {k: 0 for k in self.engs}
        self.seen = {k: {} for k in self.engs}
        self.dsem = [stack.enter_context(nc.semaphore("d%d" % i)) for i in range(self.NDS)]
        self.dval = [0] * self.NDS
        self.dnext = 0
        self.dnext_sw = 0
        self.nops = {k: 0 for k in self.engs}
        self.nwaits = 0
        self.trace = {k: [] for k in self.engs}
        self.marks = []
        self.rec = None

    def _wait(self, eng, ev):
        if ev is None:
            return
        key, sem, val = ev
        if self.seen[eng].get(key, 0) >= val:
            return
        if key == eng and val > self.cnt[eng]:
            return
        self.engs[eng].wait_ge(sem, val)
        self.trace[eng].append(("wait", key, val))
        self.nwaits += 1
        self.seen[eng][key] = val

    def _deps(self, eng, r, w):
        for d in r:
            self._wait(eng, d.w)
        for d in w:
            self._wait(eng, d.w)
            for ev in list(d.r.values()):
                self._wait(eng, ev)

    def _record(self, ev, r, w):
        for d in r:
            d.r[ev[0]] = ev
        for d in w:
            d.w = ev
            d.r = {}

    def op(self, eng, fn, r=(), w=(), signal=True):
        if self.rec is not None:
            name, a, k = fn(_RecEngine())
            self.rec.append(lambda: self.op(eng, lambda e: getattr(e, name)(*a, **k), r, w, signal))
            return None
        if any(d.excl for d in r):
            w = list(w) + [d for d in r if d.excl]
            r = [d for d in r if not d.excl]
        self._deps(eng, r, w)
        ins = fn(self.engs[eng])
        self.nops[eng] += 1
        if signal:
            self.cnt[eng] += 1
            ins.then_inc(self.sem[eng], 1)
            self.trace[eng].append(("inc", eng, 1))
            ev = (eng, self.sem[eng], self.cnt[eng])
        else:
            ev = (eng, self.sem[eng], self.cnt[eng] + 1)
        self._record(ev, r, w)
        return ins

    def dma(self, q, out, in_, r=(), w=(), evlist=None):
        if self.rec is not None:
            self.rec.append(lambda: self.dma(q, out, in_, r, w, evlist))
            return None
        half = self.NDS // 2
        if q == "pool":
            i = half + self.dnext_sw
            self.dnext_sw = (self.dnext_sw + 1) % half
        else:
            i = self.dnext
            self.dnext = (i + 1) % half
        key = "d%d" % i
        if self.dval[i] > 0:
            self._wait(q, (key, self.dsem[i], self.dval[i]))
        self._deps(q, r, w)
        ins = self.engs[q].dma_start(out=out, in_=in_)
        self.nops[q] += 1
        self.dval[i] += 16
        ins.then_inc(self.dsem[i], 16)
        self.trace[q].append(("inc", key, 16))
        ev = (key, self.dsem[i], self.dval[i])
        self._record(ev, r, w)
        if evlist is not None:
            evlist.append(ev)
        return ev

    def mark(self, label):
        if self.rec is not None:
            self.rec.append(lambda: self.mark(label))
            return
        self.marks.append((label, dict(self.nops)))

    def simulate(self):
        val = {}
        pc = {k: 0 for k in self.engs}
        progress = True
        while progress:
            progress = False
            for k in self.engs:
                tr = self.trace[k]
                while pc[k] < len(tr):
                    kind, key, v = tr[pc[k]]
                    if kind == "wait":
                        if val.get(key, 0) >= v:
                            pc[k] += 1
                            progress = True
                        else:
                            break
                    else:
                        val[key] = val.get(key, 0) + v
                        pc[k] += 1
                        progress = True
        stuck = {k: (pc[k], len(self.trace[k]), self.trace[k][pc[k]] if pc[k] < len(self.trace[k]) else None) for k in self.engs}
        ok = all(pc[k] == len(self.trace[k]) for k in self.engs)
        return ok, stuck, val

    def barrier(self):
        if self.rec is not None:
            self.rec.append(self.barrier)
            return
        self._barrier()

    def _barrier(self):
        evs = [(k, self.sem[k], self.cnt[k]) for k in self.engs if self.cnt[k] > 0]
        evs += [("d%d" % i, self.dsem[i], self.dval[i]) for i in range(self.NDS) if self.dval[i] > 0]
        for eng in self.engs:
            for ev in evs:
                self._wait(eng, ev)


class _Stop(Exception):
    pass


class WStream:
    def __init__(self, S, bufs, srcs, depth):
        self.S, self.bufs, self.srcs, self.depth = S, bufs, srcs, min(depth, len(bufs) - 1)
        self.i_issue = 0
        self.i_use = 0

    def prefetch(self, ahead=None):
        ahead = self.depth if ahead is None else min(ahead, len(self.bufs) - 1)
        while self.i_issue < min(len(self.srcs), self.i_use + ahead + 1):
            buf, d = self.bufs[self.i_issue % len(self.bufs)]
            self.S.dma("pool", buf[:], self.srcs[self.i_issue], w=[d])
            self.i_issue += 1

    def get(self):
        self.prefetch()
        buf = self.bufs[self.i_use % len(self.bufs)]
        self.i_use += 1
        return buf


class Arena:
    WORDS = 53200

    def __init__(self, nc, stack):
        self.t = stack.enter_context(nc.sbuf_tensor("arena", [128, self.WORDS], F32))
        self.ptr = 0
        self.peak = 0
        self.norelease = False

    def alloc(self, shape, dt=F32):
        n = 1
        for x in shape[1:]:
            n *= int(x)
        words = n if dt == F32 else (n + 1) // 2
        words = (words + 7) // 8 * 8
        off = self.ptr
        self.ptr += words
        self.peak = max(self.peak, self.ptr)
        assert self.ptr <= self.WORDS, ("SBUF arena overflow", self.ptr)
        ap = self.t[:, off:off + words]
        if dt != F32:
            ap = ap.bitcast(dt)
        ap = ap[:, 0:n]
        if len(shape) == 3:
            ap = ap.rearrange("p (a b) -> p a b", b=int(shape[2]))
        elif len(shape) == 4:
            ap = ap.rearrange("p (a b c) -> p a b c", b=int(shape[2]), c=int(shape[3]))
        return ap

    def scope(self):
        arena = self

        class _Scope:
            def __enter__(self_):
                self_.mark = arena.ptr
                return self_

            def __exit__(self_, *a):
                if not arena.norelease:
                    arena.ptr = self_.mark
                return False
        return _Scope()


def _tile_w(w, ncol_blk):
    K, N = w.shape
    return np.ascontiguousarray(w.reshape(K // 128, 128, N // ncol_blk, ncol_blk).transpose(2, 1, 0, 3))


def _consts():
    i = np.arange(128)
    blk = (i[:, None] // 64) == (i[None, :] // 64)
    c = {}
    c["ident"] = np.eye(128)
    c["ones"] = np.ones((128, 128))
    c["ublk"] = blk & (i[:, None] <= i[None, :])
    c["lblk"] = blk & (i[:, None] >= i[None, :])
    c["slblk"] = blk & (i[:, None] > i[None, :])
    c["sublk"] = blk & (i[:, None] < i[None, :])
    c["eblk"] = blk
    c["e0"] = np.broadcast_to((i[:, None] < 64), (128, 128))
    c["e1"] = np.broadcast_to((i[:, None] >= 64), (128, 128))
    d = i % 64
    ii = d % 32
    partner = np.where(ii < 16, i + 16, i - 16)
    prope = np.zeros((128, 128))
    prope[partner, i] = 1.0
    c["prope"] = prope
    names = ["ident", "ones", "ublk", "lblk", "slblk", "sublk", "eblk", "e0", "e1", "prope"]
    return np.ascontiguousarray(np.stack([c[n].astype(np.float32) for n in names], axis=1)), names


def _rope_tables(flip):
    t = np.arange(1024)
    if flip:
        t = t[::-1]
    rows = (t // 64).astype(np.float64)
    cols = (t % 64).astype(np.float64)
    inv = 10000.0 ** (-np.arange(0, 32, 2, dtype=np.float64) / 32.0)
    p = np.arange(128)
    d = p % 64
    half = d // 32
    ii = d % 32
    f = ii % 16
    pos = np.where(half[:, None] == 0, rows[None, :], cols[None, :])
    ang = pos * inv[f][:, None]
    cos = np.cos(ang)
    sin = np.sin(ang) * np.where(ii < 16, -1.0, 1.0)[:, None]
    return np.ascontiguousarray(np.stack([cos, sin], axis=1).astype(np.float32))


def _prep(inp):
    f32 = lambda a: np.ascontiguousarray(np.asarray(a, dtype=np.float32))
    w_in = f32(inp["w_in"])[0]
    heads = []
    for h in range(8):
        cols = np.concatenate([np.arange(o + h * 128, o + (h + 1) * 128)
                               for o in (0, 1024, 2048, 3072, 4096, 5120, 6144)])
        heads.append(_tile_w(w_in[:, cols], 128))
    WIN = np.ascontiguousarray(np.stack(heads, 0))
    MODW = _tile_w(f32(inp["mod_w"])[0], 512)
    MODB = f32(inp["mod_b"]).reshape(24, 512)
    WOUT = _tile_w(f32(inp["w_out"])[0], 512)
    w_up = f32(inp["w_up"])[0]
    upcols = np.concatenate([np.concatenate([np.arange(2 * r * 128, (2 * r + 2) * 128),
                                             np.arange(DFF + 2 * r * 128, DFF + (2 * r + 2) * 128)])
                             for r in range(22)])
    WUP = _tile_w(w_up[:, upcols], 512)
    WDOWN = np.ascontiguousarray(f32(inp["w_down"])[0].reshape(11, 4, 128, 4, 512).transpose(0, 3, 2, 1, 4))
    fm = lambda v: np.ascontiguousarray(f32(v).reshape(16, 128).T)
    GMIX, GFFN = fm(inp["norm_mix_g"]), fm(inp["norm_ffn_g"])
    FINALG = f32(inp["final_g"]).reshape(1, 2048)
    LAMV = np.concatenate([f32(inp[k]).reshape(-1) for k in ("lambda_q1", "lambda_k1", "lambda_q2", "lambda_k2")]).reshape(1, 256)
    SUBG = f32(inp["subln_g"]).reshape(1, 128)
    DNG = f32(inp["dn_norm_g"]).reshape(1, 128)
    cq = f32(inp["conv_qkv_w"])[0].reshape(3, 24, 128).transpose(2, 1, 0)
    cf = f32(inp["conv_ffn_w"])[0].reshape(3, 88, 128).transpose(2, 1, 0)
    wg = w_in[:, 7168:7200]
    alog = f32(inp["a_log"])[0].reshape(16)
    dtb = f32(inp["dt_bias"])[0].reshape(16)
    consts, _ = _consts()
    xp, xs = f32(inp["x_prompt"]), f32(inp["x_sample"])
    ck, cvv = f32(inp["cache_k"]), f32(inp["cache_v"])
    sf, sbw = f32(inp["state_fwd"]), f32(inp["state_bwd"])
    cvec, cctx = f32(inp["c"]), f32(inp["c_ctx"])
    swap = np.concatenate([np.arange(8, 16), np.arange(0, 8)])
    maps = []
    shared = {}
    for par in (0, 1):
        wgp = wg if par == 0 else wg[:, np.concatenate([swap, 16 + swap])]
        shared[par] = dict(
            WG=np.ascontiguousarray(wgp.reshape(16, 128, 32).transpose(1, 0, 2)),
            ALOG=np.ascontiguousarray((alog if par == 0 else alog[swap]).reshape(1, 16)),
            DTB=np.ascontiguousarray((dtb if par == 0 else dtb[swap]).reshape(1, 16)),
            CONVQ=np.ascontiguousarray(cq if par == 0 else cq[:, :, ::-1]),
            CONVF=np.ascontiguousarray(cf if par == 0 else cf[:, :, ::-1]),
            ROPE=_rope_tables(par == 1),
        )
    for c in range(NCORES):
        b, par = c // 2, c % 2
        fl = (lambda a, ax: a[(slice(None),) * ax + (slice(None, None, -1),)]) if par else (lambda a, ax: a)
        m = dict(
            XP=np.ascontiguousarray(fl(xp[2 * c:2 * c + 2], 1)).reshape(512, 2048),
            XS=np.ascontiguousarray(fl(xs[b], 0)),
            CK=np.ascontiguousarray(ck[b, 0].transpose(1, 0, 2)),
            CV=np.ascontiguousarray(cvv[b, 0].transpose(1, 0, 2)),
            SF0=np.ascontiguousarray((sbw if par else sf)[b, 0]),
            SB0=np.ascontiguousarray((sf if par else sbw)[b, 0]),
            CVEC=np.ascontiguousarray(np.stack([cctx, cvec[b]], 0).reshape(2, 16, 128).transpose(2, 1, 0)),
            MODW=MODW, MODB=MODB, WIN=WIN, WOUT=WOUT, WUP=WUP, WDOWN=WDOWN, GMIX=GMIX, GFFN=GFFN, FINALG=FINALG,
            LAMV=LAMV, SUBG=SUBG, DNG=DNG, CONSTS=consts,
        )
        m.update(shared[par])
        maps.append(m)
    return maps


def build(stop_after=None):
    nc = bass.Bass("TRN2", target_bir_lowering=False)
    di = lambda n, s: nc.dram_tensor(n, list(s), F32, kind="ExternalInput").ap()
    do = lambda n, s: nc.dram_tensor(n, list(s), F32, kind="ExternalOutput").ap()
    XP, XS = di("XP", (512, 2048)), di("XS", (1024, 2048))
    CK, CV = di("CK", (8, 256, 128)), di("CV", (8, 256, 128))
    SF0, SB0 = di("SF0", (8, 128, 128)), di("SB0", (8, 128, 128))
    CVEC = di("CVEC", (128, 16, 2))
    MODW, MODB = di("MODW", (24, 128, 16, 512)), di("MODB", (24, 512))
    WIN = di("WIN", (8, 7, 128, 16, 128))
    WOUT, WUP, WDOWN = di("WOUT", (4, 128, 16, 512)), di("WUP", (22, 128, 16, 512)), di("WDOWN", (11, 4, 128, 4, 512))
    GMIX, GFFN, FINALG = di("GMIX", (128, 16)), di("GFFN", (128, 16)), di("FINALG", (1, 2048))
    LAMV, SUBG, DNG = di("LAMV", (1, 256)), di("SUBG", (1, 128)), di("DNG", (1, 128))
    CONSTS = di("CONSTS", (128, 10, 128))
    WG, ALOG, DTB = di("WG", (128, 16, 32)), di("ALOG", (1, 16)), di("DTB", (1, 16))
    CONVQ, CONVF, ROPE = di("CONVQ", (128, 24, 3)), di("CONVF", (128, 88, 3)), di("ROPE", (128, 2, 1024))
    YP, YS = do("YP", (512, 2048)), do("YS", (512, 2048))
    NK, NV = do("NK", (512, 8, 128)), do("NV", (512, 8, 128))
    NSF, NSB = do("NSF", (2, 8, 128, 128)), do("NSB", (2, 8, 128, 128))

    with ExitStack() as top:
        S = Sched(nc, top)
        out_deps = []

        arena = Arena(nc, top)

        def alloc(stack, name, shape, dt=F32):
            return arena.alloc(shape, dt)

        PS = [top.enter_context(nc.psum_tensor("ps%d" % i, [128, 512], F32)) for i in range(8)]
        dPS = [Dep(excl=True) for _ in range(8)]

        cst = alloc(top, "cst", (128, 10, 128)); d_cst = Dep()
        S.dma("sp", cst[:], CONSTS, w=[d_cst])
        ident, ones, ublk, lblk, slblk, sublk, eblk, e0, e1 = [cst[:, i, :] for i in range(9)]
        propeb = alloc(top, "propeb", (128, 128), BF16)
        identb = alloc(top, "identb", (128, 128), BF16)
        S.op("dve", lambda e: e.tensor_copy(out=identb[:], in_=cst[:, 0, :]), r=[d_cst], w=[d_cst])
        S.op("dve", lambda e: e.tensor_copy(out=propeb[:], in_=cst[:, 9, :]), r=[d_cst], w=[d_cst])
        small = alloc(top, "small", (128, 1024)); d_small = Dep()
        S.dma("sp", small[:, 0:16], GMIX, w=[d_small])
        S.dma("sp", small[:, 16:32], GFFN, w=[d_small])
        S.dma("sp", small[:, 160:176], ALOG.partition_broadcast(128), w=[d_small])
        S.dma("sp", small[:, 176:192], DTB.partition_broadcast(128), w=[d_small])
        S.dma("sp", small[:, 192:448], LAMV.partition_broadcast(128), w=[d_small])
        S.dma("sp", small[:, 448:576], SUBG.partition_broadcast(128), w=[d_small])
        S.dma("sp", small[:, 576:704], DNG.partition_broadcast(128), w=[d_small])
        s1v = small[:, 32:64].rearrange("p (k c) -> p k c", c=2)
        b1v = small[:, 64:96].rearrange("p (k c) -> p k c", c=2)
        s2v = small[:, 96:128].rearrange("p (k c) -> p k c", c=2)
        b2v = small[:, 128:160].rearrange("p (k c) -> p k c", c=2)
        negA = small[:, 160:176]
        dtbb = small[:, 176:192]
        subgs = small[:, 448:576]
        dngb = small[:, 576:704]
        neglam = small[:, 704:705]
        S.op("act", lambda e: e.activation(out=negA, in_=negA, func=AF.Exp), r=[d_small], w=[d_small])
        S.op("dve", lambda e: e.tensor_scalar(out=negA, in0=negA, scalar1=-1.0, scalar2=None, op0=ALU.mult),
             r=[d_small], w=[d_small])
        S.op("dve", lambda e: e.tensor_scalar(out=subgs, in0=subgs, scalar1=1.0 - LAM_INIT, scalar2=None, op0=ALU.mult),
             r=[d_small], w=[d_small])
        S.op("dve", lambda e: e.tensor_tensor(out=small[:, 192:256], in0=small[:, 192:256], in1=small[:, 256:320], op=ALU.mult),
             r=[d_small], w=[d_small])
        S.op("dve", lambda e: e.tensor_tensor(out=small[:, 320:384], in0=small[:, 320:384], in1=small[:, 384:448], op=ALU.mult),
             r=[d_small], w=[d_small])
        S.op("dve", lambda e: e.reduce_sum(out=small[:, 705:706], in_=small[:, 192:256], axis=mybir.AxisListType.X),
             r=[d_small], w=[d_small])
        S.op("dve", lambda e: e.reduce_sum(out=small[:, 706:707], in_=small[:, 320:384], axis=mybir.AxisListType.X),
             r=[d_small], w=[d_small])
        S.op("act", lambda e: e.activation(out=small[:, 705:707], in_=small[:, 705:707], func=AF.Exp), r=[d_small], w=[d_small])
        S.op("dve", lambda e: e.tensor_tensor(out=neglam, in0=small[:, 706:707], in1=small[:, 705:706], op=ALU.subtract),
             r=[d_small], w=[d_small])
        S.op("dve", lambda e: e.tensor_scalar(out=neglam, in0=neglam, scalar1=-LAM_INIT, scalar2=None, op0=ALU.add),
             r=[d_small], w=[d_small])

        convq = alloc(top, "convq", (128, 24, 3)); d_convq = Dep()
        S.dma("sp", convq[:], CONVQ, w=[d_convq])
        wgb = alloc(top, "wgb", (128, 16, 32), BF16); d_wgb = Dep()
        S.dma("pool", wgb[:], WG, w=[d_wgb])

        cvs = alloc(top, "cvs", (128, 16, 2)); d_cvs = Dep()
        S.dma("sp", cvs[:], CVEC, w=[d_cvs])
        S.op("act", lambda e: e.activation(out=cvs[:], in_=cvs[:], func=AF.Silu), r=[d_cvs], w=[d_cvs])
        d_Lp = Dep()
        Ls = {}

        def make_L(scope):
            Lp = alloc(scope, "Lp", (128, 16, 128), BF16)
            L0 = alloc(scope, "L0", (128, 16, 128), BF16)
            L1 = alloc(scope, "L1", (128, 16, 128), BF16)
            for (dst, c0, c1, j) in ((Lp, 0, 64, 0), (Lp, 64, 128, 1), (L0, 0, 128, 0), (L1, 0, 128, 1)):
                S.op("dve", lambda e: e.tensor_copy(out=dst[:, :, c0:c1],
                                                    in_=cvs[:, :, j:j + 1].broadcast_to([128, 16, c1 - c0])),
                     r=[d_cvs], w=[d_Lp])
            Ls["Lp"], Ls["L0"], Ls["L1"] = Lp, L0, L1

        oT = alloc(top, "oT", (128, 16, 1088), BF16)
        d_oT = [Dep() for _ in range(16)]

        def mod_block(stk_bufs, blk, lhs, psum_i):
            wbuf, d_w, mbb, d_mbb, mrow, d_mrow = stk_bufs
            S.dma("sp", mbb[:], MODB[blk:blk + 1, :].partition_broadcast(128), w=[d_mbb])
            for kc in range(16):
                S.op("pe", lambda e: e.matmul(PS[psum_i][:], lhsT=lhs[:, kc, :], rhs=wbuf[:, kc, :],
                                              start=(kc == 0), stop=(kc == 15)),
                     r=[d_w, d_Lp], w=[dPS[psum_i]], signal=(kc == 15))
            S.op("dve", lambda e: e.tensor_tensor(out=mrow[:], in0=PS[psum_i][:], in1=mbb[:], op=ALU.add),
                 r=[dPS[psum_i], d_mbb], w=[d_mrow])

        def mod_featmajor(stk_bufs, vec_i, dstv, psA, psB):
            wstream, mbb, d_mbb, mrow, d_mrow = stk_bufs
            for q in range(4):
                blk = vec_i * 4 + q
                wb, d_w = wstream.get()
                mod_block((wb, d_w, mbb, d_mbb, mrow, d_mrow), blk, Ls["Lp"], psA)
                for j in range(4):
                    S.op("pe", lambda e: e.transpose(out=PS[psB][:, j * 128:(j + 1) * 128],
                                                     in_=mrow[:, j * 128:(j + 1) * 128], identity=ident),
                         r=[d_mrow, d_cst], w=[dPS[psB]], signal=(j == 3))
                for j in range(4):
                    kc = q * 4 + j
                    S.op("dve", lambda e: e.tensor_copy(out=dstv[:, kc, :], in_=PS[psB][:, j * 128:(j + 1) * 128:64]),
                         r=[dPS[psB]], w=[d_small])

        with arena.scope() as ph:
            make_L(ph)
            wbufs = [(alloc(ph, "mw%d" % i, (128, 16, 512), BF16), Dep()) for i in range(2)]
            mbb = alloc(ph, "mbb", (128, 512)); d_mbb = Dep()
            mrow = alloc(ph, "mrow", (128, 512)); d_mrow = Dep()
            bufs = (WStream(S, wbufs, [MODW[b_] for b_ in range(0, 8)], 1), mbb, d_mbb, mrow, d_mrow)
            bufs[0].prefetch()
            mod_featmajor(bufs, 0, b1v, 0, 1)
            mod_featmajor(bufs, 1, s1v, 0, 1)
            S.op("dve", lambda e: e.tensor_scalar(out=small[:, 32:64], in0=small[:, 32:64], scalar1=1.0, scalar2=None,
                                                  op0=ALU.add), r=[d_small], w=[d_small])
            S.op("dve", lambda e: e.tensor_tensor(out=s1v, in0=s1v, in1=small[:, 0:16].unsqueeze(2).broadcast_to([128, 16, 2]),
                                                  op=ALU.mult), r=[d_small], w=[d_small])
            S.barrier()

        if stop_after == "p0":
            print("ops", S.nops, "waits", S.nwaits, "sim", S.simulate()[:2])
            return nc
        NHEADS = int(os.environ.get("KHEADS", "8"))
        d_hT = Dep()
        hTbox = {}

        def rstd_from_ss(st, d_st, n, scale):
            S.op("act", lambda e: e.activation(out=st[:, n:2 * n], in_=st[:, 0:n], func=AF.Ln, bias=EPS, scale=scale),
                 r=[d_st], w=[d_st])
            S.op("act", lambda e: e.activation(out=st[:, n:2 * n], in_=st[:, n:2 * n], func=AF.Exp, scale=-0.5),
                 r=[d_st], w=[d_st])

        def norm_to_featmajor(stack, src_rows, ntile, dst, d_dst, sv, bv, cvi, tag):
            xts = [(alloc(stack, "%sxt%d" % (tag, i), (128, 2048)), Dep()) for i in range(2)]
            xns = [(alloc(stack, "%sxn%d" % (tag, i), (128, 2048)), Dep()) for i in range(2)]
            st = alloc(stack, tag + "st", (128, 2 * ntile)); d_st = Dep()
            S.op("dve", lambda e: e.memset(st[:], 0.0), w=[d_st])
            for t in range(ntile):
                xt, d_xt = xts[t % 2]
                xn, d_xn = xns[t % 2]
                S.dma("sp", xt[:], src_rows(t), w=[d_xt])
                S.op("act", lambda e: e.activation(out=xn[:], in_=xt[:], func=AF.Square, accum_out=st[:, 2 * t:2 * t + 1]),
                     r=[d_xt], w=[d_xn, d_st])
                S.op("act", lambda e: e.activation(out=st[:, 2 * t + 1:2 * t + 2], in_=st[:, 2 * t:2 * t + 1], func=AF.Ln,
                                                   bias=EPS, scale=1.0 / D), r=[d_st], w=[d_st])
                S.op("act", lambda e: e.activation(out=st[:, 2 * t + 1:2 * t + 2], in_=st[:, 2 * t + 1:2 * t + 2],
                                                   func=AF.Exp, scale=-0.5), r=[d_st], w=[d_st])
                S.op("dve", lambda e: e.tensor_scalar(out=xn[:], in0=xt[:], scalar1=st[:, 2 * t + 1:2 * t + 2], scalar2=None,
                                                      op0=ALU.mult), r=[d_xt, d_st], w=[d_xn])
                for g in range(4):
                    for j in range(4):
                        kc = g * 4 + j
                        S.op("pe", lambda e: e.transpose(out=PS[g][:, j * 128:(j + 1) * 128],
                                                         in_=xn[:, kc * 128:(kc + 1) * 128], identity=ident),
                             r=[d_xn, d_cst], w=[dPS[g]], signal=(j == 3))
                    for j in range(4):
                        kc = g * 4 + j
                        if j % 2 == 0:
                            S.op("act", lambda e: e.activation(out=dst[:, kc, t * 128:(t + 1) * 128],
                                                               in_=PS[g][:, j * 128:(j + 1) * 128], func=AF.Identity,
                                                               bias=bv[:, kc, cvi:cvi + 1], scale=sv[:, kc, cvi:cvi + 1]),
                                 r=[dPS[g], d_small], w=[d_dst])
                        else:
                            S.op("dve", lambda e: e.tensor_scalar(out=dst[:, kc, t * 128:(t + 1) * 128],
                                                                  in0=PS[g][:, j * 128:(j + 1) * 128],
                                                                  scalar1=sv[:, kc, cvi:cvi + 1], scalar2=bv[:, kc, cvi:cvi + 1],
                                                                  op0=ALU.mult, op1=ALU.add),
                                 r=[dPS[g], d_small], w=[d_dst])

        def passes(n):
            return [(a, min(a + 512, n)) for a in range(0, n, 512)]

        def proj(wt, d_wt, c0, c1, psum_i, M=128, mo=0):
            for kc in range(16):
                S.op("pe", lambda e: e.matmul(PS[psum_i][0:M, 0:c1 - c0], lhsT=wt[:, kc, mo:mo + M], rhs=hTbox["hT"][:, kc, c0:c1],
                                              start=(kc == 0), stop=(kc == 15)),
                     r=[d_wt, d_hT], w=[dPS[psum_i]], signal=(kc == 15))

        def mixer_group(gi):
            isS = gi == 1
            ntok = 1024 if isS else 512
            ntile = ntok // 128
            mcols = 576 if isS else 512
            ocol0 = 512 if isS else 0
            seqs = [(0, 1024)] if isS else [(0, 256), (256, 512)]
            mt = 5 if isS else 4
            X = XS if isS else XP
            with arena.scope() as ph:
                norm_to_featmajor(ph, lambda t: X[t * 128:(t + 1) * 128, :], ntile, hTbox["hT"], d_hT, s1v, b1v, gi, "n1")
                S.barrier()
            if stop_after == "n1":
                raise _Stop()
            with arena.scope() as gs:
                G = alloc(gs, "G", (128, 10, ntile, 16)); d_G = Dep()
                for t in range(ntile):
                    for kc in range(16):
                        S.op("pe", lambda e: e.matmul(PS[0][:, t * 32:(t + 1) * 32], lhsT=hTbox["hT"][:, kc, t * 128:(t + 1) * 128],
                                                      rhs=wgb[:, kc, :], start=(kc == 0), stop=(kc == 15)),
                             r=[d_hT, d_wgb], w=[dPS[0]], signal=(kc == 15 and t == ntile - 1))
                pg = PS[0][:, 0:ntile * 32].rearrange("p (t c) -> p t c", c=32)
                S.op("act", lambda e: e.activation(out=G[:, 0], in_=pg[:, :, 0:16], func=AF.Exp, scale=-1.0), r=[dPS[0]], w=[d_G])
                S.op("dve", lambda e: e.tensor_scalar(out=G[:, 0], in0=G[:, 0], scalar1=1.0, scalar2=None, op0=ALU.add), r=[d_G], w=[d_G])
                S.op("dve", lambda e: e.reciprocal(out=G[:, 0], in_=G[:, 0]), r=[d_G], w=[d_G])
                S.op("dve", lambda e: e.tensor_tensor(out=G[:, 1], in0=pg[:, :, 16:32],
                                                      in1=dtbb.unsqueeze(1).broadcast_to([128, ntile, 16]), op=ALU.add),
                     r=[dPS[0], d_small], w=[d_G])
                S.op("act", lambda e: e.activation(out=G[:, 1], in_=G[:, 1], func=AF.Exp), r=[d_G], w=[d_G])
                S.op("act", lambda e: e.activation(out=G[:, 1], in_=G[:, 1], func=AF.Ln, bias=1.0), r=[d_G], w=[d_G])
                S.op("dve", lambda e: e.tensor_tensor(out=G[:, 1], in0=G[:, 1],
                                                      in1=negA.unsqueeze(1).broadcast_to([128, ntile, 16]), op=ALU.mult),
                     r=[d_G, d_small], w=[d_G])
                gflat = G[:, 1].rearrange("p t c -> p (t c)")
                n16 = ntile * 16
                for i, m in enumerate((ublk, lblk, eblk)):
                    S.op("pe", lambda e: e.matmul(PS[1][:, i * n16:(i + 1) * n16], lhsT=m, rhs=gflat, start=True, stop=True),
                         r=[d_G, d_cst], w=[dPS[1]], signal=(i == 2))
                for i, m in enumerate((e0, e1)):
                    S.op("pe", lambda e: e.matmul(PS[2][:, i * n16:(i + 1) * n16], lhsT=m, rhs=gflat, start=True, stop=True),
                         r=[d_G, d_cst], w=[dPS[2]], signal=(i == 1))
                p1 = lambda i: PS[1][:, i * n16:(i + 1) * n16].rearrange("p (t c) -> p t c", c=16)
                p2 = lambda i: PS[2][:, i * n16:(i + 1) * n16].rearrange("p (t c) -> p t c", c=16)
                S.op("dve", lambda e: e.tensor_copy(out=G[:, 2, :, 0:8], in_=p1(0)[:, :, 0:8]), r=[dPS[1]], w=[d_G])
                S.op("dve", lambda e: e.tensor_copy(out=G[:, 2, :, 8:16], in_=p1(1)[:, :, 8:16]), r=[dPS[1]], w=[d_G])
                S.op("dve", lambda e: e.tensor_copy(out=G[:, 3], in_=p1(2)), r=[dPS[1]], w=[d_G])
                S.op("act", lambda e: e.activation(out=G[:, 4], in_=G[:, 2], func=AF.Exp), r=[d_G], w=[d_G])
                S.op("dve", lambda e: e.tensor_tensor(out=G[:, 9], in0=G[:, 3], in1=G[:, 2], op=ALU.subtract), r=[d_G], w=[d_G])
                S.op("act", lambda e: e.activation(out=G[:, 5], in_=G[:, 9], func=AF.Exp), r=[d_G], w=[d_G])
                S.op("act", lambda e: e.activation(out=G[:, 6], in_=p2(0), func=AF.Exp), r=[dPS[2]], w=[d_G])
                S.op("act", lambda e: e.activation(out=G[:, 7], in_=p2(1), func=AF.Exp), r=[dPS[2]], w=[d_G])
                S.op("dve", lambda e: e.tensor_tensor(out=G[:, 8], in0=G[:, 0], in1=G[:, 4], op=ALU.mult), r=[d_G], w=[d_G])
                BETA, GG, GC, EGC, EKD, DL, BEGC = G[:, 0], G[:, 1], G[:, 2], G[:, 4], G[:, 5], (G[:, 6], G[:, 7]), G[:, 8]
                if stop_after == "gates":
                    S.barrier()
                    raise _Stop()

                wch = [(alloc(gs, "wch%d" % i, (128, 16, 128), BF16), Dep()) for i in range(6)]
                win_stream = WStream(S, wch, [WIN[h_, c_] for h_ in range(NHEADS) for c_ in range(7)], 5)
                win_stream.prefetch()

                def load_w(h, ci):
                    return win_stream.get()

                def attention_head(h):
                    S.mark("g%d h%d attn" % (gi, h))
                    with arena.scope() as hs:
                        nk = ntok + (256 if isS else 0)
                        nkt = nk // 128
                        qT = alloc(hs, "qT", (128, mcols), BF16); d_qT = Dep()
                        kT = alloc(hs, "kT", (128, nk), BF16); d_kT = Dep()
                        vaug = alloc(hs, "vaug", (128, nkt, 132), BF16); d_va = Dep()
                        S.op("pool", lambda e: e.memset(vaug[:], 1.0), w=[d_va])
                        if stop_after == "attnA":
                            load_w(h, 0)
                            S.barrier()
                            raise _Stop()
                        tmpf = [(alloc(hs, "tmpf%d" % i, (128, 512)), Dep()) for i in range(2)]
                        tmpb = [(alloc(hs, "tmpb%d" % i, (128, 512), BF16), Dep()) for i in range(2)]
                        if isS:
                            rope = alloc(hs, "rope", (128, 2, 1024)); d_rope = Dep()
                            S.dma("sp", rope[:], ROPE, w=[d_rope])
                        stage = alloc(hs, "stage", (128, 4, 128)); d_stage = Dep()

                        def qk_proj(ci, dst, d_dst, ncols, keep_f32=None):
                            wt, d_wt = load_w(h, ci)
                            for pi, (c0, c1) in enumerate(passes(ncols)):
                                n = c1 - c0
                                pb = pi % 2
                                proj(wt, d_wt, c0, c1, pb)
                                KQ = int(os.environ.get("KQ", "9"))
                                if KQ == 1:
                                    continue
                                if keep_f32 is not None and KQ >= 3:
                                    S.op("act", lambda e: e.activation(out=keep_f32[0][:, c0:c1], in_=PS[pb][:, 0:n], func=AF.Identity),
                                         r=[dPS[pb]], w=[keep_f32[1]])
                                if not isS:
                                    S.op("dve", lambda e: e.tensor_copy(out=dst[:, c0:c1], in_=PS[pb][:, 0:n]),
                                         r=[dPS[pb]], w=[d_dst])
                                else:
                                    tb, d_tb = tmpb[pi % 2]
                                    tf, d_tf = tmpf[pi % 2]
                                    S.op("act", lambda e: e.activation(out=tb[:, 0:n], in_=PS[pb][:, 0:n], func=AF.Identity), r=[dPS[pb]], w=[d_tb])
                                    S.op("dve", lambda e: e.tensor_tensor(out=tf[:, 0:n], in0=PS[pb][:, 0:n],
                                                                          in1=rope[:, 0, c0:c1], op=ALU.mult),
                                         r=[dPS[pb], d_rope], w=[d_tf])
                                    S.op("pe", lambda e: e.matmul(PS[2 + pb][:, 0:n], lhsT=propeb[:], rhs=tb[:, 0:n],
                                                                  start=True, stop=True), r=[d_tb, d_cst], w=[dPS[2 + pb]])
                                    S.op("dve", lambda e: e.tensor_tensor(out=tb[:, 0:n], in0=PS[2 + pb][:, 0:n],
                                                                          in1=rope[:, 1, c0:c1], op=ALU.mult),
                                         r=[dPS[2 + pb], d_rope], w=[d_tb])
                                    S.op("pool", lambda e: e.tensor_tensor(out=dst[:, c0:c1], in0=tf[:, 0:n], in1=tb[:, 0:n],
                                                                           op=ALU.add), r=[d_tf, d_tb], w=[d_dst])

                        kf = None
                        if not isS:
                            kf = (alloc(hs, "kf", (128, 512)), Dep())
                        qk_proj(0, qT, d_qT, mcols)
                        qk_proj(1, kT, d_kT, ntok, keep_f32=kf)
                        if stop_after == "attnB":
                            S.barrier()
                            raise _Stop()
                        avf = alloc(hs, "avf", (128, ntok)); d_avf = Dep()
                        wt, d_wt = load_w(h, 2)
                        for pi, (c0, c1) in enumerate(passes(ntok)):
                            proj(wt, d_wt, c0, c1, pi % 2)
                            S.op("act", lambda e: e.activation(out=avf[:, c0:c1], in_=PS[pi % 2][:, 0:c1 - c0], func=AF.Identity),
                                 r=[dPS[pi % 2]], w=[d_avf])
                        for t in range(ntile):
                            pb = 4 + t % 2
                            S.op("pe", lambda e: e.transpose(out=PS[pb][:, 0:128], in_=avf[:, t * 128:(t + 1) * 128], identity=ident),
                                 r=[d_avf, d_cst], w=[dPS[pb]])
                            S.op("act", lambda e: e.activation(out=vaug[:, t, 0:128], in_=PS[pb][:, 0:128], func=AF.Identity), r=[dPS[pb]], w=[d_va])
                            if not isS:
                                S.op("dve", lambda e: e.tensor_copy(out=stage[:, t, :], in_=PS[pb][:, 0:128]),
                                     r=[dPS[pb]], w=[d_stage])
                        if not isS:
                            S.dma("sp", NV[:, h, :].rearrange("(t p) d -> p t d", p=128), stage[:], r=[d_stage], evlist=out_deps)
                            kf_t, d_kf = kf
                            stage2 = alloc(hs, "stage2", (128, 4, 128)); d_stage2 = Dep()
                            for t in range(4):
                                pb = 4 + t % 2
                                S.op("pe", lambda e: e.transpose(out=PS[pb][:, 0:128], in_=kf_t[:, t * 128:(t + 1) * 128], identity=ident),
                                     r=[d_kf, d_cst], w=[dPS[pb]])
                                S.op("dve", lambda e: e.tensor_copy(out=stage2[:, t, :], in_=PS[pb][:, 0:128]),
                                     r=[dPS[pb]], w=[d_stage2])
                            S.dma("sp", NK[:, h, :].rearrange("(t p) d -> p t d", p=128), stage2[:], r=[d_stage2], evlist=out_deps)
                        else:
                            ckf = alloc(hs, "ckf", (128, 2, 128)); d_ckf = Dep()
                            S.dma("sp", ckf[:], CK[h].rearrange("(t p) d -> p t d", p=128), w=[d_ckf])
                            S.dma("pool", vaug[:, 8:10, 0:128], CV[h].rearrange("(t p) d -> p t d", p=128), w=[d_va])
                            for t in range(2):
                                pb = 4 + t
                                S.op("pe", lambda e: e.transpose(out=PS[pb][:, 0:128], in_=ckf[:, t, :], identity=ident),
                                     r=[d_ckf, d_cst], w=[dPS[pb]])
                                S.op("dve", lambda e: e.tensor_copy(out=kT[:, 1024 + t * 128:1024 + (t + 1) * 128],
                                                                    in_=PS[pb][:, 0:128]), r=[dPS[pb]], w=[d_kT])
                        if stop_after == "attnproj":
                            S.barrier()
                            raise _Stop()
                        S.mark("g%d h%d scores" % (gi, h))
                        On = alloc(hs, "On", (128, 2, mt, 128)); d_On = Dep()
                        Eall = alloc(hs, "Eall", (128, nkt, 576), BF16); d_E = Dep()
                        rden = alloc(hs, "rden", (128, 16)); d_rden = Dep()
                        ectr = 0
                        for (q0, q1) in ([(0, 576)] if isS else [(0, 256), (256, 512)]):
                            nq = q1 - q0
                            kts = list(range(nkt)) if isS else [q0 // 128, q0 // 128 + 1]
                            qtl = [(a, min(a + 128, nq)) for a in range(0, nq, 128)]
                            for m in range(2):
                                rows = slice(m * 64, m * 64 + 64)
                                for ki, kt in enumerate(kts):
                                    ectr += 1
                                    for pi, (a, b) in enumerate(passes(nq)):
                                        pb = 2 * (ectr % 2) + pi
                                        S.op("pe", lambda e: e.matmul(PS[pb][:, 0:b - a], lhsT=kT[rows, kt * 128:(kt + 1) * 128],
                                                                      rhs=qT[rows, q0 + a:q0 + b], start=True, stop=True),
                                             r=[d_kT, d_qT], w=[dPS[pb]])
                                        S.op("act", lambda e: e.activation(out=Eall[:, ki, a:b], in_=PS[pb][:, 0:b - a], func=AF.Exp,
                                                                           scale=0.125), r=[dPS[pb]], w=[d_E])
                                for qi, (a, b) in enumerate(qtl):
                                    pb = 4 + qi // 3
                                    co = (qi % 3) * 129
                                    for ki, kt in enumerate(kts):
                                        S.op("pe", lambda e: e.matmul(PS[pb][0:b - a, co:co + 129], lhsT=Eall[:, ki, a:b],
                                                                      rhs=vaug[:, kt, 0:129], start=(ki == 0), stop=(ki == len(kts) - 1)),
                                             r=[d_E, d_va], w=[dPS[pb]], signal=(ki == len(kts) - 1))
                                for qi, (a, b) in enumerate(qtl):
                                    pb = 4 + qi // 3
                                    co = (qi % 3) * 129
                                    n = b - a
                                    tq = (q0 + a) // 128
                                    S.op("dve", lambda e: e.reciprocal(out=rden[0:n, qi:qi + 1], in_=PS[pb][0:n, co + 128:co + 129]),
                                         r=[dPS[pb]], w=[d_rden])
                                    S.op("dve", lambda e: e.tensor_scalar(out=On[0:n, m, tq, :], in0=PS[pb][0:n, co:co + 128],
                                                                          scalar1=rden[0:n, qi:qi + 1], scalar2=None, op0=ALU.mult),
                                         r=[dPS[pb], d_rden], w=[d_On])
                        if stop_after == "attnpv":
                            S.barrier()
                            raise _Stop()
                        st = alloc(hs, "ast", (128, 2 * mt)); d_st = Dep()
                        S.op("dve", lambda e: e.memset(st[:], 0.0), w=[d_st])
                        for tq in range(mt):
                            n = 64 if (isS and tq == 4) else 128
                            S.op("dve", lambda e: e.scalar_tensor_tensor(out=On[0:n, 0, tq, :], in0=On[0:n, 1, tq, :], scalar=neglam[0:n, :],
                                                                         in1=On[0:n, 0, tq, :], op0=ALU.mult, op1=ALU.add),
                                 r=[d_On, d_small], w=[d_On])
                            S.op("act", lambda e: e.activation(out=On[0:n, 1, tq, :], in_=On[0:n, 0, tq, :], func=AF.Square,
                                                               accum_out=st[0:n, tq:tq + 1]), r=[d_On], w=[d_On, d_st])
                        rstd_from_ss(st, d_st, mt, 1.0 / 128)
                        for tq in range(mt):
                            n = 64 if (isS and tq == 4) else 128
                            pb = 2 + tq % 2
                            S.op("dve", lambda e: e.scalar_tensor_tensor(out=On[0:n, 1, tq, :], in0=On[0:n, 0, tq, :],
                                                                         scalar=st[0:n, mt + tq:mt + tq + 1], in1=subgs[0:n, :],
                                                                         op0=ALU.mult, op1=ALU.mult),
                                 r=[d_On, d_st, d_small], w=[d_On])
                            S.op("pe", lambda e: e.transpose(out=PS[pb][:, 0:n], in_=On[0:n, 1, tq, :], identity=cst[0:n, 0, 0:n]),
                                 r=[d_On, d_cst], w=[dPS[pb]])
                            S.op("act", lambda e: e.activation(out=oT[:, h, ocol0 + tq * 128:ocol0 + tq * 128 + n], in_=PS[pb][:, 0:n], func=AF.Identity),
                                 r=[dPS[pb]], w=[d_oT[h]])
                        if S.rec is None:
                            S.barrier()

                attention_head(0)
                for h in range(NHEADS):
                    if stop_after == "attn":
                        if h + 1 < NHEADS:
                            attention_head(h + 1)
                        continue
                    with arena.scope() as hs:
                        S.mark("g%d h%d dnproj" % (gi, h))

                        def hook(h=h):
                            if h + 1 >= NHEADS or os.environ.get("KNOILV"):
                                if h + 1 < NHEADS:
                                    return None
                                return []
                            S.rec = []
                            arena.norelease = True
                            attention_head(h + 1)
                            arena.norelease = False
                            rec, S.rec = S.rec, None
                            return rec

                        dn_head(hs, gi, h, load_w, G_=(BETA, GG, GC, EGC, EKD, DL, BEGC), d_G=d_G, interleave=hook)
                        S.barrier()
                    if os.environ.get("KNOILV") and h + 1 < NHEADS:
                        attention_head(h + 1)
                S.barrier()

        def dn_head(hs, gi, h, load_w, G_, d_G, interleave=None):
            BETA, GG, GC, EGC, EKD, DL, BEGC = G_
            isS = gi == 1
            ntok = 1024 if isS else 512
            ntile = ntok // 128
            mcols = 576 if isS else 512
            ocol0 = 512 if isS else 0
            mt = 5 if isS else 4
            seqs = [(0, 1024)] if isS else [(0, 256), (256, 512)]
            yq = alloc(hs, "yq", (128, ntok))
            yqb = alloc(hs, "yqb", (128, ntok), BF16); ykb = alloc(hs, "ykb", (128, ntok), BF16); d_yb = Dep()
            gsT = alloc(hs, "gsT", (128, mcols)); d_gs = Dep()
            dirbuf = []
            for d_ in range(2):
                ntd = mt if (isS and d_ == 0) else ntile
                dirbuf.append(dict(uw=alloc(hs, "uw%d" % d_, (128, ntd, 256), BF16), kdq=alloc(hs, "kdq%d" % d_, (128, ntd, 256), BF16), d=Dep()))
            inner_mark = arena.ptr
            yk = alloc(hs, "yk", (128, ntok)); yv = alloc(hs, "yv", (128, ntok))
            d_y = [Dep(), Dep(), Dep()]
            zrs = [(alloc(hs, "zr%d" % i_, (128, ntok)), Dep()) for i_ in range(2)]
            for which, (ci, yy) in enumerate(((3, yq), (4, yk), (5, yv))):
                zr, d_zr = zrs[which % 2]
                wt, d_wt = load_w(h, ci)
                cw = convq[:, which * 8 + h, :]
                for pi, (c0, c1) in enumerate(passes(ntok)):
                    proj(wt, d_wt, c0, c1, pi % 2)
                    S.op("act", lambda e: e.activation(out=zr[:, c0:c1], in_=PS[pi % 2][:, 0:c1 - c0], func=AF.Identity), r=[dPS[pi % 2]], w=[d_zr])
                S.op("dve", lambda e: e.tensor_scalar(out=yy[:], in0=zr[:], scalar1=cw[:, 1:2], scalar2=None, op0=ALU.mult),
                     r=[d_zr, d_convq], w=[d_y[which]])
                for (a, b) in seqs:
                    S.op("dve", lambda e: e.scalar_tensor_tensor(out=yy[:, a + 1:b], in0=zr[:, a:b - 1], scalar=cw[:, 0:1],
                                                                 in1=yy[:, a + 1:b], op0=ALU.mult, op1=ALU.add),
                         r=[d_zr, d_convq], w=[d_y[which]])
                    S.op("dve", lambda e: e.scalar_tensor_tensor(out=yy[:, a:b - 1], in0=zr[:, a + 1:b], scalar=cw[:, 2:3],
                                                                  in1=yy[:, a:b - 1], op0=ALU.mult, op1=ALU.add),
                         r=[d_zr, d_convq], w=[d_y[which]])
                S.op("act", lambda e: e.activation(out=yy[:], in_=yy[:], func=AF.Silu), r=[d_y[which]], w=[d_y[which]])
            wt, d_wt = load_w(h, 6)
            for pi, (c0, c1) in enumerate(passes(mcols)):
                proj(wt, d_wt, c0, c1, pi % 2)
                S.op("act", lambda e: e.activation(out=gsT[:, c0:c1], in_=PS[pi % 2][:, 0:c1 - c0], func=AF.Silu),
                     r=[dPS[pi % 2]], w=[d_gs])
            for which, yy in ((0, yq), (1, yk)):
                zr, d_zr = zrs[(which + 1) % 2]
                S.op("pool", lambda e: e.tensor_tensor(out=zr[:], in0=yy[:], in1=yy[:], op=ALU.mult), r=[d_y[which]], w=[d_zr])
                for pi, (c0, c1) in enumerate(passes(ntok)):
                    pb = 2 + pi % 2
                    n = c1 - c0
                    S.op("pe", lambda e: e.matmul(PS[pb][:, 0:n], lhsT=ones, rhs=zr[:, c0:c1], start=True, stop=True),
                         r=[d_zr, d_cst], w=[dPS[pb]])
                    S.op("act", lambda e: e.activation(out=zr[:, c0:c1], in_=PS[pb][:, 0:n], func=AF.Ln, bias=EPS, scale=1.0),
                         r=[dPS[pb]], w=[d_zr])
                    S.op("act", lambda e: e.activation(out=zr[:, c0:c1], in_=zr[:, c0:c1], func=AF.Exp, scale=-0.5),
                         r=[d_zr], w=[d_zr])
                sc = 128.0 ** -0.5 if which == 0 else 1.0
                S.op("dve", lambda e: e.scalar_tensor_tensor(out=yy[:], in0=yy[:], scalar=sc, in1=zr[:], op0=ALU.mult, op1=ALU.mult),
                     r=[d_zr, d_y[which]], w=[d_y[which]])
                S.op("act", lambda e: e.activation(out=(yqb if which == 0 else ykb)[:], in_=yy[:], func=AF.Identity),
                     r=[d_y[which]], w=[d_yb])
            ktok = alloc(hs, "ktok", (128, ntile, 128)); vtok = alloc(hs, "vtok", (128, ntile, 128)); d_kv = Dep()
            KK = alloc(hs, "KK", (128, ntile, 128)); QKT = alloc(hs, "QKT", (128, mt, 128)); d_KQ = Dep()
            for t in range(ntile):
                ts = slice(t * 128, (t + 1) * 128)
                pb = 4 + t % 2
                S.op("pe", lambda e: e.transpose(out=PS[pb][:, 0:128], in_=yk[:, ts], identity=ident), r=[d_y[1], d_cst], w=[dPS[pb]], signal=False)
                S.op("pe", lambda e: e.transpose(out=PS[pb][:, 128:256], in_=yv[:, ts], identity=ident), r=[d_y[2], d_cst], w=[dPS[pb]])
                S.op("act", lambda e: e.activation(out=ktok[:, t, :], in_=PS[pb][:, 0:128], func=AF.Identity), r=[dPS[pb]], w=[d_kv])
                S.op("act", lambda e: e.activation(out=vtok[:, t, :], in_=PS[pb][:, 128:256], func=AF.Identity), r=[dPS[pb]], w=[d_kv])
                pb2 = 6 + t % 2
                S.op("pe", lambda e: e.matmul(PS[pb2][:, 0:128], lhsT=ykb[:, ts], rhs=ykb[:, ts], start=True, stop=True),
                     r=[d_yb], w=[dPS[pb2]], signal=(t >= mt))
                S.op("dve", lambda e: e.tensor_copy(out=KK[:, t, :], in_=PS[pb2][:, 0:128]), r=[dPS[pb2]], w=[d_KQ]) if t >= mt else None
                if t < mt:
                    S.op("pe", lambda e: e.matmul(PS[pb2][:, 128:256], lhsT=ykb[:, ts], rhs=yqb[:, ts], start=True, stop=True),
                         r=[d_yb], w=[dPS[pb2]])
                    S.op("dve", lambda e: e.tensor_copy(out=KK[:, t, :], in_=PS[pb2][:, 0:128]), r=[dPS[pb2]], w=[d_KQ])
                    S.op("dve", lambda e: e.tensor_copy(out=QKT[:, t, :], in_=PS[pb2][:, 128:256]), r=[dPS[pb2]], w=[d_KQ])
            S.mark("g%d h%d solve" % (gi, h))
            NP = 8
            sol = [dict(A=[alloc(hs, "sA%d_%d" % (i, j), (128, 128)) for j in range(2)],
                        BW=[alloc(hs, "sBW%d_%d" % (i, j), (128, 256)) for j in range(2)],
                        x1=alloc(hs, "sx1_%d" % i, (128, 128)), x2=alloc(hs, "sx2_%d" % i, (128, 128)),
                        d=Dep()) for i in range(NP)]
            vb = [(alloc(hs, "vbk%d" % i, (128, 256)), Dep()) for i in range(2)]
            probs = [(d_, t) for d_ in range(2) for t in (range(ntile) if (d_ == 1 or not isS) else range(mt))]

            def pinfo(d_):
                return (d_ * 8 + h, (ublk, slblk, ublk) if d_ == 0 else (lblk, sublk, lblk), dirbuf[d_])

            for _once in (0,):
                for g0 in range(0, len(probs), NP):
                    grp = probs[g0:g0 + NP]
                    for i, (d_, t) in enumerate(grp):
                        hd, (Mg, Ms, Mi), db = pinfo(d_)
                        sl = sol[i]
                        pb = i % 8
                        gcol = GG[:, t, hd:hd + 1]
                        S.op("pool", lambda e: e.tensor_tensor(out=sl["x1"][:], in0=Mg, in1=gcol.broadcast_to([128, 128]), op=ALU.mult),
                             r=[d_G, d_cst], w=[sl["d"]])
                        S.op("pe", lambda e: e.matmul(PS[pb][:, 0:128], lhsT=ones, rhs=sl["x1"][:], start=True, stop=True),
                             r=[sl["d"], d_cst], w=[dPS[pb]])
                        S.op("dve", lambda e: e.tensor_scalar(out=sl["x1"][:], in0=PS[pb][:, 0:128], scalar1=GC[:, t, hd:hd + 1],
                                                              scalar2=0.0, op0=ALU.subtract, op1=ALU.max),
                             r=[dPS[pb], d_G], w=[sl["d"]])
                        S.op("dve", lambda e: e.tensor_scalar(out=sl["x2"][:], in0=PS[pb][:, 0:128], scalar1=GC[:, t, hd:hd + 1],
                                                              scalar2=0.0, op0=ALU.subtract, op1=ALU.min),
                             r=[dPS[pb], d_G], w=[sl["d"]])
                        S.op("act", lambda e: e.activation(out=sl["x1"][:], in_=sl["x1"][:], func=AF.Exp, scale=-1.0),
                             r=[sl["d"]], w=[sl["d"]])
                        S.op("act", lambda e: e.activation(out=sl["x2"][:], in_=sl["x2"][:], func=AF.Exp), r=[sl["d"]], w=[sl["d"]])
                        S.op("pool", lambda e: e.tensor_tensor(out=sl["x1"][:], in0=sl["x1"][:], in1=Ms, op=ALU.mult),
                             r=[sl["d"], d_cst], w=[sl["d"]])
                        S.op("dve", lambda e: e.scalar_tensor_tensor(out=sl["A"][0][:], in0=KK[:, t, :], scalar=BETA[:, t, hd:hd + 1],
                                                                     in1=sl["x1"][:], op0=ALU.mult, op1=ALU.mult),
                             r=[d_KQ, d_G, sl["d"]], w=[sl["d"]])
                        if t < mt:
                            S.op("pool", lambda e: e.tensor_tensor(out=sl["x2"][:], in0=sl["x2"][:], in1=Mi, op=ALU.mult),
                                 r=[sl["d"], d_cst], w=[sl["d"]])
                            S.op("pool", lambda e: e.tensor_tensor(out=db["kdq"][:, t, 128:256], in0=QKT[:, t, :], in1=sl["x2"][:], op=ALU.mult),
                                 r=[d_KQ, sl["d"]], w=[db["d"]])
                    for i, (d_, t) in enumerate(grp):
                        hd, (Mg, Ms, Mi), db = pinfo(d_)
                        sl = sol[i]
                        pb = i % 8
                        S.op("pe", lambda e: e.transpose(out=PS[pb][:, 0:128], in_=sl["A"][0][:], identity=ident),
                             r=[sl["d"], d_cst], w=[dPS[pb]])
                        S.op("act", lambda e: e.activation(out=sl["BW"][0][:, 0:128], in_=PS[pb][:, 0:128], func=AF.Identity), r=[dPS[pb]], w=[sl["d"]])
                        S.op("dve", lambda e: e.tensor_tensor(out=sl["BW"][0][:, 128:256], in0=ident, in1=PS[pb][:, 0:128], op=ALU.subtract),
                             r=[dPS[pb], d_cst], w=[sl["d"]])
                    cur = 0
                    for i, (d_, t) in enumerate(grp):
                        sl = sol[i]
                        pb = i % 8
                        S.op("pe", lambda e: e.matmul(PS[pb][:, 128:256], lhsT=sl["A"][0][:], rhs=sl["BW"][0][:, 0:128], start=True, stop=True),
                             r=[sl["d"]], w=[dPS[pb]])
                        S.op("dve", lambda e: e.tensor_copy(out=sl["BW"][1][:, 0:128], in_=PS[pb][:, 128:256]), r=[dPS[pb]], w=[sl["d"]])
                        S.op("pool", lambda e: e.tensor_copy(out=sl["BW"][1][:, 128:256], in_=sl["BW"][0][:, 128:256]), r=[sl["d"]], w=[sl["d"]])
                    for i, (d_, t) in enumerate(grp):
                        sl = sol[i]
                        pb = i % 8
                        S.op("pe", lambda e: e.transpose(out=PS[pb][:, 0:128], in_=sl["BW"][1][:, 0:128], identity=ident),
                             r=[sl["d"], d_cst], w=[dPS[pb]])
                        S.op("act", lambda e: e.activation(out=sl["A"][1][:], in_=PS[pb][:, 0:128], func=AF.Identity), r=[dPS[pb]], w=[sl["d"]])
                    cur = 1
                    for lvl in range(1, 6):
                        nxt = 1 - cur
                        last_b = lvl >= 4
                        last_a = lvl >= 5
                        for i, (d_, t) in enumerate(grp):
                            sl = sol[i]
                            pb = i % 8
                            Ac, BWc, An, BWn = sl["A"][cur], sl["BW"][cur], sl["A"][nxt], sl["BW"][nxt]
                            if not last_b:
                                S.op("pe", lambda e: e.matmul(PS[pb][:, 0:256], lhsT=Ac[:], rhs=BWc[:], start=True, stop=True),
                                     r=[sl["d"]], w=[dPS[pb]])
                                S.op("act", lambda e: e.activation(out=BWn[:, 0:128], in_=PS[pb][:, 0:128], func=AF.Identity), r=[dPS[pb]], w=[sl["d"]])
                            else:
                                S.op("pe", lambda e: e.matmul(PS[pb][:, 128:256], lhsT=Ac[:], rhs=BWc[:, 128:256], start=True, stop=True),
                                     r=[sl["d"]], w=[dPS[pb]], signal=last_a)
                                if not last_a:
                                    S.op("pe", lambda e: e.matmul(PS[pb][:, 256:384], lhsT=BWc[:, 0:128], rhs=Ac[:], start=True, stop=True),
                                         r=[sl["d"]], w=[dPS[pb]])
                                    S.op("act", lambda e: e.activation(out=An[:], in_=PS[pb][:, 256:384], func=AF.Identity), r=[dPS[pb]], w=[sl["d"]])
                            S.op("dve", lambda e: e.tensor_tensor(out=BWn[:, 128:256], in0=BWc[:, 128:256], in1=PS[pb][:, 128:256], op=ALU.add),
                                 r=[dPS[pb], sl["d"]], w=[sl["d"]])
                        if not last_b:
                            for i, (d_, t) in enumerate(grp):
                                sl = sol[i]
                                pb = i % 8
                                An, BWn = sl["A"][nxt], sl["BW"][nxt]
                                S.op("pe", lambda e: e.transpose(out=PS[pb][:, 256:384], in_=BWn[:, 0:128], identity=ident),
                                     r=[sl["d"], d_cst], w=[dPS[pb]])
                                S.op("act", lambda e: e.activation(out=An[:], in_=PS[pb][:, 256:384], func=AF.Identity), r=[dPS[pb]], w=[sl["d"]])
                        cur = nxt
                    for i, (d_, t) in enumerate(grp):
                        hd, (Mg, Ms, Mi), db = pinfo(d_)
                        sl = sol[i]
                        W = sl["BW"][cur][:, 128:256]
                        vbk, d_vb = vb[i % 2]
                        pb = i % 8
                        S.op("pool", lambda e: e.tensor_tensor(out=vbk[:, 0:128], in0=vtok[:, t, :], in1=BETA[:, t, hd:hd + 1].broadcast_to([128, 128]), op=ALU.mult),
                             r=[d_kv, d_G], w=[d_vb])
                        S.op("pool", lambda e: e.tensor_tensor(out=vbk[:, 128:256], in0=ktok[:, t, :], in1=BEGC[:, t, hd:hd + 1].broadcast_to([128, 128]), op=ALU.mult),
                             r=[d_kv, d_G], w=[d_vb])
                        S.op("pool", lambda e: e.tensor_tensor(out=db["kdq"][:, t, 0:128], in0=ktok[:, t, :], in1=EKD[:, t, hd:hd + 1].broadcast_to([128, 128]), op=ALU.mult),
                             r=[d_kv, d_G], w=[db["d"]])
                        S.op("pe", lambda e: e.matmul(PS[pb][:, 0:256], lhsT=W, rhs=vbk[:], start=True, stop=True),
                             r=[sl["d"], d_vb], w=[dPS[pb]])
                        S.op("act", lambda e: e.activation(out=db["uw"][:, t, :], in_=PS[pb][:, 0:256], func=AF.Identity), r=[dPS[pb]], w=[db["d"]])
            S.barrier()
            arena.ptr = inner_mark
            S.mark("g%d h%d recur" % (gi, h))
            oacc = alloc(hs, "oacc", (128, mt, 128)); d_oacc = Dep()
            S.op("pool", lambda e: e.memset(oacc[:], 0.0), w=[d_oacc])
            junk = alloc(hs, "dnjunk", (128, 128)); d_junk = Dep()
            chains = []
            for si, (a, b) in enumerate(seqs):
                t0, t1 = a // 128, b // 128
                for d_ in range(2):
                    if d_ == 0:
                        chunks = [(t, hf) for t in range(t0, t1) for hf in (0, 1)]
                        if isS:
                            chunks = chunks[:9]
                    else:
                        chunks = [(t, hf) for t in range(t1 - 1, t0 - 1, -1) for hf in (1, 0)]
                    outs = [(t < mt and not (isS and t == 4 and hf == 1)) for (t, hf) in chunks]
                    nch = len(chunks)
                    ch = dict(St=[alloc(hs, "St%d_%d_%d" % (si, d_, k), (128, 128)) for k in range(2)],
                              MT=alloc(hs, "MT%d_%d" % (si, d_), (128, nch, 128)), CC=alloc(hs, "CC%d_%d" % (si, d_), (128, nch, 128), BF16),
                              NWQ=alloc(hs, "NWQ%d_%d" % (si, d_), (128, max(1, sum(outs)), 128), BF16),
                              Sb=[alloc(hs, "Sb%d_%d_%d" % (si, d_, k), (128, 128), BF16) for k in range(2)], dSb=[Dep(), Dep()],
                              tmp=alloc(hs, "rt%d_%d" % (si, d_), (128, 128)),
                              dS=[Dep(), Dep()], dpre=[Dep() for _ in range(nch)], dtmp=Dep(),
                              chunks=chunks, outs=outs, dir=d_, si=si, idx=len(chains))
                    if isS:
                        S.dma("sp", ch["St"][0][:], (SF0 if d_ == 0 else SB0)[h], w=[ch["dS"][0]])
                    else:
                        S.op("pool", lambda e: e.memset(ch["St"][0][:], 0.0), w=[ch["dS"][0]])
                    chains.append(ch)
            pctr = 0
            for ch in chains:
                d_ = ch["dir"]
                hd = d_ * 8 + h
                db = dirbuf[d_]
                oi = 0
                for ci, (t, hf) in enumerate(ch["chunks"]):
                    rows = slice(hf * 64, hf * 64 + 64)
                    pb = pctr % 4
                    pctr += 1
                    need_o = ch["outs"][ci]
                    nn = 256 if need_o else 128
                    S.op("pe", lambda e: e.matmul(PS[pb][:, 0:nn], lhsT=db["uw"][rows, t, 128:256], rhs=db["kdq"][rows, t, 0:nn], start=True, stop=True),
                         r=[db["d"]], w=[dPS[pb]], signal=False)
                    S.op("pe", lambda e: e.matmul(PS[pb][:, 256:384], lhsT=db["kdq"][rows, t, 0:128], rhs=db["uw"][rows, t, 0:128], start=True, stop=True),
                         r=[db["d"]], w=[dPS[pb]])
                    S.op("act", lambda e: e.activation(out=ch["MT"][:, ci, :], in_=PS[pb][:, 0:128], func=AF.Identity, scale=-1.0),
                         r=[dPS[pb]], w=[ch["dpre"][ci]])
                    S.op("dve", lambda e: e.tensor_copy(out=ch["CC"][:, ci, :], in_=PS[pb][:, 256:384]), r=[dPS[pb]], w=[ch["dpre"][ci]])
                    if need_o:
                        S.op("act", lambda e: e.activation(out=ch["NWQ"][:, oi, :], in_=PS[pb][:, 128:256], func=AF.Identity, scale=-1.0),
                             r=[dPS[pb]], w=[ch["dpre"][ci]])
                        oi += 1
            maxlen = max(len(c["chunks"]) for c in chains)
            for ch in chains:
                ch["oi"] = 0
            thunks = (interleave() if interleave is not None else None) or []
            nslots = sum(len(c["chunks"]) for c in chains)
            per = (len(thunks) + nslots - 1) // max(1, nslots)
            tpos = [0]

            def replay(n):
                for th in thunks[tpos[0]:tpos[0] + n]:
                    th()
                tpos[0] = min(len(thunks), tpos[0] + n)

            for step in range(maxlen):
                for ch in chains:
                    if step >= len(ch["chunks"]):
                        continue
                    replay(per)
                    t, hf = ch["chunks"][step]
                    d_ = ch["dir"]
                    hd = d_ * 8 + h
                    db = dirbuf[d_]
                    rows = slice(hf * 64, hf * 64 + 64)
                    Sc, Sn = ch["St"][step % 2], ch["St"][(step + 1) % 2]
                    dSc, dSn = ch["dS"][step % 2], ch["dS"][(step + 1) % 2]
                    pbs = 6
                    ts = slice(t * 128, (t + 1) * 128)
                    S.op("pe", lambda e: e.matmul(PS[pbs][:, 0:128], lhsT=identb[:], rhs=ch["CC"][:, step, :], start=True, stop=False),
                         r=[ch["dpre"][step], d_cst], w=[dPS[pbs]], signal=False)
                    S.op("pe", lambda e: e.matmul(PS[pbs][:, 0:128], lhsT=ch["MT"][:, step, :], rhs=Sc[:], start=False, stop=True),
                         r=[ch["dpre"][step], dSc], w=[dPS[pbs]])
                    S.op("dve", lambda e: e.scalar_tensor_tensor(out=Sn[:], in0=Sc[:], scalar=DL[hf][:, t, hd:hd + 1], in1=PS[pbs][:, 0:128],
                                                                 op0=ALU.mult, op1=ALU.add),
                         r=[dPS[pbs], dSc, d_G], w=[dSn])
                    if ch["outs"][step]:
                        pbo = 7
                        oi = ch["oi"]
                        ch["oi"] += 1
                        Sb, dSb = ch["Sb"][step % 2], ch["dSb"][step % 2]
                        S.op("act", lambda e: e.activation(out=Sb[:], in_=Sc[:], func=AF.Identity), r=[dSc], w=[dSb])
                        S.op("pe", lambda e: e.matmul(PS[pbo][:, 0:128], lhsT=yqb[:, ts], rhs=Sb[:], start=True, stop=True),
                             r=[d_yb, dSb], w=[dPS[pbo]], signal=False)
                        S.op("pe", lambda e: e.matmul(PS[pbo][:, 128:256], lhsT=ch["NWQ"][:, oi, :], rhs=Sb[:], start=True, stop=False),
                             r=[ch["dpre"][step], dSb], w=[dPS[pbo]], signal=False)
                        S.op("pe", lambda e: e.matmul(PS[pbo][:, 128:256], lhsT=db["kdq"][rows, t, 128:256], rhs=db["uw"][rows, t, 0:128], start=False, stop=True),
                             r=[db["d"]], w=[dPS[pbo]])
                        S.op("dve", lambda e: e.tensor_tensor(out=ch["tmp"][rows, :], in0=PS[pbo][rows, 128:256], in1=oacc[rows, t, :], op=ALU.add),
                             r=[dPS[pbo], d_oacc], w=[ch["dtmp"]])
                        S.op("dve", lambda e: e.scalar_tensor_tensor(out=oacc[rows, t, :], in0=PS[pbo][rows, 0:128], scalar=EGC[rows, t, hd:hd + 1],
                                                                     in1=ch["tmp"][rows, :], op0=ALU.mult, op1=ALU.add),
                             r=[dPS[pbo], d_G, ch["dtmp"]], w=[d_oacc])
            replay(len(thunks))
            if not isS:
                for ch in chains:
                    n = len(ch["chunks"])
                    ev = S.dma("sp", (NSF if ch["dir"] == 0 else NSB)[ch["si"], h], ch["St"][n % 2][:], r=[ch["dS"][n % 2]])
                    out_deps.append(ev)
            S.mark("g%d h%d dnout" % (gi, h))
            st = alloc(hs, "dst", (128, 2 * mt)); d_st = Dep()
            S.op("dve", lambda e: e.memset(st[:], 0.0), w=[d_st])
            for tq in range(mt):
                n = 64 if (isS and tq == 4) else 128
                S.op("act", lambda e: e.activation(out=junk[0:n, :], in_=oacc[0:n, tq, :], func=AF.Square,
                                                   accum_out=st[0:n, tq:tq + 1]), r=[d_oacc], w=[d_junk, d_st])
            rstd_from_ss(st, d_st, mt, 1.0 / 128)
            for tq in range(mt):
                n = 64 if (isS and tq == 4) else 128
                pb = 6 + tq % 2
                S.op("dve", lambda e: e.scalar_tensor_tensor(out=oacc[0:n, tq, :], in0=oacc[0:n, tq, :], scalar=st[0:n, mt + tq:mt + tq + 1],
                                                             in1=dngb[0:n, :], op0=ALU.mult, op1=ALU.mult),
                     r=[d_st, d_small], w=[d_oacc])
                S.op("pe", lambda e: e.transpose(out=PS[pb][:, 0:n], in_=oacc[0:n, tq, :], identity=cst[0:n, 0, 0:n]),
                     r=[d_oacc, d_cst], w=[dPS[pb]])
                S.op("dve", lambda e: e.tensor_tensor(out=oT[:, 8 + h, ocol0 + tq * 128:ocol0 + tq * 128 + n], in0=PS[pb][:, 0:n],
                                                      in1=gsT[:, tq * 128:tq * 128 + n], op=ALU.mult),
                     r=[dPS[pb], d_gs], w=[d_oT[8 + h]])

        try:
            with arena.scope() as ms:
                hTbox["hT"] = alloc(ms, "hT", (128, 16, 1024), BF16)
                mixer_group(0)
                if stop_after != "g0":
                    mixer_group(1)
                S.barrier()
        except _Stop:
            stop_after = "mixer"

        if stop_after in ("mixer", "attn", "g0"):
            for ev in out_deps:
                S._wait("sp", ev)
            print("ops", S.nops, "waits", S.nwaits, "sim", S.simulate()[:2])
            return nc

        S.mark("phase3")
        xmid = alloc(top, "xmid", (128, 8, 2048)); d_xm = [Dep() for _ in range(9)]
        gbc = alloc(top, "gbc", (128, 2, 2048)); d_gbc = Dep()

        def mod_rowbc(ph_bufs, vec_i, bank0=0):
            for q in range(4):
                mod_rowbc_blk(ph_bufs, vec_i, q, bank0)

        def mod_rowbc_blk(ph_bufs, vec_i, q, bank0=0):
            wstream, mbb, d_mbb, mrow, d_mrow = ph_bufs
            if True:
                blk = vec_i * 4 + q
                wb, d_w = wstream.get()
                for cvi, L in enumerate((Ls["L0"], Ls["L1"])):
                    mod_block((wb, d_w, mbb, d_mbb, mrow, d_mrow), blk, L, bank0 + cvi)
                    S.op("act", lambda e: e.activation(out=gbc[:, cvi, q * 512:(q + 1) * 512], in_=mrow[:], func=AF.Identity), r=[d_mrow], w=[d_gbc])

        with arena.scope() as ph:
            make_L(ph)
            wbufs = [(alloc(ph, "mw%d" % i, (128, 16, 512), BF16), Dep()) for i in range(2)]
            mbb = alloc(ph, "mbb", (128, 512)); d_mbb = Dep()
            mrow = alloc(ph, "mrow", (128, 512)); d_mrow = Dep()
            bufs = (WStream(S, wbufs, [MODW[b_] for b_ in range(8, 20)], 1), mbb, d_mbb, mrow, d_mrow)
            bufs[0].prefetch()
            mod_rowbc(bufs, 2)
            mod_featmajor(bufs, 3, b2v, 2, 3)
            mod_featmajor(bufs, 4, s2v, 2, 3)
            S.op("dve", lambda e: e.tensor_scalar(out=small[:, 96:128], in0=small[:, 96:128], scalar1=1.0, scalar2=None,
                                                  op0=ALU.add), r=[d_small], w=[d_small])
            S.op("dve", lambda e: e.tensor_tensor(out=s2v, in0=s2v, in1=small[:, 16:32].unsqueeze(2).broadcast_to([128, 16, 2]),
                                                  op=ALU.mult), r=[d_small], w=[d_small])
            S.barrier()

        S.mark("phase4")
        NTM = 9
        with arena.scope() as ph4:
          xh = alloc(ph4, "xh", (128, 2048))
          xm = lambda t: (xmid[:, t, :] if t < 8 else xh)
          with arena.scope() as ph:
            wob = [(alloc(ph, "wo%d" % i, (128, 16, 512), BF16), Dep()) for i in range(2)]
            xin = [(alloc(ph, "xin%d" % i, (128, 512)), Dep()) for i in range(3)]
            tmpm = [(alloc(ph, "tmpm%d" % i, (128, 512)), Dep()) for i in range(2)]
            ctr = 0
            wo_stream = WStream(S, wob, [WOUT[b_] for b_ in range(4)], 1)
            wo_stream.prefetch()
            for nb in range(4):
                wb, d_w = wo_stream.get()
                for t in range(NTM):
                    n = 64 if t == 8 else 128
                    cvi = 0 if t < 4 else 1
                    pb = ctr % 4
                    xi, d_xi = xin[ctr % 3]
                    tm, d_tm = tmpm[ctr % 2]
                    ctr += 1
                    src = XP[t * 128:t * 128 + n, nb * 512:(nb + 1) * 512] if t < 4 else XS[(t - 4) * 128:(t - 4) * 128 + n, nb * 512:(nb + 1) * 512]
                    S.dma("sp", xi[0:n, :], src, w=[d_xi])
                    for kc in range(16):
                        S.op("pe", lambda e: e.matmul(PS[pb][0:n, :], lhsT=oT[:, kc, t * 128:t * 128 + n], rhs=wb[:, kc, :],
                                                      start=(kc == 0), stop=(kc == 15)),
                             r=[d_oT[kc], d_w], w=[dPS[pb]], signal=(kc == 15))
                    S.op("dve", lambda e: e.tensor_tensor(out=tm[0:n, :], in0=PS[pb][0:n, :], in1=gbc[0:n, cvi, nb * 512:(nb + 1) * 512], op=ALU.mult),
                         r=[dPS[pb], d_gbc], w=[d_tm])
                    S.op("pool", lambda e: e.tensor_tensor(out=xm(t)[0:n, nb * 512:(nb + 1) * 512], in0=tm[0:n, :], in1=xi[0:n, :], op=ALU.add),
                         r=[d_tm, d_xi], w=[d_xm[t]])
            S.barrier()
          h2T = oT; d_h2 = Dep()
          with arena.scope() as ph:
            xns = [(alloc(ph, "n2xn%d" % i, (128, 2048)), Dep()) for i in range(2)]
            st = alloc(ph, "n2st", (128, 2 * NTM)); d_st = Dep()
            S.op("dve", lambda e: e.memset(st[:], 0.0), w=[d_st])
            make_L(ph)
            wbufs5 = [(alloc(ph, "mw5_%d" % i, (128, 16, 512), BF16), Dep()) for i in range(2)]
            mbb5 = alloc(ph, "mbb5", (128, 512)); d_mbb5 = Dep()
            mrow5 = alloc(ph, "mrow5", (128, 512)); d_mrow5 = Dep()
            ws5 = WStream(S, wbufs5, [MODW[b_] for b_ in range(20, 24)], 1)
            ws5.prefetch()
            for t in range(NTM):
                if t in (1, 3, 5, 7):
                    mod_rowbc_blk((ws5, mbb5, d_mbb5, mrow5, d_mrow5), 5, (t - 1) // 2, bank0=4)
                n = 64 if t == 8 else 128
                cvi = 0 if t < 4 else 1
                xn, d_xn = xns[t % 2]
                S.op("act", lambda e: e.activation(out=xn[0:n, :], in_=xm(t)[0:n, :], func=AF.Square, accum_out=st[0:n, 2 * t:2 * t + 1]),
                     r=[d_xm[t]], w=[d_xn, d_st])
                S.op("act", lambda e: e.activation(out=st[0:n, 2 * t + 1:2 * t + 2], in_=st[0:n, 2 * t:2 * t + 1], func=AF.Ln, bias=EPS, scale=1.0 / D),
                     r=[d_st], w=[d_st])
                S.op("act", lambda e: e.activation(out=st[0:n, 2 * t + 1:2 * t + 2], in_=st[0:n, 2 * t + 1:2 * t + 2], func=AF.Exp, scale=-0.5),
                     r=[d_st], w=[d_st])
                S.op("dve", lambda e: e.tensor_scalar(out=xn[0:n, :], in0=xm(t)[0:n, :], scalar1=st[0:n, 2 * t + 1:2 * t + 2], scalar2=None, op0=ALU.mult),
                     r=[d_xm[t], d_st], w=[d_xn])
                for g in range(4):
                    for j in range(4):
                        kc = g * 4 + j
                        S.op("pe", lambda e: e.transpose(out=PS[g][:, j * 128:j * 128 + n], in_=xn[0:n, kc * 128:(kc + 1) * 128], identity=cst[0:n, 0, 0:n]),
                             r=[d_xn, d_cst], w=[dPS[g]], signal=(j == 3))
                    for j in range(4):
                        kc = g * 4 + j
                        if j % 2 == 0:
                            S.op("act", lambda e: e.activation(out=h2T[:, kc, t * 128:t * 128 + n], in_=PS[g][:, j * 128:j * 128 + n], func=AF.Identity,
                                                               bias=b2v[:, kc, cvi:cvi + 1], scale=s2v[:, kc, cvi:cvi + 1]),
                                 r=[dPS[g], d_small], w=[d_h2])
                        else:
                            S.op("dve", lambda e: e.tensor_scalar(out=h2T[:, kc, t * 128:t * 128 + n], in0=PS[g][:, j * 128:j * 128 + n],
                                                                  scalar1=s2v[:, kc, cvi:cvi + 1], scalar2=b2v[:, kc, cvi:cvi + 1], op0=ALU.mult, op1=ALU.add),
                                 r=[dPS[g], d_small], w=[d_h2])
            S.barrier()

        S.mark("phase5")
        with arena.scope() as ph:
            convf = alloc(ph, "convf", (128, 88, 3)); d_cf = Dep()
            S.dma("sp", convf[:], CONVF, w=[d_cf])
            wub = [(alloc(ph, "wu%d" % i, (128, 16, 512), BF16), Dep()) for i in range(2)]
            wdb = [(alloc(ph, "wd%d" % i, (128, 4, 512), BF16), Dep()) for i in range(2)]
            aTg = alloc(ph, "aT", (128, 4, 1024), BF16); d_aT = [Dep() for _ in range(4)]
            ur = [(alloc(ph, "ur%d" % i, (128, 1032)), Dep()) for i in range(2)]
            yc = [[(alloc(ph, "yc%d_%d" % (i, k), (128, 1024)), Dep()) for k in range(2)] for i in range(2)]
            tmpc = alloc(ph, "tmpc", (128, 1024)); d_tc = Dep()
            tmpd = [(alloc(ph, "tmpd%d" % i, (128, 512)), Dep()) for i in range(2)]
            segs = [(0, 256), (256, 512), (512, 1024)]
            FP = [(0, 512), (512, 1024), (1024, 1025)]
            dctr = [0]
            wu_cur = [None]
            wu_stream = WStream(S, wub, [WUP[r] for r in range(22)], 1)
            wd_stream = WStream(S, wdb, [WDOWN[g_, n_] for g_ in range(11) for n_ in range(4)], 1)
            wu_stream.prefetch()

            def ffn_up(j):
                r_, cc = j // 2, j % 2
                if cc == 0:
                    wu_cur[0] = wu_stream.get()
                if j % 4 == 2:
                    wd_stream.prefetch(0)
                if j % 4 == 3:
                    wd_stream.prefetch(1)
                wu, d_wu = wu_cur[0]
                for gv in range(2):
                    co = gv * 256 + cc * 128
                    cidx = gv * 44 + j
                    u_, d_u = ur[gv]
                    y_, d_yc = yc[gv][j % 2]
                    for pi, (c0, c1) in enumerate(FP):
                        pb = pi
                        n = c1 - c0
                        for kc in range(16):
                            S.op("pe", lambda e: e.matmul(PS[pb][:, 0:n], lhsT=wu[:, kc, co:co + 128], rhs=h2T[:, kc, c0:c1],
                                                          start=(kc == 0), stop=(kc == 15)),
                                 r=[d_wu, d_h2], w=[dPS[pb]], signal=(kc == 15))
                        S.op("act", lambda e: e.activation(out=u_[:, c0:c1], in_=PS[pb][:, 0:n], func=AF.Identity), r=[dPS[pb]], w=[d_u])
                    cw = convf[:, cidx, :]
                    S.op("dve", lambda e: e.tensor_scalar(out=y_[:], in0=u_[:, 0:1024], scalar1=cw[:, 1:2], scalar2=None, op0=ALU.mult),
                         r=[d_u, d_cf], w=[d_yc])
                    S.op("act", lambda e: e.activation(out=tmpc[:], in_=u_[:, 1:1025], func=AF.Identity, scale=cw[:, 2:3]),
                         r=[d_u, d_cf], w=[d_tc])
                    for (a_, b_) in segs:
                        S.op("dve", lambda e: e.scalar_tensor_tensor(out=y_[:, a_ + 1:b_], in0=u_[:, a_:b_ - 1], scalar=cw[:, 0:1],
                                                                     in1=y_[:, a_ + 1:b_], op0=ALU.mult, op1=ALU.add),
                             r=[d_u, d_cf], w=[d_yc])
                    for (a_, b_) in segs:
                        b2 = b_ - 1 if b_ < 1024 else 1024
                        S.op("dve", lambda e: e.tensor_tensor(out=y_[:, a_:b2], in0=y_[:, a_:b2], in1=tmpc[:, a_:b2], op=ALU.add),
                             r=[d_tc], w=[d_yc])

            def ffn_fin(j):
                yg, d_g = yc[0][j % 2]
                yv_, d_v = yc[1][j % 2]
                S.op("act", lambda e: e.activation(out=yg[:], in_=yg[:], func=AF.Silu), r=[d_g], w=[d_g])
                S.op("dve", lambda e: e.tensor_tensor(out=aTg[:, j % 4, :], in0=yg[:], in1=yv_[:], op=ALU.mult),
                     r=[d_g, d_v], w=[d_aT[j % 4]])

            def ffn_down(grp):
                for nb in range(4):
                    wd, d_wd = wd_stream.get()
                    for t in range(8):
                        cvi = 0 if t < 4 else 1
                        pb = 3 + dctr[0] % 5
                        tm, d_tm = tmpd[dctr[0] % 2]
                        dctr[0] += 1
                        for jj in range(4):
                            S.op("pe", lambda e: e.matmul(PS[pb][:], lhsT=aTg[:, jj, t * 128:(t + 1) * 128], rhs=wd[:, jj, :],
                                                          start=(jj == 0), stop=(jj == 3)),
                                 r=[d_aT[jj], d_wd], w=[dPS[pb]], signal=(jj == 3))
                        S.op("dve", lambda e: e.tensor_tensor(out=tm[:], in0=PS[pb][:], in1=gbc[:, cvi, nb * 512:(nb + 1) * 512], op=ALU.mult),
                             r=[dPS[pb], d_gbc], w=[d_tm])
                        S.op("pool", lambda e: e.tensor_tensor(out=xmid[:, t, nb * 512:(nb + 1) * 512], in0=xmid[:, t, nb * 512:(nb + 1) * 512], in1=tm[:], op=ALU.add),
                             r=[d_tm], w=[d_xm[t]])

            for j in range(44):
                ffn_up(j)
                if j >= 1:
                    ffn_fin(j - 1)
                    if (j - 1) % 4 == 3:
                        ffn_down((j - 1) // 4)
            ffn_fin(43)
            ffn_down(10)
            S.barrier()

        S.mark("phase6")
        with arena.scope() as ph:
            fg = alloc(ph, "fg", (128, 2048)); d_fg = Dep()
            S.dma("sp", fg[:], FINALG.partition_broadcast(128), w=[d_fg])
            yo = [(alloc(ph, "yo%d" % i, (128, 2048)), Dep()) for i in range(2)]
            st = alloc(ph, "fst", (128, 16)); d_st = Dep()
            S.op("dve", lambda e: e.memset(st[:], 0.0), w=[d_st])
            for t in range(8):
                y_, d_yo = yo[t % 2]
                S.op("act", lambda e: e.activation(out=y_[:], in_=xmid[:, t, :], func=AF.Square, accum_out=st[:, 2 * t:2 * t + 1]),
                     r=[d_xm[t]], w=[d_yo, d_st])
                S.op("act", lambda e: e.activation(out=st[:, 2 * t + 1:2 * t + 2], in_=st[:, 2 * t:2 * t + 1], func=AF.Ln, bias=EPS, scale=1.0 / D),
                     r=[d_st], w=[d_st])
                S.op("act", lambda e: e.activation(out=st[:, 2 * t + 1:2 * t + 2], in_=st[:, 2 * t + 1:2 * t + 2], func=AF.Exp, scale=-0.5),
                     r=[d_st], w=[d_st])
                S.op("dve", lambda e: e.scalar_tensor_tensor(out=y_[:], in0=xmid[:, t, :], scalar=st[:, 2 * t + 1:2 * t + 2], in1=fg[:],
                                                             op0=ALU.mult, op1=ALU.mult), r=[d_xm[t], d_st, d_fg], w=[d_yo])
                dst = YP[t * 128:(t + 1) * 128, :] if t < 4 else YS[(t - 4) * 128:(t - 3) * 128, :]
                ev = S.dma("sp", dst, y_[:], r=[d_yo])
                out_deps.append(ev)
        for ev in out_deps:
            S._wait("sp", ev)
        S.mark("end")
        print("ops", S.nops, "waits", S.nwaits, "sim", S.simulate()[:2])
        if os.environ.get("KMARKS"):
            import json
            json.dump(S.marks, open(os.environ["KMARKS"], "w"))
    return nc


_NC_CACHE = {}


def kernel(**inputs):
    maps = _prep(inputs)
    stop = os.environ.get("KSTOP")
    if stop not in _NC_CACHE:
        _NC_CACHE[stop] = build(stop)
    nc = _NC_CACHE[stop]
    ncr = int(os.environ.get("KCORES", str(NCORES)))
    res = run_bass_kernel_spmd(nc, maps[:ncr], core_ids=list(range(ncr)))
    R = list(res.results) + [res.results[0]] * (NCORES - ncr)
    y_prompt = np.zeros((16, 256, 2048), np.float32)
    y_sample = np.zeros((4, 1024, 2048), np.float32)
    nk = np.zeros((16, 1, 256, 8, 128), np.float32)
    nv = np.zeros((16, 1, 256, 8, 128), np.float32)
    nsf = np.zeros((16, 1, 8, 128, 128), np.float32)
    nsb = np.zeros((16, 1, 8, 128, 128), np.float32)
    for c in range(NCORES):
        b, par = c // 2, c % 2
        r = R[c]
        yp = r["YP"].reshape(2, 256, 2048)
        ys = r["YS"]
        k = r["NK"].reshape(2, 256, 8, 128)
        v = r["NV"].reshape(2, 256, 8, 128)
        f, bw = r["NSF"], r["NSB"]
        if par:
            yp, k, v = yp[:, ::-1], k[:, ::-1], v[:, ::-1]
            ys = ys[::-1]
            f, bw = bw, f
            y_sample[b, 512:1024] = ys
        else:
            y_sample[b, 0:512] = ys
        y_prompt[2 * c:2 * c + 2] = yp
        nk[2 * c:2 * c + 2, 0] = k
        nv[2 * c:2 * c + 2, 0] = v
        nsf[2 * c:2 * c + 2, 0] = f
        nsb[2 * c:2 * c + 2, 0] = bw
    return (y_prompt, y_sample, nk, nv, nsf, nsb)
```

```python
import os
import math
import numpy as np
from contextlib import ExitStack
import concourse.bass as bass
import concourse.mybir as mybir
from concourse.bass_utils import run_bass_kernel_spmd

F32 = mybir.dt.float32
BF16 = mybir.dt.bfloat16
ALU = mybir.AluOpType
AF = mybir.ActivationFunctionType

D = 2048
NCORES = 8
EPS = 1e-6
DFF = 5632
LAM_INIT = 0.8 - 0.6 * math.exp(0.0)


class _RecEngine:
    def __getattr__(self, name):
        return lambda *a, **k: (name, a, k)


class Dep:
    __slots__ = ("w", "r", "excl")

    def __init__(self, excl=False):
        self.w = None
        self.r = {}
        self.excl = excl


class Sched:
    NDS = 24

    def __init__(self, nc, stack):
        self.nc = nc
        self.engs = {"pe": nc.tensor, "act": nc.scalar, "dve": nc.vector, "pool": nc.gpsimd, "sp": nc.sync}
        self.sem = {k: stack.enter_context(nc.semaphore("s_" + k)) for k in self.engs}
        self.cnt = {k: 0 for k in self.engs}
        self.seen = {k: {} for k in self.engs}
        self.dsem = [stack.enter_context(nc.semaphore("d%d" % i)) for i in range(self.NDS)]
        self.dval = [0] * self.NDS
        self.dnext = 0
        self.dnext_sw = 0
        self.nops = {k: 0 for k in self.engs}
        self.nwaits = 0
        self.trace = {k: [] for k in self.engs}
        self.marks = []
        self.rec = None

    def _wait(self, eng, ev):
        if ev is None:
            return
        key, sem, val = ev
        if self.seen[eng].get(key, 0) >= val:
            return
        if key == eng and val > self.cnt[eng]:
            return
        self.engs[eng].wait_ge(sem, val)
        self.trace[eng].append(("wait", key, val))
        self.nwaits += 1
        self.seen[eng][key] = val

    def _deps(self, eng, r, w):
        for d in r:
            self._wait(eng, d.w)
        for d in w:
            self._wait(eng, d.w)
            for ev in list(d.r.values()):
                self._wait(eng, ev)

    def _record(self, ev, r, w):
        for d in r:
            d.r[ev[0]] = ev
        for d in w:
            d.w = ev
            d.r = {}

    def op(self, eng, fn, r=(), w=(), signal=True):
        if self.rec is not None:
            name, a, k = fn(_RecEngine())
            self.rec.append(lambda: self.op(eng, lambda e: getattr(e, name)(*a, **k), r, w, signal))
            return None
        if any(d.excl for d in r):
            w = list(w) + [d for d in r if d.excl]
            r = [d for d in r if not d.excl]
        self._deps(eng, r, w)
        ins = fn(self.engs[eng])
        self.nops[eng] += 1
        if signal:
            self.cnt[eng] += 1
            ins.then_inc(self.sem[eng], 1)
            self.trace[eng].append(("inc", eng, 1))
            ev = (eng, self.sem[eng], self.cnt[eng])
        else:
            ev = (eng, self.sem[eng], self.cnt[eng] + 1)
        self._record(ev, r, w)
        return ins

    def dma(self, q, out, in_, r=(), w=(), evlist=None):
        if self.rec is not None:
            self.rec.append(lambda: self.dma(q, out, in_, r, w, evlist))
            return None
        half = self.NDS // 2
        if q == "pool":
            i = half + self.dnext_sw
            self.dnext_sw = (self.dnext_sw + 1) % half
        else:
            i = self.dnext
            self.dnext = (i + 1) % half
        key = "d%d" % i
        if self.dval[i] > 0:
            self._wait(q, (key, self.dsem[i], self.dval[i]))
        self._deps(q, r, w)
        ins = self.engs[q].dma_start(out=out, in_=in_)
        self.nops[q] += 1
        self.dval[i] += 16
        ins.then_inc(self.dsem[i], 16)
        self.trace[q].append(("inc", key, 16))
        ev = (key, self.dsem[i], self.dval[i])
        self._record(ev, r, w)
        if evlist is not None:
            evlist.append(ev)
        return ev

    def mark(self, label):
        if self.rec is not None:
            self.rec.append(lambda: self.mark(label))
            return
        self.marks.append((label, dict(self.nops)))

    def simulate(self):
        val = {}
        pc = {k: 0 for k in self.engs}
        progress = True
        while progress:
            progress = False
            for k in self.engs:
                tr = self.trace[k]
                while pc[k] < len(tr):
                    kind, key, v = tr[pc[k]]
                    if kind == "wait":
                        if val.get(key, 0) >= v:
                            pc[k] += 1
                            progress = True
                        else:
                            break
                    else:
                        val[key] = val.get(key, 0) + v
                        pc[k] += 1
                        progress = True
        stuck = {k: (pc[k], len(self.trace[k]), self.trace[k][pc[k]] if pc[k] < len(self.trace[k]) else None) for k in self.engs}
        ok = all(pc[k] == len(self.trace[k]) for k in self.engs)
        return ok, stuck, val

    def barrier(self):
        if self.rec is not None:
            self.rec.append(self.barrier)
            return
        self._barrier()

    def _barrier(self):
        evs = [(k, self.sem[k], self.cnt[k]) for k in self.engs if self.cnt[k] > 0]
        evs += [("d%d" % i, self.dsem[i], self.dval[i]) for i in range(self.NDS) if self.dval[i] > 0]
        for eng in self.engs:
            for ev in evs:
                self._wait(eng, ev)


class _Stop(Exception):
    pass


class WStream:
    def __init__(self, S, bufs, srcs, depth):
        self.S, self.bufs, self.srcs, self.depth = S, bufs, srcs, min(depth, len(bufs) - 1)
        self.i_issue = 0
        self.i_use = 0

    def prefetch(self, ahead=None):
        ahead = self.depth if ahead is None else min(ahead, len(self.bufs) - 1)
        while self.i_issue < min(len(self.srcs), self.i_use + ahead + 1):
            buf, d = self.bufs[self.i_issue % len(self.bufs)]
            self.S.dma("pool", buf[:], self.srcs[self.i_issue], w=[d])
            self.i_issue += 1

    def get(self):
        self.prefetch()
        buf = self.bufs[self.i_use % len(self.bufs)]
        self.i_use += 1
        return buf


class Arena:
    WORDS = 53200

    def __init__(self, nc, stack):
        self.t = stack.enter_context(nc.sbuf_tensor("arena", [128, self.WORDS], F32))
        self.ptr = 0
        self.peak = 0
        self.norelease = False

    def alloc(self, shape, dt=F32):
        n = 1
        for x in shape[1:]:
            n *= int(x)
        words = n if dt == F32 else (n + 1) // 2
        words = (words + 7) // 8 * 8
        off = self.ptr
        self.ptr += words
        self.peak = max(self.peak, self.ptr)
        assert self.ptr <= self.WORDS, ("SBUF arena overflow", self.ptr)
        ap = self.t[:, off:off + words]
        if dt != F32:
            ap = ap.bitcast(dt)
        ap = ap[:, 0:n]
        if len(shape) == 3:
            ap = ap.rearrange("p (a b) -> p a b", b=int(shape[2]))
        elif len(shape) == 4:
            ap = ap.rearrange("p (a b c) -> p a b c", b=int(shape[2]), c=int(shape[3]))
        return ap

    def scope(self):
        arena = self

        class _Scope:
            def __enter__(self_):
                self_.mark = arena.ptr
                return self_

            def __exit__(self_, *a):
                if not arena.norelease:
                    arena.ptr = self_.mark
                return False
        return _Scope()


def _tile_w(w, ncol_blk):
    K, N = w.shape
    return np.ascontiguousarray(w.reshape(K // 128, 128, N // ncol_blk, ncol_blk).transpose(2, 1, 0, 3))


def _consts():
    i = np.arange(128)
    blk = (i[:, None] // 64) == (i[None, :] // 64)
    c = {}
    c["ident"] = np.eye(128)
    c["ones"] = np.ones((128, 128))
    c["ublk"] = blk & (i[:, None] <= i[None, :])
    c["lblk"] = blk & (i[:, None] >= i[None, :])
    c["slblk"] = blk & (i[:, None] > i[None, :])
    c["sublk"] = blk & (i[:, None] < i[None, :])
    c["eblk"] = blk
    c["e0"] = np.broadcast_to((i[:, None] < 64), (128, 128))
    c["e1"] = np.broadcast_to((i[:, None] >= 64), (128, 128))
    d = i % 64
    ii = d % 32
    partner = np.where(ii < 16, i + 16, i - 16)
    prope = np.zeros((128, 128))
    prope[partner, i] = 1.0
    c["prope"] = prope
    names = ["ident", "ones", "ublk", "lblk", "slblk", "sublk", "eblk", "e0", "e1", "prope"]
    return np.ascontiguousarray(np.stack([c[n].astype(np.float32) for n in names], axis=1)), names


def _rope_tables(flip):
    t = np.arange(1024)
    if flip:
        t = t[::-1]
    rows = (t // 64).astype(np.float64)
    cols = (t % 64).astype(np.float64)
    inv = 10000.0 ** (-np.arange(0, 32, 2, dtype=np.float64) / 32.0)
    p = np.arange(128)
    d = p % 64
    half = d // 32
    ii = d % 32
    f = ii % 16
    pos = np.where(half[:, None] == 0, rows[None, :], cols[None, :])
    ang = pos * inv[f][:, None]
    cos = np.cos(ang)
    sin = np.sin(ang) * np.where(ii < 16, -1.0, 1.0)[:, None]
    return np.ascontiguousarray(np.stack([cos, sin], axis=1).astype(np.float32))


def _prep(inp):
    f32 = lambda a: np.ascontiguousarray(np.asarray(a, dtype=np.float32))
    w_in = f32(inp["w_in"])[0]
    heads = []
    for h in range(8):
        cols = np.concatenate([np.arange(o + h * 128, o + (h + 1) * 128)
                               for o in (0, 1024, 2048, 3072, 4096, 5120, 6144)])
        heads.append(_tile_w(w_in[:, cols], 128))
    WIN = np.ascontiguousarray(np.stack(heads, 0))
    MODW = _tile_w(f32(inp["mod_w"])[0], 512)
    MODB = f32(inp["mod_b"]).reshape(24, 512)
    WOUT = _tile_w(f32(inp["w_out"])[0], 512)
    w_up = f32(inp["w_up"])[0]
    upcols = np.concatenate([np.concatenate([np.arange(2 * r * 128, (2 * r + 2) * 128),
                                             np.arange(DFF + 2 * r * 128, DFF + (2 * r + 2) * 128)])
                             for r in range(22)])
    WUP = _tile_w(w_up[:, upcols], 512)
    WDOWN = np.ascontiguousarray(f32(inp["w_down"])[0].reshape(11, 4, 128, 4, 512).transpose(0, 3, 2, 1, 4))
    fm = lambda v: np.ascontiguousarray(f32(v).reshape(16, 128).T)
    GMIX, GFFN = fm(inp["norm_mix_g"]), fm(inp["norm_ffn_g"])
    FINALG = f32(inp["final_g"]).reshape(1, 2048)
    LAMV = np.concatenate([f32(inp[k]).reshape(-1) for k in ("lambda_q1", "lambda_k1", "lambda_q2", "lambda_k2")]).reshape(1, 256)
    SUBG = f32(inp["subln_g"]).reshape(1, 128)
    DNG = f32(inp["dn_norm_g"]).reshape(1, 128)
    cq = f32(inp["conv_qkv_w"])[0].reshape(3, 24, 128).transpose(2, 1, 0)
    cf = f32(inp["conv_ffn_w"])[0].reshape(3, 88, 128).transpose(2, 1, 0)
    wg = w_in[:, 7168:7200]
    alog = f32(inp["a_log"])[0].reshape(16)
    dtb = f32(inp["dt_bias"])[0].reshape(16)
    consts, _ = _consts()
    xp, xs = f32(inp["x_prompt"]), f32(inp["x_sample"])
    ck, cvv = f32(inp["cache_k"]), f32(inp["cache_v"])
    sf, sbw = f32(inp["state_fwd"]), f32(inp["state_bwd"])
    cvec, cctx = f32(inp["c"]), f32(inp["c_ctx"])
    swap = np.concatenate([np.arange(8, 16), np.arange(0, 8)])
    maps = []
    shared = {}
    for par in (0, 1):
        wgp = wg if par == 0 else wg[:, np.concatenate([swap, 16 + swap])]
        shared[par] = dict(
            WG=np.ascontiguousarray(wgp.reshape(16, 128, 32).transpose(1, 0, 2)),
            ALOG=np.ascontiguousarray((alog if par == 0 else alog[swap]).reshape(1, 16)),
            DTB=np.ascontiguousarray((dtb if par == 0 else dtb[swap]).reshape(1, 16)),
            CONVQ=np.ascontiguousarray(cq if par == 0 else cq[:, :, ::-1]),
            CONVF=np.ascontiguousarray(cf if par == 0 else cf[:, :, ::-1]),
            ROPE=_rope_tables(par == 1),
        )
    for c in range(NCORES):
        b, par = c // 2, c % 2
        fl = (lambda a, ax: a[(slice(None),) * ax + (slice(None, None, -1),)]) if par else (lambda a, ax: a)
        m = dict(
            XP=np.ascontiguousarray(fl(xp[2 * c:2 * c + 2], 1)).reshape(512, 2048),
            XS=np.ascontiguousarray(fl(xs[b], 0)),
            CK=np.ascontiguousarray(ck[b, 0].transpose(1, 0, 2)),
            CV=np.ascontiguousarray(cvv[b, 0].transpose(1, 0, 2)),
            SF0=np.ascontiguousarray((sbw if par else sf)[b, 0]),
            SB0=np.ascontiguousarray((sf if par else sbw)[b, 0]),
            CVEC=np.ascontiguousarray(np.stack([cctx, cvec[b]], 0).reshape(2, 16, 128).transpose(2, 1, 0)),
            MODW=MODW, MODB=MODB, WIN=WIN, WOUT=WOUT, WUP=WUP, WDOWN=WDOWN, GMIX=GMIX, GFFN=GFFN, FINALG=FINALG,
            LAMV=LAMV, SUBG=SUBG, DNG=DNG, CONSTS=consts,
        )
        m.update(shared[par])
        maps.append(m)
    return maps


def build(stop_after=None):
    nc = bass.Bass("TRN2", target_bir_lowering=False)
    di = lambda n, s: nc.dram_tensor(n, list(s), F32, kind="ExternalInput").ap()
    do = lambda n, s: nc.dram_tensor(n, list(s), F32, kind="ExternalOutput").ap()
    XP, XS = di("XP", (512, 2048)), di("XS", (1024, 2048))
    CK, CV = di("CK", (8, 256, 128)), di("CV", (8, 256, 128))
    SF0, SB0 = di("SF0", (8, 128, 128)), di("SB0", (8, 128, 128))
    CVEC = di("CVEC", (128, 16, 2))
    MODW, MODB = di("MODW", (24, 128, 16, 512)), di("MODB", (24, 512))
    WIN = di("WIN", (8, 7, 128, 16, 128))
    WOUT, WUP, WDOWN = di("WOUT", (4, 128, 16, 512)), di("WUP", (22, 128, 16, 512)), di("WDOWN", (11, 4, 128, 4, 512))
    GMIX, GFFN, FINALG = di("GMIX", (128, 16)), di("GFFN", (128, 16)), di("FINALG", (1, 2048))
    LAMV, SUBG, DNG = di("LAMV", (1, 256)), di("SUBG", (1, 128)), di("DNG", (1, 128))
    CONSTS = di("CONSTS", (128, 10, 128))
    WG, ALOG, DTB = di("WG", (128, 16, 32)), di("ALOG", (1, 16)), di("DTB", (1, 16))
    CONVQ, CONVF, ROPE = di("CONVQ", (128, 24, 3)), di("CONVF", (128, 88, 3)), di("ROPE", (128, 2, 1024))
    YP, YS = do("YP", (512, 2048)), do("YS", (512, 2048))
    NK, NV = do("NK", (512, 8, 128)), do("NV", (512, 8, 128))
    NSF, NSB = do("NSF", (2, 8, 128, 128)), do("NSB", (2, 8, 128, 128))

    with ExitStack() as top:
        S = Sched(nc, top)
        out_deps = []

        arena = Arena(nc, top)

        def alloc(stack, name, shape, dt=F32):
            return arena.alloc(shape, dt)

        PS = [top.enter_context(nc.psum_tensor("ps%d" % i, [128, 512], F32)) for i in range(8)]
        dPS = [Dep(excl=True) for _ in range(8)]

        cst = alloc(top, "cst", (128, 10, 128)); d_cst = Dep()
        S.dma("sp", cst[:], CONSTS, w=[d_cst])
        ident, ones, ublk, lblk, slblk, sublk, eblk, e0, e1 = [cst[:, i, :] for i in range(9)]
        propeb = alloc(top, "propeb", (128, 128), BF16)
        identb = alloc(top, "identb", (128, 128), BF16)
        S.op("dve", lambda e: e.tensor_copy(out=identb[:], in_=cst[:, 0, :]), r=[d_cst], w=[d_cst])
        S.op("dve", lambda e: e.tensor_copy(out=propeb[:], in_=cst[:, 9, :]), r=[d_cst], w=[d_cst])
        small = alloc(top, "small", (128, 1024)); d_small = Dep()
        S.dma("sp", small[:, 0:16], GMIX, w=[d_small])
        S.dma("sp", small[:, 16:32], GFFN, w=[d_small])
        S.dma("sp", small[:, 160:176], ALOG.partition_broadcast(128), w=[d_small])
        S.dma("sp", small[:, 176:192], DTB.partition_broadcast(128), w=[d_small])
        S.dma("sp", small[:, 192:448], LAMV.partition_broadcast(128), w=[d_small])
        S.dma("sp", small[:, 448:576], SUBG.partition_broadcast(128), w=[d_small])
        S.dma("sp", small[:, 576:704], DNG.partition_broadcast(128), w=[d_small])
        s1v = small[:, 32:64].rearrange("p (k c) -> p k c", c=2)
        b1v = small[:, 64:96].rearrange("p (k c) -> p k c", c=2)
        s2v = small[:, 96:128].rearrange("p (k c) -> p k c", c=2)
        b2v = small[:, 128:160].rearrange("p (k c) -> p k c", c=2)
        negA = small[:, 160:176]
        dtbb = small[:, 176:192]
        subgs = small[:, 448:576]
        dngb = small[:, 576:704]
        neglam = small[:, 704:705]
        S.op("act", lambda e: e.activation(out=negA, in_=negA, func=AF.Exp), r=[d_small], w=[d_small])
        S.op("dve", lambda e: e.tensor_scalar(out=negA, in0=negA, scalar1=-1.0, scalar2=None, op0=ALU.mult),
             r=[d_small], w=[d_small])
        S.op("dve", lambda e: e.tensor_scalar(out=subgs, in0=subgs, scalar1=1.0 - LAM_INIT, scalar2=None, op0=ALU.mult),
             r=[d_small], w=[d_small])
        S.op("dve", lambda e: e.tensor_tensor(out=small[:, 192:256], in0=small[:, 192:256], in1=small[:, 256:320], op=ALU.mult),
             r=[d_small], w=[d_small])
        S.op("dve", lambda e: e.tensor_tensor(out=small[:, 320:384], in0=small[:, 320:384], in1=small[:, 384:448], op=ALU.mult),
             r=[d_small], w=[d_small])
        S.op("dve", lambda e: e.reduce_sum(out=small[:, 705:706], in_=small[:, 192:256], axis=mybir.AxisListType.X),
             r=[d_small], w=[d_small])
        S.op("dve", lambda e: e.reduce_sum(out=small[:, 706:707], in_=small[:, 320:384], axis=mybir.AxisListType.X),
             r=[d_small], w=[d_small])
        S.op("act", lambda e: e.activation(out=small[:, 705:707], in_=small[:, 705:707], func=AF.Exp), r=[d_small], w=[d_small])
        S.op("dve", lambda e: e.tensor_tensor(out=neglam, in0=small[:, 706:707], in1=small[:, 705:706], op=ALU.subtract),
             r=[d_small], w=[d_small])
        S.op("dve", lambda e: e.tensor_scalar(out=neglam, in0=neglam, scalar1=-LAM_INIT, scalar2=None, op0=ALU.add),
             r=[d_small], w=[d_small])

        convq = alloc(top, "convq", (128, 24, 3)); d_convq = Dep()
        S.dma("sp", convq[:], CONVQ, w=[d_convq])
        wgb = alloc(top, "wgb", (128, 16, 32), BF16); d_wgb = Dep()
        S.dma("pool", wgb[:], WG, w=[d_wgb])

        cvs = alloc(top, "cvs", (128, 16, 2)); d_cvs = Dep()
        S.dma("sp", cvs[:], CVEC, w=[d_cvs])
        S.op("act", lambda e: e.activation(out=cvs[:], in_=cvs[:], func=AF.Silu), r=[d_cvs], w=[d_cvs])
        d_Lp = Dep()
        Ls = {}

        def make_L(scope):
            Lp = alloc(scope, "Lp", (128, 16, 128), BF16)
            L0 = alloc(scope, "L0", (128, 16, 128), BF16)
            L1 = alloc(scope, "L1", (128, 16, 128), BF16)
            for (dst, c0, c1, j) in ((Lp, 0, 64, 0), (Lp, 64, 128, 1), (L0, 0, 128, 0), (L1, 0, 128, 1)):
                S.op("dve", lambda e: e.tensor_copy(out=dst[:, :, c0:c1],
                                                    in_=cvs[:, :, j:j + 1].broadcast_to([128, 16, c1 - c0])),
                     r=[d_cvs], w=[d_Lp])
            Ls["Lp"], Ls["L0"], Ls["L1"] = Lp, L0, L1

        oT = alloc(top, "oT", (128, 16, 1088), BF16)
        d_oT = [Dep() for _ in range(16)]

        def mod_block(stk_bufs, blk, lhs, psum_i):
            wbuf, d_w, mbb, d_mbb, mrow, d_mrow = stk_bufs
            S.dma("sp", mbb[:], MODB[blk:blk + 1, :].partition_broadcast(128), w=[d_mbb])
            for kc in range(16):
                S.op("pe", lambda e: e.matmul(PS[psum_i][:], lhsT=lhs[:, kc, :], rhs=wbuf[:, kc, :],
                                              start=(kc == 0), stop=(kc == 15)),
                     r=[d_w, d_Lp], w=[dPS[psum_i]], signal=(kc == 15))
            S.op("dve", lambda e: e.tensor_tensor(out=mrow[:], in0=PS[psum_i][:], in1=mbb[:], op=ALU.add),
                 r=[dPS[psum_i], d_mbb], w=[d_mrow])

        def mod_featmajor(stk_bufs, vec_i, dstv, psA, psB):
            wstream, mbb, d_mbb, mrow, d_mrow = stk_bufs
            for q in range(4):
                blk = vec_i * 4 + q
                wb, d_w = wstream.get()
                mod_block((wb, d_w, mbb, d_mbb, mrow, d_mrow), blk, Ls["Lp"], psA)
                for j in range(4):
                    S.op("pe", lambda e: e.transpose(out=PS[psB][:, j * 128:(j + 1) * 128],
                                                     in_=mrow[:, j * 128:(j + 1) * 128], identity=ident),
                         r=[d_mrow, d_cst], w=[dPS[psB]], signal=(j == 3))
                for j in range(4):
                    kc = q * 4 + j
                    S.op("dve", lambda e: e.tensor_copy(out=dstv[:, kc, :], in_=PS[psB][:, j * 128:(j + 1) * 128:64]),
                         r=[dPS[psB]], w=[d_small])

        with arena.scope() as ph:
            make_L(ph)
            wbufs = [(alloc(ph, "mw%d" % i, (128, 16, 512), BF16), Dep()) for i in range(2)]
            mbb = alloc(ph, "mbb", (128, 512)); d_mbb = Dep()
            mrow = alloc(ph, "mrow", (128, 512)); d_mrow = Dep()
            bufs = (WStream(S, wbufs, [MODW[b_] for b_ in range(0, 8)], 1), mbb, d_mbb, mrow, d_mrow)
            bufs[0].prefetch()
            mod_featmajor(bufs, 0, b1v, 0, 1)
            mod_featmajor(bufs, 1, s1v, 0, 1)
            S.op("dve", lambda e: e.tensor_scalar(out=small[:, 32:64], in0=small[:, 32:64], scalar1=1.0, scalar2=None,
                                                  op0=ALU.add), r=[d_small], w=[d_small])
            S.op("dve", lambda e: e.tensor_tensor(out=s1v, in0=s1v, in1=small[:, 0:16].unsqueeze(2).broadcast_to([128, 16, 2]),
                                                  op=ALU.mult), r=[d_small], w=[d_small])
            S.barrier()

        if stop_after == "p0":
            print("ops", S.nops, "waits", S.nwaits, "sim", S.simulate()[:2])
            return nc
        NHEADS = int(os.environ.get("KHEADS", "8"))
        d_hT = Dep()
        hTbox = {}

        def rstd_from_ss(st, d_st, n, scale):
            S.op("act", lambda e: e.activation(out=st[:, n:2 * n], in_=st[:, 0:n], func=AF.Ln, bias=EPS, scale=scale),
                 r=[d_st], w=[d_st])
            S.op("act", lambda e: e.activation(out=st[:, n:2 * n], in_=st[:, n:2 * n], func=AF.Exp, scale=-0.5),
                 r=[d_st], w=[d_st])

        def norm_to_featmajor(stack, src_rows, ntile, dst, d_dst, sv, bv, cvi, tag):
            xts = [(alloc(stack, "%sxt%d" % (tag, i), (128, 2048)), Dep()) for i in range(2)]
            xns = [(alloc(stack, "%sxn%d" % (tag, i), (128, 2048)), Dep()) for i in range(2)]
            st = alloc(stack, tag + "st", (128, 2 * ntile)); d_st = Dep()
            S.op("dve", lambda e: e.memset(st[:], 0.0), w=[d_st])
            for t in range(ntile):
                xt, d_xt = xts[t % 2]
                xn, d_xn = xns[t % 2]
                S.dma("sp", xt[:], src_rows(t), w=[d_xt])
                S.op("act", lambda e: e.activation(out=xn[:], in_=xt[:], func=AF.Square, accum_out=st[:, 2 * t:2 * t + 1]),
                     r=[d_xt], w=[d_xn, d_st])
                S.op("act", lambda e: e.activation(out=st[:, 2 * t + 1:2 * t + 2], in_=st[:, 2 * t:2 * t + 1], func=AF.Ln,
                                                   bias=EPS, scale=1.0 / D), r=[d_st], w=[d_st])
                S.op("act", lambda e: e.activation(out=st[:, 2 * t + 1:2 * t + 2], in_=st[:, 2 * t + 1:2 * t + 2],
                                                   func=AF.Exp, scale=-0.5), r=[d_st], w=[d_st])
                S.op("dve", lambda e: e.tensor_scalar(out=xn[:], in0=xt[:], scalar1=st[:, 2 * t + 1:2 * t + 2], scalar2=None,
                                                      op0=ALU.mult), r=[d_xt, d_st], w=[d_xn])
                for g in range(4):
                    for j in range(4):
                        kc = g * 4 + j
                        S.op("pe", lambda e: e.transpose(out=PS[g][:, j * 128:(j + 1) * 128],
                                                         in_=xn[:, kc * 128:(kc + 1) * 128], identity=ident),
                             r=[d_xn, d_cst], w=[dPS[g]], signal=(j == 3))
                    for j in range(4):
                        kc = g * 4 + j
                        if j % 2 == 0:
                            S.op("act", lambda e: e.activation(out=dst[:, kc, t * 128:(t + 1) * 128],
                                                               in_=PS[g][:, j * 128:(j + 1) * 128], func=AF.Identity,
                                                               bias=bv[:, kc, cvi:cvi + 1], scale=sv[:, kc, cvi:cvi + 1]),
                                 r=[dPS[g], d_small], w=[d_dst])
                        else:
                            S.op("dve", lambda e: e.tensor_scalar(out=dst[:, kc, t * 128:(t + 1) * 128],
                                                                  in0=PS[g][:, j * 128:(j + 1) * 128],
                                                                  scalar1=sv[:, kc, cvi:cvi + 1], scalar2=bv[:, kc, cvi:cvi + 1],
                                                                  op0=ALU.mult, op1=ALU.add),
                                 r=[dPS[g], d_small], w=[d_dst])

        def passes(n):
            return [(a, min(a + 512, n)) for a in range(0, n, 512)]

        def proj(wt, d_wt, c0, c1, psum_i, M=128, mo=0):
            for kc in range(16):
                S.op("pe", lambda e: e.matmul(PS[psum_i][0:M, 0:c1 - c0], lhsT=wt[:, kc, mo:mo + M], rhs=hTbox["hT"][:, kc, c0:c1],
                                              start=(kc == 0), stop=(kc == 15)),
                     r=[d_wt, d_hT], w=[dPS[psum_i]], signal=(kc == 15))

        def mixer_group(gi):
            isS = gi == 1
            ntok = 1024 if isS else 512
            ntile = ntok // 128
            mcols = 576 if isS else 512
            ocol0 = 512 if isS else 0
            seqs = [(0, 1024)] if isS else [(0, 256), (256, 512)]
            mt = 5 if isS else 4
            X = XS if isS else XP
            with arena.scope() as ph:
                norm_to_featmajor(ph, lambda t: X[t * 128:(t + 1) * 128, :], ntile, hTbox["hT"], d_hT, s1v, b1v, gi, "n1")
                S.barrier()
            if stop_after == "n1":
                raise _Stop()
            with arena.scope() as gs:
                G = alloc(gs, "G", (128, 10, ntile, 16)); d_G = Dep()
                for t in range(ntile):
                    for kc in range(16):
                        S.op("pe", lambda e: e.matmul(PS[0][:, t * 32:(t + 1) * 32], lhsT=hTbox["hT"][:, kc, t * 128:(t + 1) * 128],
                                                      rhs=wgb[:, kc, :], start=(kc == 0), stop=(kc == 15)),
                             r=[d_hT, d_wgb], w=[dPS[0]], signal=(kc == 15 and t == ntile - 1))
                pg = PS[0][:, 0:ntile * 32].rearrange("p (t c) -> p t c", c=32)
                S.op("act", lambda e: e.activation(out=G[:, 0], in_=pg[:, :, 0:16], func=AF.Exp, scale=-1.0), r=[dPS[0]], w=[d_G])
                S.op("dve", lambda e: e.tensor_scalar(out=G[:, 0], in0=G[:, 0], scalar1=1.0, scalar2=None, op0=ALU.add), r=[d_G], w=[d_G])
                S.op("dve", lambda e: e.reciprocal(out=G[:, 0], in_=G[:, 0]), r=[d_G], w=[d_G])
                S.op("dve", lambda e: e.tensor_tensor(out=G[:, 1], in0=pg[:, :, 16:32],
                                                      in1=dtbb.unsqueeze(1).broadcast_to([128, ntile, 16]), op=ALU.add),
                     r=[dPS[0], d_small], w=[d_G])
                S.op("act", lambda e: e.activation(out=G[:, 1], in_=G[:, 1], func=AF.Exp), r=[d_G], w=[d_G])
                S.op("act", lambda e: e.activation(out=G[:, 1], in_=G[:, 1], func=AF.Ln, bias=1.0), r=[d_G], w=[d_G])
                S.op("dve", lambda e: e.tensor_tensor(out=G[:, 1], in0=G[:, 1],
                                                      in1=negA.unsqueeze(1).broadcast_to([128, ntile, 16]), op=ALU.mult),
                     r=[d_G, d_small], w=[d_G])
                gflat = G[:, 1].rearrange("p t c -> p (t c)")
                n16 = ntile * 16
                for i, m in enumerate((ublk, lblk, eblk)):
                    S.op("pe", lambda e: e.matmul(PS[1][:, i * n16:(i + 1) * n16], lhsT=m, rhs=gflat, start=True, stop=True),
                         r=[d_G, d_cst], w=[dPS[1]], signal=(i == 2))
                for i, m in enumerate((e0, e1)):
                    S.op("pe", lambda e: e.matmul(PS[2][:, i * n16:(i + 1) * n16], lhsT=m, rhs=gflat, start=True, stop=True),
                         r=[d_G, d_cst], w=[dPS[2]], signal=(i == 1))
                p1 = lambda i: PS[1][:, i * n16:(i + 1) * n16].rearrange("p (t c) -> p t c", c=16)
                p2 = lambda i: PS[2][:, i * n16:(i + 1) * n16].rearrange("p (t c) -> p t c", c=16)
                S.op("dve", lambda e: e.tensor_copy(out=G[:, 2, :, 0:8], in_=p1(0)[:, :, 0:8]), r=[dPS[1]], w=[d_G])
                S.op("dve", lambda e: e.tensor_copy(out=G[:, 2, :, 8:16], in_=p1(1)[:, :, 8:16]), r=[dPS[1]], w=[d_G])
                S.op("dve", lambda e: e.tensor_copy(out=G[:, 3], in_=p1(2)), r=[dPS[1]], w=[d_G])
                S.op("act", lambda e: e.activation(out=G[:, 4], in_=G[:, 2], func=AF.Exp), r=[d_G], w=[d_G])
                S.op("dve", lambda e: e.tensor_tensor(out=G[:, 9], in0=G[:, 3], in1=G[:, 2], op=ALU.subtract), r=[d_G], w=[d_G])
                S.op("act", lambda e: e.activation(out=G[:, 5], in_=G[:, 9], func=AF.Exp), r=[d_G], w=[d_G])
                S.op("act", lambda e: e.activation(out=G[:, 6], in_=p2(0), func=AF.Exp), r=[dPS[2]], w=[d_G])
                S.op("act", lambda e: e.activation(out=G[:, 7], in_=p2(1), func=AF.Exp), r=[dPS[2]], w=[d_G])
                S.op("dve", lambda e: e.tensor_tensor(out=G[:, 8], in0=G[:, 0], in1=G[:, 4], op=ALU.mult), r=[d_G], w=[d_G])
                BETA, GG, GC, EGC, EKD, DL, BEGC = G[:, 0], G[:, 1], G[:, 2], G[:, 4], G[:, 5], (G[:, 6], G[:, 7]), G[:, 8]
                if stop_after == "gates":
                    S.barrier()
                    raise _Stop()

                wch = [(alloc(gs, "wch%d" % i, (128, 16, 128), BF16), Dep()) for i in range(6)]
                win_stream = WStream(S, wch, [WIN[h_, c_] for h_ in range(NHEADS) for c_ in range(7)], 5)
                win_stream.prefetch()

                def load_w(h, ci):
                    return win_stream.get()

                def attention_head(h):
                    S.mark("g%d h%d attn" % (gi, h))
                    with arena.scope() as hs:
                        nk = ntok + (256 if isS else 0)
                        nkt = nk // 128
                        qT = alloc(hs, "qT", (128, mcols), BF16); d_qT = Dep()
                        kT = alloc(hs, "kT", (128, nk), BF16); d_kT = Dep()
                        vaug = alloc(hs, "vaug", (128, nkt, 132), BF16); d_va = Dep()
                        S.op("pool", lambda e: e.memset(vaug[:], 1.0), w=[d_va])
                        if stop_after == "attnA":
                            load_w(h, 0)
                            S.barrier()
                            raise _Stop()
                        tmpf = [(alloc(hs, "tmpf%d" % i, (128, 512)), Dep()) for i in range(2)]
                        tmpb = [(alloc(hs, "tmpb%d" % i, (128, 512), BF16), Dep()) for i in range(2)]
                        if isS:
                            rope = alloc(hs, "rope", (128, 2, 1024)); d_rope = Dep()
                            S.dma("sp", rope[:], ROPE, w=[d_rope])
                        stage = alloc(hs, "stage", (128, 4, 128)); d_stage = Dep()

                        def qk_proj(ci, dst, d_dst, ncols, keep_f32=None):
                            wt, d_wt = load_w(h, ci)
                            for pi, (c0, c1) in enumerate(passes(ncols)):
                                n = c1 - c0
                                pb = pi % 2
                                proj(wt, d_wt, c0, c1, pb)
                                KQ = int(os.environ.get("KQ", "9"))
                                if KQ == 1:
                                    continue
                                if keep_f32 is not None and KQ >= 3:
                                    S.op("act", lambda e: e.activation(out=keep_f32[0][:, c0:c1], in_=PS[pb][:, 0:n], func=AF.Identity),
                                         r=[dPS[pb]], w=[keep_f32[1]])
                                if not isS:
                                    S.op("dve", lambda e: e.tensor_copy(out=dst[:, c0:c1], in_=PS[pb][:, 0:n]),
                                         r=[dPS[pb]], w=[d_dst])
                                else:
                                    tb, d_tb = tmpb[pi % 2]
                                    tf, d_tf = tmpf[pi % 2]
                                    S.op("act", lambda e: e.activation(out=tb[:, 0:n], in_=PS[pb][:, 0:n], func=AF.Identity), r=[dPS[pb]], w=[d_tb])
                                    S.op("dve", lambda e: e.tensor_tensor(out=tf[:, 0:n], in0=PS[pb][:, 0:n],
                                                                          in1=rope[:, 0, c0:c1], op=ALU.mult),
                                         r=[dPS[pb], d_rope], w=[d_tf])
                                    S.op("pe", lambda e: e.matmul(PS[2 + pb][:, 0:n], lhsT=propeb[:], rhs=tb[:, 0:n],
                                                                  start=True, stop=True), r=[d_tb, d_cst], w=[dPS[2 + pb]])
                                    S.op("dve", lambda e: e.tensor_tensor(out=tb[:, 0:n], in0=PS[2 + pb][:, 0:n],
                                                                          in1=rope[:, 1, c0:c1], op=ALU.mult),
                                         r=[dPS[2 + pb], d_rope], w=[d_tb])
                                    S.op("pool", lambda e: e.tensor_tensor(out=dst[:, c0:c1], in0=tf[:, 0:n], in1=tb[:, 0:n],
                                                                           op=ALU.add), r=[d_tf, d_tb], w=[d_dst])

                        kf = None
                        if not isS:
                            kf = (alloc(hs, "kf", (128, 512)), Dep())
                        qk_proj(0, qT, d_qT, mcols)
                        qk_proj(1, kT, d_kT, ntok, keep_f32=kf)
                        if stop_after == "attnB":
                            S.barrier()
                            raise _Stop()
                        avf = alloc(hs, "avf", (128, ntok)); d_avf = Dep()
                        wt, d_wt = load_w(h, 2)
                        for pi, (c0, c1) in enumerate(passes(ntok)):
                            proj(wt, d_wt, c0, c1, pi % 2)
                            S.op("act", lambda e: e.activation(out=avf[:, c0:c1], in_=PS[pi % 2][:, 0:c1 - c0], func=AF.Identity),
                                 r=[dPS[pi % 2]], w=[d_avf])
                        for t in range(ntile):
                            pb = 4 + t % 2
                            S.op("pe", lambda e: e.transpose(out=PS[pb][:, 0:128], in_=avf[:, t * 128:(t + 1) * 128], identity=ident),
                                 r=[d_avf, d_cst], w=[dPS[pb]])
                            S.op("act", lambda e: e.activation(out=vaug[:, t, 0:128], in_=PS[pb][:, 0:128], func=AF.Identity), r=[dPS[pb]], w=[d_va])
                            if not isS:
                                S.op("dve", lambda e: e.tensor_copy(out=stage[:, t, :], in_=PS[pb][:, 0:128]),
                                     r=[dPS[pb]], w=[d_stage])
                        if not isS:
                            S.dma("sp", NV[:, h, :].rearrange("(t p) d -> p t d", p=128), stage[:], r=[d_stage], evlist=out_deps)
                            kf_t, d_kf = kf
                            stage2 = alloc(hs, "stage2", (128, 4, 128)); d_stage2 = Dep()
                            for t in range(4):
                                pb = 4 + t % 2
                                S.op("pe", lambda e: e.transpose(out=PS[pb][:, 0:128], in_=kf_t[:, t * 128:(t + 1) * 128], identity=ident),
                                     r=[d_kf, d_cst], w=[dPS[pb]])
                                S.op("dve", lambda e: e.tensor_copy(out=stage2[:, t, :], in_=PS[pb][:, 0:128]),
                                     r=[dPS[pb]], w=[d_stage2])
                            S.dma("sp", NK[:, h, :].rearrange("(t p) d -> p t d", p=128), stage2[:], r=[d_stage2], evlist=out_deps)
                        else:
                            ckf = alloc(hs, "ckf", (128, 2, 128)); d_ckf = Dep()
                            S.dma("sp", ckf[:], CK[h].rearrange("(t p) d -> p t d", p=128), w=[d_ckf])
                            S.dma("pool", vaug[:, 8:10, 0:128], CV[h].rearrange("(t p) d -> p t d", p=128), w=[d_va])
                            for t in range(2):
                                pb = 4 + t
                                S.op("pe", lambda e: e.transpose(out=PS[pb][:, 0:128], in_=ckf[:, t, :], identity=ident),
                                     r=[d_ckf, d_cst], w=[dPS[pb]])
                                S.op("dve", lambda e: e.tensor_copy(out=kT[:, 1024 + t * 128:1024 + (t + 1) * 128],
                                                                    in_=PS[pb][:, 0:128]), r=[dPS[pb]], w=[d_kT])
                        if stop_after == "attnproj":
                            S.barrier()
                            raise _Stop()
                        S.mark("g%d h%d scores" % (gi, h))
                        On = alloc(hs, "On", (128, 2, mt, 128)); d_On = Dep()
                        Eall = alloc(hs, "Eall", (128, nkt, 576), BF16); d_E = Dep()
                        rden = alloc(hs, "rden", (128, 16)); d_rden = Dep()
                        ectr = 0
                        for (q0, q1) in ([(0, 576)] if isS else [(0, 256), (256, 512)]):
                            nq = q1 - q0
                            kts = list(range(nkt)) if isS else [q0 // 128, q0 // 128 + 1]
                            qtl = [(a, min(a + 128, nq)) for a in range(0, nq, 128)]
                            for m in range(2):
                                rows = slice(m * 64, m * 64 + 64)
                                for ki, kt in enumerate(kts):
                                    ectr += 1
                                    for pi, (a, b) in enumerate(passes(nq)):
                                        pb = 2 * (ectr % 2) + pi
                                        S.op("pe", lambda e: e.matmul(PS[pb][:, 0:b - a], lhsT=kT[rows, kt * 128:(kt + 1) * 128],
                                                                      rhs=qT[rows, q0 + a:q0 + b], start=True, stop=True),
                                             r=[d_kT, d_qT], w=[dPS[pb]])
                                        S.op("act", lambda e: e.activation(out=Eall[:, ki, a:b], in_=PS[pb][:, 0:b - a], func=AF.Exp,
                                                                           scale=0.125), r=[dPS[pb]], w=[d_E])
                                for qi, (a, b) in enumerate(qtl):
                                    pb = 4 + qi // 3
                                    co = (qi % 3) * 129
                                    for ki, kt in enumerate(kts):
                                        S.op("pe", lambda e: e.matmul(PS[pb][0:b - a, co:co + 129], lhsT=Eall[:, ki, a:b],
                                                                      rhs=vaug[:, kt, 0:129], start=(ki == 0), stop=(ki == len(kts) - 1)),
                                             r=[d_E, d_va], w=[dPS[pb]], signal=(ki == len(kts) - 1))
                                for qi, (a, b) in enumerate(qtl):
                                    pb = 4 + qi // 3
                                    co = (qi % 3) * 129
                                    n = b - a
                                    tq = (q0 + a) // 128
                                    S.op("dve", lambda e: e.reciprocal(out=rden[0:n, qi:qi + 1], in_=PS[pb][0:n, co + 128:co + 129]),
                                         r=[dPS[pb]], w=[d_rden])
                                    S.op("dve", lambda e: e.tensor_scalar(out=On[0:n, m, tq, :], in0=PS[pb][0:n, co:co + 128],
                                                                          scalar1=rden[0:n, qi:qi + 1], scalar2=None, op0=ALU.mult),
                                         r=[dPS[pb], d_rden], w=[d_On])
                        if stop_after == "attnpv":
                            S.barrier()
                            raise _Stop()
                        st = alloc(hs, "ast", (128, 2 * mt)); d_st = Dep()
                        S.op("dve", lambda e: e.memset(st[:], 0.0), w=[d_st])
                        for tq in range(mt):
                            n = 64 if (isS and tq == 4) else 128
                            S.op("dve", lambda e: e.scalar_tensor_tensor(out=On[0:n, 0, tq, :], in0=On[0:n, 1, tq, :], scalar=neglam[0:n, :],
                                                                         in1=On[0:n, 0, tq, :], op0=ALU.mult, op1=ALU.add),
                                 r=[d_On, d_small], w=[d_On])
                            S.op("act", lambda e: e.activation(out=On[0:n, 1, tq, :], in_=On[0:n, 0, tq, :], func=AF.Square,
                                                               accum_out=st[0:n, tq:tq + 1]), r=[d_On], w=[d_On, d_st])
                        rstd_from_ss(st, d_st, mt, 1.0 / 128)
                        for tq in range(mt):
                            n = 64 if (isS and tq == 4) else 128
                            pb = 2 + tq % 2
                            S.op("dve", lambda e: e.scalar_tensor_tensor(out=On[0:n, 1, tq, :], in0=On[0:n, 0, tq, :],
                                                                         scalar=st[0:n, mt + tq:mt + tq + 1], in1=subgs[0:n, :],
                                                                         op0=ALU.mult, op1=ALU.mult),
                                 r=[d_On, d_st, d_small], w=[d_On])
                            S.op("pe", lambda e: e.transpose(out=PS[pb][:, 0:n], in_=On[0:n, 1, tq, :], identity=cst[0:n, 0, 0:n]),
                                 r=[d_On, d_cst], w=[dPS[pb]])
                            S.op("act", lambda e: e.activation(out=oT[:, h, ocol0 + tq * 128:ocol0 + tq * 128 + n], in_=PS[pb][:, 0:n], func=AF.Identity),
                                 r=[dPS[pb]], w=[d_oT[h]])
                        if S.rec is None:
                            S.barrier()

                attention_head(0)
                for h in range(NHEADS):
                    if stop_after == "attn":
                        if h + 1 < NHEADS:
                            attention_head(h + 1)
                        continue
                    with arena.scope() as hs:
                        S.mark("g%d h%d dnproj" % (gi, h))

                        def hook(h=h):
                            if h + 1 >= NHEADS or os.environ.get("KNOILV"):
                                if h + 1 < NHEADS:
                                    return None
                                return []
                            S.rec = []
                            arena.norelease = True
                            attention_head(h + 1)
                            arena.norelease = False
                            rec, S.rec = S.rec, None
                            return rec

                        dn_head(hs, gi, h, load_w, G_=(BETA, GG, GC, EGC, EKD, DL, BEGC), d_G=d_G, interleave=hook)
                        S.barrier()
                    if os.environ.get("KNOILV") and h + 1 < NHEADS:
                        attention_head(h + 1)
                S.barrier()

        def dn_head(hs, gi, h, load_w, G_, d_G, interleave=None):
            BETA, GG, GC, EGC, EKD, DL, BEGC = G_
            isS = gi == 1
            ntok = 1024 if isS else 512
            ntile = ntok // 128
            mcols = 576 if isS else 512
            ocol0 = 512 if isS else 0
            mt = 5 if isS else 4
            seqs = [(0, 1024)] if isS else [(0, 256), (256, 512)]
            yq = alloc(hs, "yq", (128, ntok))
            yqb = alloc(hs, "yqb", (128, ntok), BF16); ykb = alloc(hs, "ykb", (128, ntok), BF16); d_yb = Dep()
            gsT = alloc(hs, "gsT", (128, mcols)); d_gs = Dep()
            dirbuf = []
            for d_ in range(2):
                ntd = mt if (isS and d_ == 0) else ntile
                dirbuf.append(dict(uw=alloc(hs, "uw%d" % d_, (128, ntd, 256), BF16), kdq=alloc(hs, "kdq%d" % d_, (128, ntd, 256), BF16), d=Dep()))
            inner_mark = arena.ptr
            yk = alloc(hs, "yk", (128, ntok)); yv = alloc(hs, "yv", (128, ntok))
            d_y = [Dep(), Dep(), Dep()]
            zrs = [(alloc(hs, "zr%d" % i_, (128, ntok)), Dep()) for i_ in range(2)]
            for which, (ci, yy) in enumerate(((3, yq), (4, yk), (5, yv))):
                zr, d_zr = zrs[which % 2]
                wt, d_wt = load_w(h, ci)
                cw = convq[:, which * 8 + h, :]
                for pi, (c0, c1) in enumerate(passes(ntok)):
                    proj(wt, d_wt, c0, c1, pi % 2)
                    S.op("act", lambda e: e.activation(out=zr[:, c0:c1], in_=PS[pi % 2][:, 0:c1 - c0], func=AF.Identity), r=[dPS[pi % 2]], w=[d_zr])
                S.op("dve", lambda e: e.tensor_scalar(out=yy[:], in0=zr[:], scalar1=cw[:, 1:2], scalar2=None, op0=ALU.mult),
                     r=[d_zr, d_convq], w=[d_y[which]])
                for (a, b) in seqs:
                    S.op("dve", lambda e: e.scalar_tensor_tensor(out=yy[:, a + 1:b], in0=zr[:, a:b - 1], scalar=cw[:, 0:1],
                                                                 in1=yy[:, a + 1:b], op0=ALU.mult, op1=ALU.add),
                         r=[d_zr, d_convq], w=[d_y[which]])
                    S.op("dve", lambda e: e.scalar_tensor_tensor(out=yy[:, a:b - 1], in0=zr[:, a + 1:b], scalar=cw[:, 2:3],
                                                                  in1=yy[:, a:b - 1], op0=ALU.mult, op1=ALU.add),
                         r=[d_zr, d_convq], w=[d_y[which]])
                S.op("act", lambda e: e.activation(out=yy[:], in_=yy[:], func=AF.Silu), r=[d_y[which]], w=[d_y[which]])
            wt, d_wt = load_w(h, 6)
            for pi, (c0, c1) in enumerate(passes(mcols)):
                proj(wt, d_wt, c0, c1, pi % 2)
                S.op("act", lambda e: e.activation(out=gsT[:, c0:c1], in_=PS[pi % 2][:, 0:c1 - c0], func=AF.Silu),
                     r=[dPS[pi % 2]], w=[d_gs])
            for which, yy in ((0, yq), (1, yk)):
                zr, d_zr = zrs[(which + 1) % 2]
                S.op("pool", lambda e: e.tensor_tensor(out=zr[:], in0=yy[:], in1=yy[:], op=ALU.mult), r=[d_y[which]], w=[d_zr])
                for pi, (c0, c1) in enumerate(passes(ntok)):
                    pb = 2 + pi % 2
                    n = c1 - c0
                    S.op("pe", lambda e: e.matmul(PS[pb][:, 0:n], lhsT=ones, rhs=zr[:, c0:c1], start=True, stop=True),
                         r=[d_zr, d_cst], w=[dPS[pb]])
                    S.op("act", lambda e: e.activation(out=zr[:, c0:c1], in_=PS[pb][:, 0:n], func=AF.Ln, bias=EPS, scale=1.0),
                         r=[dPS[pb]], w=[d_zr])
                    S.op("act", lambda e: e.activation(out=zr[:, c0:c1], in_=zr[:, c0:c1], func=AF.Exp, scale=-0.5),
                         r=[d_zr], w=[d_zr])
                sc = 128.0 ** -0.5 if which == 0 else 1.0
                S.op("dve", lambda e: e.scalar_tensor_tensor(out=yy[:], in0=yy[:], scalar=sc, in1=zr[:], op0=ALU.mult, op1=ALU.mult),
                     r=[d_zr, d_y[which]], w=[d_y[which]])
                S.op("act", lambda e: e.activation(out=(yqb if which == 0 else ykb)[:], in_=yy[:], func=AF.Identity),
                     r=[d_y[which]], w=[d_yb])
            ktok = alloc(hs, "ktok", (128, ntile, 128)); vtok = alloc(hs, "vtok", (128, ntile, 128)); d_kv = Dep()
            KK = alloc(hs, "KK", (128, ntile, 128)); QKT = alloc(hs, "QKT", (128, mt, 128)); d_KQ = Dep()
            for t in range(ntile):
                ts = slice(t * 128, (t + 1) * 128)
                pb = 4 + t % 2
                S.op("pe", lambda e: e.transpose(out=PS[pb][:, 0:128], in_=yk[:, ts], identity=ident), r=[d_y[1], d_cst], w=[dPS[pb]], signal=False)
                S.op("pe", lambda e: e.transpose(out=PS[pb][:, 128:256], in_=yv[:, ts], identity=ident), r=[d_y[2], d_cst], w=[dPS[pb]])
                S.op("act", lambda e: e.activation(out=ktok[:, t, :], in_=PS[pb][:, 0:128], func=AF.Identity), r=[dPS[pb]], w=[d_kv])
                S.op("act", lambda e: e.activation(out=vtok[:, t, :], in_=PS[pb][:, 128:256], func=AF.Identity), r=[dPS[pb]], w=[d_kv])
                pb2 = 6 + t % 2
                S.op("pe", lambda e: e.matmul(PS[pb2][:, 0:128], lhsT=ykb[:, ts], rhs=ykb[:, ts], start=True, stop=True),
                     r=[d_yb], w=[dPS[pb2]], signal=(t >= mt))
                S.op("dve", lambda e: e.tensor_copy(out=KK[:, t, :], in_=PS[pb2][:, 0:128]), r=[dPS[pb2]], w=[d_KQ]) if t >= mt else None
                if t < mt:
                    S.op("pe", lambda e: e.matmul(PS[pb2][:, 128:256], lhsT=ykb[:, ts], rhs=yqb[:, ts], start=True, stop=True),
                         r=[d_yb], w=[dPS[pb2]])
                    S.op("dve", lambda e: e.tensor_copy(out=KK[:, t, :], in_=PS[pb2][:, 0:128]), r=[dPS[pb2]], w=[d_KQ])
                    S.op("dve", lambda e: e.tensor_copy(out=QKT[:, t, :], in_=PS[pb2][:, 128:256]), r=[dPS[pb2]], w=[d_KQ])
            S.mark("g%d h%d solve" % (gi, h))
            NP = 8
            sol = [dict(A=[alloc(hs, "sA%d_%d" % (i, j), (128, 128)) for j in range(2)],
                        BW=[alloc(hs, "sBW%d_%d" % (i, j), (128, 256)) for j in range(2)],
                        x1=alloc(hs, "sx1_%d" % i, (128, 128)), x2=alloc(hs, "sx2_%d" % i, (128, 128)),
                        d=Dep()) for i in range(NP)]
            vb = [(alloc(hs, "vbk%d" % i, (128, 256)), Dep()) for i in range(2)]
            probs = [(d_, t) for d_ in range(2) for t in (range(ntile) if (d_ == 1 or not isS) else range(mt))]

            def pinfo(d_):
                return (d_ * 8 + h, (ublk, slblk, ublk) if d_ == 0 else (lblk, sublk, lblk), dirbuf[d_])

            for _once in (0,):
                for g0 in range(0, len(probs), NP):
                    grp = probs[g0:g0 + NP]
                    for i, (d_, t) in enumerate(grp):
                        hd, (Mg, Ms, Mi), db = pinfo(d_)
                        sl = sol[i]
                        pb = i % 8
                        gcol = GG[:, t, hd:hd + 1]
                        S.op("pool", lambda e: e.tensor_tensor(out=sl["x1"][:], in0=Mg, in1=gcol.broadcast_to([128, 128]), op=ALU.mult),
                             r=[d_G, d_cst], w=[sl["d"]])
                        S.op("pe", lambda e: e.matmul(PS[pb][:, 0:128], lhsT=ones, rhs=sl["x1"][:], start=True, stop=True),
                             r=[sl["d"], d_cst], w=[dPS[pb]])
                        S.op("dve", lambda e: e.tensor_scalar(out=sl["x1"][:], in0=PS[pb][:, 0:128], scalar1=GC[:, t, hd:hd + 1],
                                                              scalar2=0.0, op0=ALU.subtract, op1=ALU.max),
                             r=[dPS[pb], d_G], w=[sl["d"]])
                        S.op("dve", lambda e: e.tensor_scalar(out=sl["x2"][:], in0=PS[pb][:, 0:128], scalar1=GC[:, t, hd:hd + 1],
                                                              scalar2=0.0, op0=ALU.subtract, op1=ALU.min),
                             r=[dPS[pb], d_G], w=[sl["d"]])
                        S.op("act", lambda e: e.activation(out=sl["x1"][:], in_=sl["x1"][:], func=AF.Exp, scale=-1.0),
                             r=[sl["d"]], w=[sl["d"]])
                        S.op("act", lambda e: e.activation(out=sl["x2"][:], in_=sl["x2"][:], func=AF.Exp), r=[sl["d"]], w=[sl["d"]])
                        S.op("pool", lambda e: e.tensor_tensor(out=sl["x1"][:], in0=sl["x1"][:], in1=Ms, op=ALU.mult),
                             r=[sl["d"], d_cst], w=[sl["d"]])
                        S.op("dve", lambda e: e.scalar_tensor_tensor(out=sl["A"][0][:], in0=KK[:, t, :], scalar=BETA[:, t, hd:hd + 1],
                                                                     in1=sl["x1"][:], op0=ALU.mult, op1=ALU.mult),
                             r=[d_KQ, d_G, sl["d"]], w=[sl["d"]])
                        if t < mt:
                            S.op("pool", lambda e: e.tensor_tensor(out=sl["x2"][:], in0=sl["x2"][:], in1=Mi, op=ALU.mult),
                                 r=[sl["d"], d_cst], w=[sl["d"]])
                            S.op("pool", lambda e: e.tensor_tensor(out=db["kdq"][:, t, 128:256], in0=QKT[:, t, :], in1=sl["x2"][:], op=ALU.mult),
                                 r=[d_KQ, sl["d"]], w=[db["d"]])
                    for i, (d_, t) in enumerate(grp):
                        hd, (Mg, Ms, Mi), db = pinfo(d_)
                        sl = sol[i]
                        pb = i % 8
                        S.op("pe", lambda e: e.transpose(out=PS[pb][:, 0:128], in_=sl["A"][0][:], identity=ident),
                             r=[sl["d"], d_cst], w=[dPS[pb]])
                        S.op("act", lambda e: e.activation(out=sl["BW"][0][:, 0:128], in_=PS[pb][:, 0:128], func=AF.Identity), r=[dPS[pb]], w=[sl["d"]])
                        S.op("dve", lambda e: e.tensor_tensor(out=sl["BW"][0][:, 128:256], in0=ident, in1=PS[pb][:, 0:128], op=ALU.subtract),
                             r=[dPS[pb], d_cst], w=[sl["d"]])
                    cur = 0
                    for i, (d_, t) in enumerate(grp):
                        sl = sol[i]
                        pb = i % 8
                        S.op("pe", lambda e: e.matmul(PS[pb][:, 128:256], lhsT=sl["A"][0][:], rhs=sl["BW"][0][:, 0:128], start=True, stop=True),
                             r=[sl["d"]], w=[dPS[pb]])
                        S.op("dve", lambda e: e.tensor_copy(out=sl["BW"][1][:, 0:128], in_=PS[pb][:, 128:256]), r=[dPS[pb]], w=[sl["d"]])
                        S.op("pool", lambda e: e.tensor_copy(out=sl["BW"][1][:, 128:256], in_=sl["BW"][0][:, 128:256]), r=[sl["d"]], w=[sl["d"]])
                    for i, (d_, t) in enumerate(grp):
                        sl = sol[i]
                        pb = i % 8
                        S.op("pe", lambda e: e.transpose(out=PS[pb][:, 0:128], in_=sl["BW"][1][:, 0:128], identity=ident),
                             r=[sl["d"], d_cst], w=[dPS[pb]])
                        S.op("act", lambda e: e.activation(out=sl["A"][1][:], in_=PS[pb][:, 0:128], func=AF.Identity), r=[dPS[pb]], w=[sl["d"]])
                    cur = 1
                    for lvl in range(1, 6):
                        nxt = 1 - cur
                        last_b = lvl >= 4
                        last_a = lvl >= 5
                        for i, (d_, t) in enumerate(grp):
                            sl = sol[i]
                            pb = i % 8
                            Ac, BWc, An, BWn = sl["A"][cur], sl["BW"][cur], sl["A"][nxt], sl["BW"][nxt]
                            if not last_b:
                                S.op("pe", lambda e: e.matmul(PS[pb][:, 0:256], lhsT=Ac[:], rhs=BWc[:], start=True, stop=True),
                                     r=[sl["d"]], w=[dPS[pb]])
                                S.op("act", lambda e: e.activation(out=BWn[:, 0:128], in_=PS[pb][:, 0:128], func=AF.Identity), r=[dPS[pb]], w=[sl["d"]])
                            else:
                                S.op("pe", lambda e: e.matmul(PS[pb][:, 128:256], lhsT=Ac[:], rhs=BWc[:, 128:256], start=True, stop=True),
                                     r=[sl["d"]], w=[dPS[pb]], signal=last_a)
                                if not last_a:
                                    S.op("pe", lambda e: e.matmul(PS[pb][:, 256:384], lhsT=BWc[:, 0:128], rhs=Ac[:], start=True, stop=True),
                                         r=[sl["d"]], w=[dPS[pb]])
                                    S.op("act", lambda e: e.activation(out=An[:], in_=PS[pb][:, 256:384], func=AF.Identity), r=[dPS[pb]], w=[sl["d"]])
                            S.op("dve", lambda e: e.tensor_tensor(out=BWn[:, 128:256], in0=BWc[:, 128:256], in1=PS[pb][:, 128:256], op=ALU.add),
                                 r=[dPS[pb], sl["d"]], w=[sl["d"]])
                        if not last_b:
                            for i, (d_, t) in enumerate(grp):
                                sl = sol[i]
                                pb = i % 8
                                An, BWn = sl["A"][nxt], sl["BW"][nxt]
                                S.op("pe", lambda e: e.transpose(out=PS[pb][:, 256:384], in_=BWn[:, 0:128], identity=ident),
                                     r=[sl["d"], d_cst], w=[dPS[pb]])
                                S.op("act", lambda e: e.activation(out=An[:], in_=PS[pb][:, 256:384], func=AF.Identity), r=[dPS[pb]], w=[sl["d"]])
                        cur = nxt
                    for i, (d_, t) in enumerate(grp):
                        hd, (Mg, Ms, Mi), db = pinfo(d_)
                        sl = sol[i]
                        W = sl["BW"][cur][:, 128:256]
                        vbk, d_vb = vb[i % 2]
                        pb = i % 8
                        S.op("pool", lambda e: e.tensor_tensor(out=vbk[:, 0:128], in0=vtok[:, t, :], in1=BETA[:, t, hd:hd + 1].broadcast_to([128, 128]), op=ALU.mult),
                             r=[d_kv, d_G], w=[d_vb])
                        S.op("pool", lambda e: e.tensor_tensor(out=vbk[:, 128:256], in0=ktok[:, t, :], in1=BEGC[:, t, hd:hd + 1].broadcast_to([128, 128]), op=ALU.mult),
                             r=[d_kv, d_G], w=[d_vb])
                        S.op("pool", lambda e: e.tensor_tensor(out=db["kdq"][:, t, 0:128], in0=ktok[:, t, :], in1=EKD[:, t, hd:hd + 1].broadcast_to([128, 128]), op=ALU.mult),
                             r=[d_kv, d_G], w=[db["d"]])
                        S.op("pe", lambda e: e.matmul(PS[pb][:, 0:256], lhsT=W, rhs=vbk[:], start=True, stop=True),
                             r=[sl["d"], d_vb], w=[dPS[pb]])
                        S.op("act", lambda e: e.activation(out=db["uw"][:, t, :], in_=PS[pb][:, 0:256], func=AF.Identity), r=[dPS[pb]], w=[db["d"]])
            S.barrier()
            arena.ptr = inner_mark
            S.mark("g%d h%d recur" % (gi, h))
            oacc = alloc(hs, "oacc", (128, mt, 128)); d_oacc = Dep()
            S.op("pool", lambda e: e.memset(oacc[:], 0.0), w=[d_oacc])
            junk = alloc(hs, "dnjunk", (128, 128)); d_junk = Dep()
            chains = []
            for si, (a, b) in enumerate(seqs):
                t0, t1 = a // 128, b // 128
                for d_ in range(2):
                    if d_ == 0:
                        chunks = [(t, hf) for t in range(t0, t1) for hf in (0, 1)]
                        if isS:
                            chunks = chunks[:9]
                    else:
                        chunks = [(t, hf) for t in range(t1 - 1, t0 - 1, -1) for hf in (1, 0)]
                    outs = [(t < mt and not (isS and t == 4 and hf == 1)) for (t, hf) in chunks]
                    nch = len(chunks)
                    ch = dict(St=[alloc(hs, "St%d_%d_%d" % (si, d_, k), (128, 128)) for k in range(2)],
                              MT=alloc(hs, "MT%d_%d" % (si, d_), (128, nch, 128)), CC=alloc(hs, "CC%d_%d" % (si, d_), (128, nch, 128), BF16),
                              NWQ=alloc(hs, "NWQ%d_%d" % (si, d_), (128, max(1, sum(outs)), 128), BF16),
                              Sb=[alloc(hs, "Sb%d_%d_%d" % (si, d_, k), (128, 128), BF16) for k in range(2)], dSb=[Dep(), Dep()],
                              tmp=alloc(hs, "rt%d_%d" % (si, d_), (128, 128)),
                              dS=[Dep(), Dep()], dpre=[Dep() for _ in range(nch)], dtmp=Dep(),
                              chunks=chunks, outs=outs, dir=d_, si=si, idx=len(chains))
                    if isS:
                        S.dma("sp", ch["St"][0][:], (SF0 if d_ == 0 else SB0)[h], w=[ch["dS"][0]])
                    else:
                        S.op("pool", lambda e: e.memset(ch["St"][0][:], 0.0), w=[ch["dS"][0]])
                    chains.append(ch)
            thunks = (interleave() if interleave is not None else None) or []
            nslots = 2 * sum(len(c["chunks"]) for c in chains)
            per = (len(thunks) + nslots - 1) // max(1, nslots)
            tpos = [0]

            def replay(n):
                for th in thunks[tpos[0]:tpos[0] + n]:
                    th()
                tpos[0] = min(len(thunks), tpos[0] + n)

            pctr = 0
            for ch in chains:
                d_ = ch["dir"]
                hd = d_ * 8 + h
                db = dirbuf[d_]
                oi = 0
                for ci, (t, hf) in enumerate(ch["chunks"]):
                    rows = slice(hf * 64, hf * 64 + 64)
                    replay(per)
                    pb = 6 + pctr % 2
                    pctr += 1
                    need_o = ch["outs"][ci]
                    nn = 256 if need_o else 128
                    S.op("pe", lambda e: e.matmul(PS[pb][:, 0:nn], lhsT=db["uw"][rows, t, 128:256], rhs=db["kdq"][rows, t, 0:nn], start=True, stop=True),
                         r=[db["d"]], w=[dPS[pb]], signal=False)
                    S.op("pe", lambda e: e.matmul(PS[pb][:, 256:384], lhsT=db["kdq"][rows, t, 0:128], rhs=db["uw"][rows, t, 0:128], start=True, stop=True),
                         r=[db["d"]], w=[dPS[pb]])
                    S.op("act", lambda e: e.activation(out=ch["MT"][:, ci, :], in_=PS[pb][:, 0:128], func=AF.Identity, scale=-1.0),
                         r=[dPS[pb]], w=[ch["dpre"][ci]])
                    S.op("dve", lambda e: e.tensor_copy(out=ch["CC"][:, ci, :], in_=PS[pb][:, 256:384]), r=[dPS[pb]], w=[ch["dpre"][ci]])
                    if need_o:
                        S.op("act", lambda e: e.activation(out=ch["NWQ"][:, oi, :], in_=PS[pb][:, 128:256], func=AF.Identity, scale=-1.0),
                             r=[dPS[pb]], w=[ch["dpre"][ci]])
                        oi += 1
            maxlen = max(len(c["chunks"]) for c in chains)
            for ch in chains:
                ch["oi"] = 0
            for step in range(maxlen):
                for ch in chains:
                    if step >= len(ch["chunks"]):
                        continue
                    replay(per)
                    t, hf = ch["chunks"][step]
                    d_ = ch["dir"]
                    hd = d_ * 8 + h
                    db = dirbuf[d_]
                    rows = slice(hf * 64, hf * 64 + 64)
                    Sc, Sn = ch["St"][step % 2], ch["St"][(step + 1) % 2]
                    dSc, dSn = ch["dS"][step % 2], ch["dS"][(step + 1) % 2]
                    pbs = 6
                    ts = slice(t * 128, (t + 1) * 128)
                    S.op("pe", lambda e: e.matmul(PS[pbs][:, 0:128], lhsT=identb[:], rhs=ch["CC"][:, step, :], start=True, stop=False),
                         r=[ch["dpre"][step], d_cst], w=[dPS[pbs]], signal=False)
                    S.op("pe", lambda e: e.matmul(PS[pbs][:, 0:128], lhsT=ch["MT"][:, step, :], rhs=Sc[:], start=False, stop=True),
                         r=[ch["dpre"][step], dSc], w=[dPS[pbs]])
                    S.op("dve", lambda e: e.scalar_tensor_tensor(out=Sn[:], in0=Sc[:], scalar=DL[hf][:, t, hd:hd + 1], in1=PS[pbs][:, 0:128],
                                                                 op0=ALU.mult, op1=ALU.add),
                         r=[dPS[pbs], dSc, d_G], w=[dSn])
                    if ch["outs"][step]:
                        pbo = 7
                        oi = ch["oi"]
                        ch["oi"] += 1
                        Sb, dSb = ch["Sb"][step % 2], ch["dSb"][step % 2]
                        S.op("act", lambda e: e.activation(out=Sb[:], in_=Sc[:], func=AF.Identity), r=[dSc], w=[dSb])
                        S.op("pe", lambda e: e.matmul(PS[pbo][:, 0:128], lhsT=yqb[:, ts], rhs=Sb[:], start=True, stop=True),
                             r=[d_yb, dSb], w=[dPS[pbo]], signal=False)
                        S.op("pe", lambda e: e.matmul(PS[pbo][:, 128:256], lhsT=ch["NWQ"][:, oi, :], rhs=Sb[:], start=True, stop=False),
                             r=[ch["dpre"][step], dSb], w=[dPS[pbo]], signal=False)
                        S.op("pe", lambda e: e.matmul(PS[pbo][:, 128:256], lhsT=db["kdq"][rows, t, 128:256], rhs=db["uw"][rows, t, 0:128], start=False, stop=True),
                             r=[db["d"]], w=[dPS[pbo]])
                        S.op("dve", lambda e: e.tensor_tensor(out=ch["tmp"][rows, :], in0=PS[pbo][rows, 128:256], in1=oacc[rows, t, :], op=ALU.add),
                             r=[dPS[pbo], d_oacc], w=[ch["dtmp"]])
                        S.op("dve", lambda e: e.scalar_tensor_tensor(out=oacc[rows, t, :], in0=PS[pbo][rows, 0:128], scalar=EGC[rows, t, hd:hd + 1],
                                                                     in1=ch["tmp"][rows, :], op0=ALU.mult, op1=ALU.add),
                             r=[dPS[pbo], d_G, ch["dtmp"]], w=[d_oacc])
            replay(len(thunks))
            if not isS:
                for ch in chains:
                    n = len(ch["chunks"])
                    ev = S.dma("sp", (NSF if ch["dir"] == 0 else NSB)[ch["si"], h], ch["St"][n % 2][:], r=[ch["dS"][n % 2]])
                    out_deps.append(ev)
            S.mark("g%d h%d dnout" % (gi, h))
            st = alloc(hs, "dst", (128, 2 * mt)); d_st = Dep()
            S.op("dve", lambda e: e.memset(st[:], 0.0), w=[d_st])
            for tq in range(mt):
                n = 64 if (isS and tq == 4) else 128
                S.op("act", lambda e: e.activation(out=junk[0:n, :], in_=oacc[0:n, tq, :], func=AF.Square,
                                                   accum_out=st[0:n, tq:tq + 1]), r=[d_oacc], w=[d_junk, d_st])
            rstd_from_ss(st, d_st, mt, 1.0 / 128)
            for tq in range(mt):
                n = 64 if (isS and tq == 4) else 128
                pb = 6 + tq % 2
                S.op("dve", lambda e: e.scalar_tensor_tensor(out=oacc[0:n, tq, :], in0=oacc[0:n, tq, :], scalar=st[0:n, mt + tq:mt + tq + 1],
                                                             in1=dngb[0:n, :], op0=ALU.mult, op1=ALU.mult),
                     r=[d_st, d_small], w=[d_oacc])
                S.op("pe", lambda e: e.transpose(out=PS[pb][:, 0:n], in_=oacc[0:n, tq, :], identity=cst[0:n, 0, 0:n]),
                     r=[d_oacc, d_cst], w=[dPS[pb]])
                S.op("dve", lambda e: e.tensor_tensor(out=oT[:, 8 + h, ocol0 + tq * 128:ocol0 + tq * 128 + n], in0=PS[pb][:, 0:n],
                                                      in1=gsT[:, tq * 128:tq * 128 + n], op=ALU.mult),
                     r=[dPS[pb], d_gs], w=[d_oT[8 + h]])

        try:
            with arena.scope() as ms:
                hTbox["hT"] = alloc(ms, "hT", (128, 16, 1024), BF16)
                mixer_group(0)
                if stop_after != "g0":
                    mixer_group(1)
                S.barrier()
        except _Stop:
            stop_after = "mixer"

        if stop_after in ("mixer", "attn", "g0"):
            for ev in out_deps:
                S._wait("sp", ev)
            print("ops", S.nops, "waits", S.nwaits, "sim", S.simulate()[:2])
            return nc

        S.mark("phase3")
        xmid = alloc(top, "xmid", (128, 8, 2048)); d_xm = [Dep() for _ in range(9)]
        gbc = alloc(top, "gbc", (128, 2, 2048)); d_gbc = Dep()

        def mod_rowbc(ph_bufs, vec_i, bank0=0):
            for q in range(4):
                mod_rowbc_blk(ph_bufs, vec_i, q, bank0)

        def mod_rowbc_blk(ph_bufs, vec_i, q, bank0=0):
            wstream, mbb, d_mbb, mrow, d_mrow = ph_bufs
            if True:
                blk = vec_i * 4 + q
                wb, d_w = wstream.get()
                for cvi, L in enumerate((Ls["L0"], Ls["L1"])):
                    mod_block((wb, d_w, mbb, d_mbb, mrow, d_mrow), blk, L, bank0 + cvi)
                    S.op("act", lambda e: e.activation(out=gbc[:, cvi, q * 512:(q + 1) * 512], in_=mrow[:], func=AF.Identity), r=[d_mrow], w=[d_gbc])

        with arena.scope() as ph:
            make_L(ph)
            wbufs = [(alloc(ph, "mw%d" % i, (128, 16, 512), BF16), Dep()) for i in range(2)]
            mbb = alloc(ph, "mbb", (128, 512)); d_mbb = Dep()
            mrow = alloc(ph, "mrow", (128, 512)); d_mrow = Dep()
            bufs = (WStream(S, wbufs, [MODW[b_] for b_ in range(8, 20)], 1), mbb, d_mbb, mrow, d_mrow)
            bufs[0].prefetch()
            mod_rowbc(bufs, 2)
            mod_featmajor(bufs, 3, b2v, 2, 3)
            mod_featmajor(bufs, 4, s2v, 2, 3)
            S.op("dve", lambda e: e.tensor_scalar(out=small[:, 96:128], in0=small[:, 96:128], scalar1=1.0, scalar2=None,
                                                  op0=ALU.add), r=[d_small], w=[d_small])
            S.op("dve", lambda e: e.tensor_tensor(out=s2v, in0=s2v, in1=small[:, 16:32].unsqueeze(2).broadcast_to([128, 16, 2]),
                                                  op=ALU.mult), r=[d_small], w=[d_small])
            S.barrier()

        S.mark("phase4")
        NTM = 9
        with arena.scope() as ph4:
          xh = alloc(ph4, "xh", (128, 2048))
          xm = lambda t: (xmid[:, t, :] if t < 8 else xh)
          with arena.scope() as ph:
            wob = [(alloc(ph, "wo%d" % i, (128, 16, 512), BF16), Dep()) for i in range(2)]
            xin = [(alloc(ph, "xin%d" % i, (128, 512)), Dep()) for i in range(3)]
            tmpm = [(alloc(ph, "tmpm%d" % i, (128, 512)), Dep()) for i in range(2)]
            ctr = 0
            wo_stream = WStream(S, wob, [WOUT[b_] for b_ in range(4)], 1)
            wo_stream.prefetch()
            for nb in range(4):
                wb, d_w = wo_stream.get()
                for t in range(NTM):
                    n = 64 if t == 8 else 128
                    cvi = 0 if t < 4 else 1
                    pb = ctr % 4
                    xi, d_xi = xin[ctr % 3]
                    tm, d_tm = tmpm[ctr % 2]
                    ctr += 1
                    src = XP[t * 128:t * 128 + n, nb * 512:(nb + 1) * 512] if t < 4 else XS[(t - 4) * 128:(t - 4) * 128 + n, nb * 512:(nb + 1) * 512]
                    S.dma("sp", xi[0:n, :], src, w=[d_xi])
                    for kc in range(16):
                        S.op("pe", lambda e: e.matmul(PS[pb][0:n, :], lhsT=oT[:, kc, t * 128:t * 128 + n], rhs=wb[:, kc, :],
                                                      start=(kc == 0), stop=(kc == 15)),
                             r=[d_oT[kc], d_w], w=[dPS[pb]], signal=(kc == 15))
                    S.op("dve", lambda e: e.tensor_tensor(out=tm[0:n, :], in0=PS[pb][0:n, :], in1=gbc[0:n, cvi, nb * 512:(nb + 1) * 512], op=ALU.mult),
                         r=[dPS[pb], d_gbc], w=[d_tm])
                    S.op("pool", lambda e: e.tensor_tensor(out=xm(t)[0:n, nb * 512:(nb + 1) * 512], in0=tm[0:n, :], in1=xi[0:n, :], op=ALU.add),
                         r=[d_tm, d_xi], w=[d_xm[t]])
            S.barrier()
          h2T = oT; d_h2 = Dep()
          with arena.scope() as ph:
            xns = [(alloc(ph, "n2xn%d" % i, (128, 2048)), Dep()) for i in range(2)]
            st = alloc(ph, "n2st", (128, 2 * NTM)); d_st = Dep()
            S.op("dve", lambda e: e.memset(st[:], 0.0), w=[d_st])
            make_L(ph)
            wbufs5 = [(alloc(ph, "mw5_%d" % i, (128, 16, 512), BF16), Dep()) for i in range(2)]
            mbb5 = alloc(ph, "mbb5", (128, 512)); d_mbb5 = Dep()
            mrow5 = alloc(ph, "mrow5", (128, 512)); d_mrow5 = Dep()
            ws5 = WStream(S, wbufs5, [MODW[b_] for b_ in range(20, 24)], 1)
            ws5.prefetch()
            for t in range(NTM):
                if t in (1, 3, 5, 7):
                    mod_rowbc_blk((ws5, mbb5, d_mbb5, mrow5, d_mrow5), 5, (t - 1) // 2, bank0=4)
                n = 64 if t == 8 else 128
                cvi = 0 if t < 4 else 1
                xn, d_xn = xns[t % 2]
                S.op("act", lambda e: e.activation(out=xn[0:n, :], in_=xm(t)[0:n, :], func=AF.Square, accum_out=st[0:n, 2 * t:2 * t + 1]),
                     r=[d_xm[t]], w=[d_xn, d_st])
                S.op("act", lambda e: e.activation(out=st[0:n, 2 * t + 1:2 * t + 2], in_=st[0:n, 2 * t:2 * t + 1], func=AF.Ln, bias=EPS, scale=1.0 / D),
                     r=[d_st], w=[d_st])
                S.op("act", lambda e: e.activation(out=st[0:n, 2 * t + 1:2 * t + 2], in_=st[0:n, 2 * t + 1:2 * t + 2], func=AF.Exp, scale=-0.5),
                     r=[d_st], w=[d_st])
                S.op("dve", lambda e: e.tensor_scalar(out=xn[0:n, :], in0=xm(t)[0:n, :], scalar1=st[0:n, 2 * t + 1:2 * t + 2], scalar2=None, op0=ALU.mult),
                     r=[d_xm[t], d_st], w=[d_xn])
                for g in range(4):
                    for j in range(4):
                        kc = g * 4 + j
                        S.op("pe", lambda e: e.transpose(out=PS[g][:, j * 128:j * 128 + n], in_=xn[0:n, kc * 128:(kc + 1) * 128], identity=cst[0:n, 0, 0:n]),
                             r=[d_xn, d_cst], w=[dPS[g]], signal=(j == 3))
                    for j in range(4):
                        kc = g * 4 + j
                        if j % 2 == 0:
                            S.op("act", lambda e: e.activation(out=h2T[:, kc, t * 128:t * 128 + n], in_=PS[g][:, j * 128:j * 128 + n], func=AF.Identity,
                                                               bias=b2v[:, kc, cvi:cvi + 1], scale=s2v[:, kc, cvi:cvi + 1]),
                                 r=[dPS[g], d_small], w=[d_h2])
                        else:
                            S.op("dve", lambda e: e.tensor_scalar(out=h2T[:, kc, t * 128:t * 128 + n], in0=PS[g][:, j * 128:j * 128 + n],
                                                                  scalar1=s2v[:, kc, cvi:cvi + 1], scalar2=b2v[:, kc, cvi:cvi + 1], op0=ALU.mult, op1=ALU.add),
                                 r=[dPS[g], d_small], w=[d_h2])
            S.barrier()

        S.mark("phase5")
        with arena.scope() as ph:
            convf = alloc(ph, "convf", (128, 88, 3)); d_cf = Dep()
            S.dma("sp", convf[:], CONVF, w=[d_cf])
            wub = [(alloc(ph, "wu%d" % i, (128, 16, 512), BF16), Dep()) for i in range(2)]
            wdb = [(alloc(ph, "wd%d" % i, (128, 4, 512), BF16), Dep()) for i in range(2)]
            aTg = alloc(ph, "aT", (128, 4, 1024), BF16); d_aT = [Dep() for _ in range(4)]
            ur = [(alloc(ph, "ur%d" % i, (128, 1032)), Dep()) for i in range(2)]
            yc = [[(alloc(ph, "yc%d_%d" % (i, k), (128, 1024)), Dep()) for k in range(2)] for i in range(2)]
            tmpc = alloc(ph, "tmpc", (128, 1024)); d_tc = Dep()
            tmpd = [(alloc(ph, "tmpd%d" % i, (128, 512)), Dep()) for i in range(2)]
            segs = [(0, 256), (256, 512), (512, 1024)]
            FP = [(0, 512), (512, 1024), (1024, 1025)]
            dctr = [0]
            wu_cur = [None]
            wu_stream = WStream(S, wub, [WUP[r] for r in range(22)], 1)
            wd_stream = WStream(S, wdb, [WDOWN[g_, n_] for g_ in range(11) for n_ in range(4)], 1)
            wu_stream.prefetch()

            def ffn_up(j):
                r_, cc = j // 2, j % 2
                if cc == 0:
                    wu_cur[0] = wu_stream.get()
                if j % 4 == 2:
                    wd_stream.prefetch(0)
                if j % 4 == 3:
                    wd_stream.prefetch(1)
                wu, d_wu = wu_cur[0]
                for gv in range(2):
                    co = gv * 256 + cc * 128
                    cidx = gv * 44 + j
                    u_, d_u = ur[gv]
                    y_, d_yc = yc[gv][j % 2]
                    for pi, (c0, c1) in enumerate(FP):
                        pb = pi
                        n = c1 - c0
                        for kc in range(16):
                            S.op("pe", lambda e: e.matmul(PS[pb][:, 0:n], lhsT=wu[:, kc, co:co + 128], rhs=h2T[:, kc, c0:c1],
                                                          start=(kc == 0), stop=(kc == 15)),
                                 r=[d_wu, d_h2], w=[dPS[pb]], signal=(kc == 15))
                        S.op("act", lambda e: e.activation(out=u_[:, c0:c1], in_=PS[pb][:, 0:n], func=AF.Identity), r=[dPS[pb]], w=[d_u])
                    cw = convf[:, cidx, :]
                    S.op("dve", lambda e: e.tensor_scalar(out=y_[:], in0=u_[:, 0:1024], scalar1=cw[:, 1:2], scalar2=None, op0=ALU.mult),
                         r=[d_u, d_cf], w=[d_yc])
                    S.op("act", lambda e: e.activation(out=tmpc[:], in_=u_[:, 1:1025], func=AF.Identity, scale=cw[:, 2:3]),
                         r=[d_u, d_cf], w=[d_tc])
                    for (a_, b_) in segs:
                        S.op("dve", lambda e: e.scalar_tensor_tensor(out=y_[:, a_ + 1:b_], in0=u_[:, a_:b_ - 1], scalar=cw[:, 0:1],
                                                                     in1=y_[:, a_ + 1:b_], op0=ALU.mult, op1=ALU.add),
                             r=[d_u, d_cf], w=[d_yc])
                    for (a_, b_) in segs:
                        b2 = b_ - 1 if b_ < 1024 else 1024
                        S.op("pool", lambda e: e.tensor_tensor(out=y_[:, a_:b2], in0=y_[:, a_:b2], in1=tmpc[:, a_:b2], op=ALU.add),
                             r=[d_tc], w=[d_yc])

            def ffn_fin(j):
                yg, d_g = yc[0][j % 2]
                yv_, d_v = yc[1][j % 2]
                S.op("act", lambda e: e.activation(out=yg[:], in_=yg[:], func=AF.Silu), r=[d_g], w=[d_g])
                S.op("dve", lambda e: e.tensor_tensor(out=aTg[:, j % 4, :], in0=yg[:], in1=yv_[:], op=ALU.mult),
                     r=[d_g, d_v], w=[d_aT[j % 4]])

            def ffn_down(grp):
                for nb in range(4):
                    wd, d_wd = wd_stream.get()
                    for t in range(8):
                        cvi = 0 if t < 4 else 1
                        pb = 3 + dctr[0] % 5
                        tm, d_tm = tmpd[dctr[0] % 2]
                        dctr[0] += 1
                        for jj in range(4):
                            S.op("pe", lambda e: e.matmul(PS[pb][:], lhsT=aTg[:, jj, t * 128:(t + 1) * 128], rhs=wd[:, jj, :],
                                                          start=(jj == 0), stop=(jj == 3)),
                                 r=[d_aT[jj], d_wd], w=[dPS[pb]], signal=(jj == 3))
                        S.op("dve", lambda e: e.tensor_tensor(out=tm[:], in0=PS[pb][:], in1=gbc[:, cvi, nb * 512:(nb + 1) * 512], op=ALU.mult),
                             r=[dPS[pb], d_gbc], w=[d_tm])
                        S.op("pool", lambda e: e.tensor_tensor(out=xmid[:, t, nb * 512:(nb + 1) * 512], in0=xmid[:, t, nb * 512:(nb + 1) * 512], in1=tm[:], op=ALU.add),
                             r=[d_tm], w=[d_xm[t]])

            for j in range(44):
                ffn_up(j)
                if j >= 1:
                    ffn_fin(j - 1)
                    if (j - 1) % 4 == 3:
                        ffn_down((j - 1) // 4)
            ffn_fin(43)
            ffn_down(10)
            S.barrier()

        S.mark("phase6")
        with arena.scope() as ph:
            fg = alloc(ph, "fg", (128, 2048)); d_fg = Dep()
            S.dma("sp", fg[:], FINALG.partition_broadcast(128), w=[d_fg])
            yo = [(alloc(ph, "yo%d" % i, (128, 2048)), Dep()) for i in range(2)]
            st = alloc(ph, "fst", (128, 16)); d_st = Dep()
            S.op("dve", lambda e: e.memset(st[:], 0.0), w=[d_st])
            for t in range(8):
                y_, d_yo = yo[t % 2]
                S.op("act", lambda e: e.activation(out=y_[:], in_=xmid[:, t, :], func=AF.Square, accum_out=st[:, 2 * t:2 * t + 1]),
                     r=[d_xm[t]], w=[d_yo, d_st])
                S.op("act", lambda e: e.activation(out=st[:, 2 * t + 1:2 * t + 2], in_=st[:, 2 * t:2 * t + 1], func=AF.Ln, bias=EPS, scale=1.0 / D),
                     r=[d_st], w=[d_st])
                S.op("act", lambda e: e.activation(out=st[:, 2 * t + 1:2 * t + 2], in_=st[:, 2 * t + 1:2 * t + 2], func=AF.Exp, scale=-0.5),
                     r=[d_st], w=[d_st])
                S.op("dve", lambda e: e.scalar_tensor_tensor(out=y_[:], in0=xmid[:, t, :], scalar=st[:, 2 * t + 1:2 * t + 2], in1=fg[:],
                                                             op0=ALU.mult, op1=ALU.mult), r=[d_xm[t], d_st, d_fg], w=[d_yo])
                dst = YP[t * 128:(t + 1) * 128, :] if t < 4 else YS[(t - 4) * 128:(t - 3) * 128, :]
                ev = S.dma("sp", dst, y_[:], r=[d_yo])
                out_deps.append(ev)
        for ev in out_deps:
            S._wait("sp", ev)
        S.mark("end")
        print("ops", S.nops, "waits", S.nwaits, "sim", S.simulate()[:2])
        if os.environ.get("KMARKS"):
            import json
            json.dump(S.marks, open(os.environ["KMARKS"], "w"))
    return nc


_NC_CACHE = {}


def kernel(**inputs):
    maps = _prep(inputs)
    stop = os.environ.get("KSTOP")
    if stop not in _NC_CACHE:
        _NC_CACHE[stop] = build(stop)
    nc = _NC_CACHE[stop]
    ncr = int(os.environ.get("KCORES", str(NCORES)))
    res = run_bass_kernel_spmd(nc, maps[:ncr], core_ids=list(range(ncr)))
    R = list(res.results) + [res.results[0]] * (NCORES - ncr)
    y_prompt = np.zeros((16, 256, 2048), np.float32)
    y_sample = np.zeros((4, 1024, 2048), np.float32)
    nk = np.zeros((16, 1, 256, 8, 128), np.float32)
    nv = np.zeros((16, 1, 256, 8, 128), np.float32)
    nsf = np.zeros((16, 1, 8, 128, 128), np.float32)
    nsb = np.zeros((16, 1, 8, 128, 128), np.float32)
    for c in range(NCORES):
        b, par = c // 2, c % 2
        r = R[c]
        yp = r["YP"].reshape(2, 256, 2048)
        ys = r["YS"]
        k = r["NK"].reshape(2, 256, 8, 128)
        v = r["NV"].reshape(2, 256, 8, 128)
        f, bw = r["NSF"], r["NSB"]
        if par:
            yp, k, v = yp[:, ::-1], k[:, ::-1], v[:, ::-1]
            ys = ys[::-1]
            f, bw = bw, f
            y_sample[b, 512:1024] = ys
        else:
            y_sample[b, 0:512] = ys
        y_prompt[2 * c:2 * c + 2] = yp
        nk[2 * c:2 * c + 2, 0] = k
        nv[2 * c:2 * c + 2, 0] = v
        nsf[2 * c:2 * c + 2, 0] = f
        nsb[2 * c:2 * c + 2, 0] = bw
    return (y_prompt, y_sample, nk, nv, nsf, nsb)
```

```python
import os
import math
import numpy as np
from contextlib import ExitStack
import concourse.bass as bass
import concourse.mybir as mybir
from concourse.bass_utils import run_bass_kernel_spmd

F32 = mybir.dt.float32
BF16 = mybir.dt.bfloat16
ALU = mybir.AluOpType
AF = mybir.ActivationFunctionType

D = 2048
NCORES = 8
EPS = 1e-6
DFF = 5632
LAM_INIT = 0.8 - 0.6 * math.exp(0.0)


class _RecEngine:
    def __getattr__(self, name):
        return lambda *a, **k: (name, a, k)


class Dep:
    __slots__ = ("w", "r", "excl")

    def __init__(self, excl=False):
        self.w = None
        self.r = {}
        self.excl = excl


class Sched:
    NDS = 24

    def __init__(self, nc, stack):
        self.nc = nc
        self.engs = {"pe": nc.tensor, "act": nc.scalar, "dve": nc.vector, "pool": nc.gpsimd, "sp": nc.sync}
        self.sem = {k: stack.enter_context(nc.semaphore("s_" + k)) for k in self.engs}
        self.cnt = {k: 0 for k in self.engs}
        self.seen = {k: {} for k in self.engs}
        self.dsem = [stack.enter_context(nc.semaphore("d%d" % i)) for i in range(self.NDS)]
        self.dval = [0] * self.NDS
        self.dnext = 0
        self.dnext_sw = 0
        self.nops = {k: 0 for k in self.engs}
        self.nwaits = 0
        self.trace = {k: [] for k in self.engs}
        self.marks = []
        self.rec = None

    def _wait(self, eng, ev):
        if ev is None:
            return
        key, sem, val = ev
        if self.seen[eng].get(key, 0) >= val:
            return
        if key == eng and val > self.cnt[eng]:
            return
        self.engs[eng].wait_ge(sem, val)
        self.trace[eng].append(("wait", key, val))
        self.nwaits += 1
        self.seen[eng][key] = val

    def _deps(self, eng, r, w):
        for d in r:
            self._wait(eng, d.w)
        for d in w:
            self._wait(eng, d.w)
            for ev in list(d.r.values()):
                self._wait(eng, ev)

    def _record(self, ev, r, w):
        for d in r:
            d.r[ev[0]] = ev
        for d in w:
            d.w = ev
            d.r = {}

    def op(self, eng, fn, r=(), w=(), signal=True):
        if self.rec is not None:
            name, a, k = fn(_RecEngine())
            self.rec.append(lambda: self.op(eng, lambda e: getattr(e, name)(*a, **k), r, w, signal))
            return None
        if any(d.excl for d in r):
            w = list(w) + [d for d in r if d.excl]
            r = [d for d in r if not d.excl]
        self._deps(eng, r, w)
        ins = fn(self.engs[eng])
        self.nops[eng] += 1
        if signal:
            self.cnt[eng] += 1
            ins.then_inc(self.sem[eng], 1)
            self.trace[eng].append(("inc", eng, 1))
            ev = (eng, self.sem[eng], self.cnt[eng])
        else:
            ev = (eng, self.sem[eng], self.cnt[eng] + 1)
        self._record(ev, r, w)
        return ins

    def dma(self, q, out, in_, r=(), w=(), evlist=None):
        if self.rec is not None:
            self.rec.append(lambda: self.dma(q, out, in_, r, w, evlist))
            return None
        half = self.NDS // 2
        if q == "pool":
            i = half + self.dnext_sw
            self.dnext_sw = (self.dnext_sw + 1) % half
        else:
            i = self.dnext
            self.dnext = (i + 1) % half
        key = "d%d" % i
        if self.dval[i] > 0:
            self._wait(q, (key, self.dsem[i], self.dval[i]))
        self._deps(q, r, w)
        ins = self.engs[q].dma_start(out=out, in_=in_)
        self.nops[q] += 1
        self.dval[i] += 16
        ins.then_inc(self.dsem[i], 16)
        self.trace[q].append(("inc", key, 16))
        ev = (key, self.dsem[i], self.dval[i])
        self._record(ev, r, w)
        if evlist is not None:
            evlist.append(ev)
        return ev

    def mark(self, label):
        if self.rec is not None:
            self.rec.append(lambda: self.mark(label))
            return
        self.marks.append((label, dict(self.nops)))

    def simulate(self):
        val = {}
        pc = {k: 0 for k in self.engs}
        progress = True
        while progress:
            progress = False
            for k in self.engs:
                tr = self.trace[k]
                while pc[k] < len(tr):
                    kind, key, v = tr[pc[k]]
                    if kind == "wait":
                        if val.get(key, 0) >= v:
                            pc[k] += 1
                            progress = True
                        else:
                            break
                    else:
                        val[key] = val.get(key, 0) + v
                        pc[k] += 1
                        progress = True
        stuck = {k: (pc[k], len(self.trace[k]), self.trace[k][pc[k]] if pc[k] < len(self.trace[k]) else None) for k in self.engs}
        ok = all(pc[k] == len(self.trace[k]) for k in self.engs)
        return ok, stuck, val

    def barrier(self):
        if self.rec is not None:
            self.rec.append(self.barrier)
            return
        self._barrier()

    def _barrier(self):
        evs = [(k, self.sem[k], self.cnt[k]) for k in self.engs if self.cnt[k] > 0]
        evs += [("d%d" % i, self.dsem[i], self.dval[i]) for i in range(self.NDS) if self.dval[i] > 0]
        for eng in self.engs:
            for ev in evs:
                self._wait(eng, ev)


class _Stop(Exception):
    pass


class WStream:
    def __init__(self, S, bufs, srcs, depth):
        self.S, self.bufs, self.srcs, self.depth = S, bufs, srcs, min(depth, len(bufs) - 1)
        self.i_issue = 0
        self.i_use = 0

    def prefetch(self, ahead=None):
        ahead = self.depth if ahead is None else min(ahead, len(self.bufs) - 1)
        while self.i_issue < min(len(self.srcs), self.i_use + ahead + 1):
            buf, d = self.bufs[self.i_issue % len(self.bufs)]
            self.S.dma("pool", buf[:], self.srcs[self.i_issue], w=[d])
            self.i_issue += 1

    def get(self):
        self.prefetch()
        buf = self.bufs[self.i_use % len(self.bufs)]
        self.i_use += 1
        return buf


class Arena:
    WORDS = 53200

    def __init__(self, nc, stack):
        self.t = stack.enter_context(nc.sbuf_tensor("arena", [128, self.WORDS], F32))
        self.ptr = 0
        self.peak = 0
        self.norelease = False

    def alloc(self, shape, dt=F32):
        n = 1
        for x in shape[1:]:
            n *= int(x)
        words = n if dt == F32 else (n + 1) // 2
        words = (words + 7) // 8 * 8
        off = self.ptr
        self.ptr += words
        self.peak = max(self.peak, self.ptr)
        assert self.ptr <= self.WORDS, ("SBUF arena overflow", self.ptr)
        ap = self.t[:, off:off + words]
        if dt != F32:
            ap = ap.bitcast(dt)
        ap = ap[:, 0:n]
        if len(shape) == 3:
            ap = ap.rearrange("p (a b) -> p a b", b=int(shape[2]))
        elif len(shape) == 4:
            ap = ap.rearrange("p (a b c) -> p a b c", b=int(shape[2]), c=int(shape[3]))
        return ap

    def scope(self):
        arena = self

        class _Scope:
            def __enter__(self_):
                self_.mark = arena.ptr
                return self_

            def __exit__(self_, *a):
                if not arena.norelease:
                    arena.ptr = self_.mark
                return False
        return _Scope()


def _tile_w(w, ncol_blk):
    K, N = w.shape
    return np.ascontiguousarray(w.reshape(K // 128, 128, N // ncol_blk, ncol_blk).transpose(2, 1, 0, 3))


def _consts():
    i = np.arange(128)
    blk = (i[:, None] // 64) == (i[None, :] // 64)
    c = {}
    c["ident"] = np.eye(128)
    c["ones"] = np.ones((128, 128))
    c["ublk"] = blk & (i[:, None] <= i[None, :])
    c["lblk"] = blk & (i[:, None] >= i[None, :])
    c["slblk"] = blk & (i[:, None] > i[None, :])
    c["sublk"] = blk & (i[:, None] < i[None, :])
    c["eblk"] = blk
    c["e0"] = np.broadcast_to((i[:, None] < 64), (128, 128))
    c["e1"] = np.broadcast_to((i[:, None] >= 64), (128, 128))
    d = i % 64
    ii = d % 32
    partner = np.where(ii < 16, i + 16, i - 16)
    prope = np.zeros((128, 128))
    prope[partner, i] = 1.0
    c["prope"] = prope
    names = ["ident", "ones", "ublk", "lblk", "slblk", "sublk", "eblk", "e0", "e1", "prope"]
    return np.ascontiguousarray(np.stack([c[n].astype(np.float32) for n in names], axis=1)), names


def _rope_tables(flip):
    t = np.arange(1024)
    if flip:
        t = t[::-1]
    rows = (t // 64).astype(np.float64)
    cols = (t % 64).astype(np.float64)
    inv = 10000.0 ** (-np.arange(0, 32, 2, dtype=np.float64) / 32.0)
    p = np.arange(128)
    d = p % 64
    half = d // 32
    ii = d % 32
    f = ii % 16
    pos = np.where(half[:, None] == 0, rows[None, :], cols[None, :])
    ang = pos * inv[f][:, None]
    cos = np.cos(ang)
    sin = np.sin(ang) * np.where(ii < 16, -1.0, 1.0)[:, None]
    return np.ascontiguousarray(np.stack([cos, sin], axis=1).astype(np.float32))


def _prep(inp):
    f32 = lambda a: np.ascontiguousarray(np.asarray(a, dtype=np.float32))
    w_in = f32(inp["w_in"])[0]
    heads = []
    for h in range(8):
        cols = np.concatenate([np.arange(o + h * 128, o + (h + 1) * 128)
                               for o in (0, 1024, 2048, 3072, 4096, 5120, 6144)])
        heads.append(_tile_w(w_in[:, cols], 128))
    WIN = np.ascontiguousarray(np.stack(heads, 0))
    MODW = _tile_w(f32(inp["mod_w"])[0], 512)
    MODB = f32(inp["mod_b"]).reshape(24, 512)
    WOUT = _tile_w(f32(inp["w_out"])[0], 512)
    w_up = f32(inp["w_up"])[0]
    upcols = np.concatenate([np.concatenate([np.arange(2 * r * 128, (2 * r + 2) * 128),
                                             np.arange(DFF + 2 * r * 128, DFF + (2 * r + 2) * 128)])
                             for r in range(22)])
    WUP = _tile_w(w_up[:, upcols], 512)
    WDOWN = np.ascontiguousarray(f32(inp["w_down"])[0].reshape(11, 4, 128, 4, 512).transpose(0, 3, 2, 1, 4))
    fm = lambda v: np.ascontiguousarray(f32(v).reshape(16, 128).T)
    GMIX, GFFN = fm(inp["norm_mix_g"]), fm(inp["norm_ffn_g"])
    FINALG = f32(inp["final_g"]).reshape(1, 2048)
    LAMV = np.concatenate([f32(inp[k]).reshape(-1) for k in ("lambda_q1", "lambda_k1", "lambda_q2", "lambda_k2")]).reshape(1, 256)
    SUBG = f32(inp["subln_g"]).reshape(1, 128)
    DNG = f32(inp["dn_norm_g"]).reshape(1, 128)
    cq = f32(inp["conv_qkv_w"])[0].reshape(3, 24, 128).transpose(2, 1, 0)
    cf = f32(inp["conv_ffn_w"])[0].reshape(3, 88, 128).transpose(2, 1, 0)
    wg = w_in[:, 7168:7200]
    alog = f32(inp["a_log"])[0].reshape(16)
    dtb = f32(inp["dt_bias"])[0].reshape(16)
    consts, _ = _consts()
    xp, xs = f32(inp["x_prompt"]), f32(inp["x_sample"])
    ck, cvv = f32(inp["cache_k"]), f32(inp["cache_v"])
    sf, sbw = f32(inp["state_fwd"]), f32(inp["state_bwd"])
    cvec, cctx = f32(inp["c"]), f32(inp["c_ctx"])
    swap = np.concatenate([np.arange(8, 16), np.arange(0, 8)])
    maps = []
    shared = {}
    for par in (0, 1):
        wgp = wg if par == 0 else wg[:, np.concatenate([swap, 16 + swap])]
        shared[par] = dict(
            WG=np.ascontiguousarray(wgp.reshape(16, 128, 32).transpose(1, 0, 2)),
            ALOG=np.ascontiguousarray((alog if par == 0 else alog[swap]).reshape(1, 16)),
            DTB=np.ascontiguousarray((dtb if par == 0 else dtb[swap]).reshape(1, 16)),
            CONVQ=np.ascontiguousarray(cq if par == 0 else cq[:, :, ::-1]),
            CONVF=np.ascontiguousarray(cf if par == 0 else cf[:, :, ::-1]),
            ROPE=_rope_tables(par == 1),
        )
    for c in range(NCORES):
        b, par = c // 2, c % 2
        fl = (lambda a, ax: a[(slice(None),) * ax + (slice(None, None, -1),)]) if par else (lambda a, ax: a)
        m = dict(
            XP=np.ascontiguousarray(fl(xp[2 * c:2 * c + 2], 1)).reshape(512, 2048),
            XS=np.ascontiguousarray(fl(xs[b], 0)),
            CK=np.ascontiguousarray(ck[b, 0].transpose(1, 0, 2)),
            CV=np.ascontiguousarray(cvv[b, 0].transpose(1, 0, 2)),
            SF0=np.ascontiguousarray((sbw if par else sf)[b, 0]),
            SB0=np.ascontiguousarray((sf if par else sbw)[b, 0]),
            CVEC=np.ascontiguousarray(np.stack([cctx, cvec[b]], 0).reshape(2, 16, 128).transpose(2, 1, 0)),
            MODW=MODW, MODB=MODB, WIN=WIN, WOUT=WOUT, WUP=WUP, WDOWN=WDOWN, GMIX=GMIX, GFFN=GFFN, FINALG=FINALG,
            LAMV=LAMV, SUBG=SUBG, DNG=DNG, CONSTS=consts,
        )
        m.update(shared[par])
        maps.append(m)
    return maps


def build(stop_after=None):
    nc = bass.Bass("TRN2", target_bir_lowering=False)
    di = lambda n, s: nc.dram_tensor(n, list(s), F32, kind="ExternalInput").ap()
    do = lambda n, s: nc.dram_tensor(n, list(s), F32, kind="ExternalOutput").ap()
    XP, XS = di("XP", (512, 2048)), di("XS", (1024, 2048))
    CK, CV = di("CK", (8, 256, 128)), di("CV", (8, 256, 128))
    SF0, SB0 = di("SF0", (8, 128, 128)), di("SB0", (8, 128, 128))
    CVEC = di("CVEC", (128, 16, 2))
    MODW, MODB = di("MODW", (24, 128, 16, 512)), di("MODB", (24, 512))
    WIN = di("WIN", (8, 7, 128, 16, 128))
    WOUT, WUP, WDOWN = di("WOUT", (4, 128, 16, 512)), di("WUP", (22, 128, 16, 512)), di("WDOWN", (11, 4, 128, 4, 512))
    GMIX, GFFN, FINALG = di("GMIX", (128, 16)), di("GFFN", (128, 16)), di("FINALG", (1, 2048))
    LAMV, SUBG, DNG = di("LAMV", (1, 256)), di("SUBG", (1, 128)), di("DNG", (1, 128))
    CONSTS = di("CONSTS", (128, 10, 128))
    WG, ALOG, DTB = di("WG", (128, 16, 32)), di("ALOG", (1, 16)), di("DTB", (1, 16))
    CONVQ, CONVF, ROPE = di("CONVQ", (128, 24, 3)), di("CONVF", (128, 88, 3)), di("ROPE", (128, 2, 1024))
    YP, YS = do("YP", (512, 2048)), do("YS", (512, 2048))
    NK, NV = do("NK", (512, 8, 128)), do("NV", (512, 8, 128))
    NSF, NSB = do("NSF", (2, 8, 128, 128)), do("NSB", (2, 8, 128, 128))

    with ExitStack() as top:
        S = Sched(nc, top)
        out_deps = []

        arena = Arena(nc, top)

        def alloc(stack, name, shape, dt=F32):
            return arena.alloc(shape, dt)

        PS = [top.enter_context(nc.psum_tensor("ps%d" % i, [128, 512], F32)) for i in range(8)]
        dPS = [Dep(excl=True) for _ in range(8)]

        cst = alloc(top, "cst", (128, 10, 128)); d_cst = Dep()
        S.dma("sp", cst[:], CONSTS, w=[d_cst])
        ident, ones, ublk, lblk, slblk, sublk, eblk, e0, e1 = [cst[:, i, :] for i in range(9)]
        propeb = alloc(top, "propeb", (128, 128), BF16)
        identb = alloc(top, "identb", (128, 128), BF16)
        S.op("dve", lambda e: e.tensor_copy(out=identb[:], in_=cst[:, 0, :]), r=[d_cst], w=[d_cst])
        S.op("dve", lambda e: e.tensor_copy(out=propeb[:], in_=cst[:, 9, :]), r=[d_cst], w=[d_cst])
        small = alloc(top, "small", (128, 1024)); d_small = Dep()
        S.dma("sp", small[:, 0:16], GMIX, w=[d_small])
        S.dma("sp", small[:, 16:32], GFFN, w=[d_small])
        S.dma("sp", small[:, 160:176], ALOG.partition_broadcast(128), w=[d_small])
        S.dma("sp", small[:, 176:192], DTB.partition_broadcast(128), w=[d_small])
        S.dma("sp", small[:, 192:448], LAMV.partition_broadcast(128), w=[d_small])
        S.dma("sp", small[:, 448:576], SUBG.partition_broadcast(128), w=[d_small])
        S.dma("sp", small[:, 576:704], DNG.partition_broadcast(128), w=[d_small])
        s1v = small[:, 32:64].rearrange("p (k c) -> p k c", c=2)
        b1v = small[:, 64:96].rearrange("p (k c) -> p k c", c=2)
        s2v = small[:, 96:128].rearrange("p (k c) -> p k c", c=2)
        b2v = small[:, 128:160].rearrange("p (k c) -> p k c", c=2)
        negA = small[:, 160:176]
        dtbb = small[:, 176:192]
        subgs = small[:, 448:576]
        dngb = small[:, 576:704]
        neglam = small[:, 704:705]
        S.op("act", lambda e: e.activation(out=negA, in_=negA, func=AF.Exp), r=[d_small], w=[d_small])
        S.op("dve", lambda e: e.tensor_scalar(out=negA, in0=negA, scalar1=-1.0, scalar2=None, op0=ALU.mult),
             r=[d_small], w=[d_small])
        S.op("dve", lambda e: e.tensor_scalar(out=subgs, in0=subgs, scalar1=1.0 - LAM_INIT, scalar2=None, op0=ALU.mult),
             r=[d_small], w=[d_small])
        S.op("dve", lambda e: e.tensor_tensor(out=small[:, 192:256], in0=small[:, 192:256], in1=small[:, 256:320], op=ALU.mult),
             r=[d_small], w=[d_small])
        S.op("dve", lambda e: e.tensor_tensor(out=small[:, 320:384], in0=small[:, 320:384], in1=small[:, 384:448], op=ALU.mult),
             r=[d_small], w=[d_small])
        S.op("dve", lambda e: e.reduce_sum(out=small[:, 705:706], in_=small[:, 192:256], axis=mybir.AxisListType.X),
             r=[d_small], w=[d_small])
        S.op("dve", lambda e: e.reduce_sum(out=small[:, 706:707], in_=small[:, 320:384], axis=mybir.AxisListType.X),
             r=[d_small], w=[d_small])
        S.op("act", lambda e: e.activation(out=small[:, 705:707], in_=small[:, 705:707], func=AF.Exp), r=[d_small], w=[d_small])
        S.op("dve", lambda e: e.tensor_tensor(out=neglam, in0=small[:, 706:707], in1=small[:, 705:706], op=ALU.subtract),
             r=[d_small], w=[d_small])
        S.op("dve", lambda e: e.tensor_scalar(out=neglam, in0=neglam, scalar1=-LAM_INIT, scalar2=None, op0=ALU.add),
             r=[d_small], w=[d_small])

        convq = alloc(top, "convq", (128, 24, 3)); d_convq = Dep()
        S.dma("sp", convq[:], CONVQ, w=[d_convq])
        wgb = alloc(top, "wgb", (128, 16, 32), BF16); d_wgb = Dep()
        S.dma("pool", wgb[:], WG, w=[d_wgb])

        cvs = alloc(top, "cvs", (128, 16, 2)); d_cvs = Dep()
        S.dma("sp", cvs[:], CVEC, w=[d_cvs])
        S.op("act", lambda e: e.activation(out=cvs[:], in_=cvs[:], func=AF.Silu), r=[d_cvs], w=[d_cvs])
        d_Lp = Dep()
        Ls = {}

        def make_L(scope):
            Lp = alloc(scope, "Lp", (128, 16, 128), BF16)
            L0 = alloc(scope, "L0", (128, 16, 128), BF16)
            L1 = alloc(scope, "L1", (128, 16, 128), BF16)
            for (dst, c0, c1, j) in ((Lp, 0, 64, 0), (Lp, 64, 128, 1), (L0, 0, 128, 0), (L1, 0, 128, 1)):
                S.op("dve", lambda e: e.tensor_copy(out=dst[:, :, c0:c1],
                                                    in_=cvs[:, :, j:j + 1].broadcast_to([128, 16, c1 - c0])),
                     r=[d_cvs], w=[d_Lp])
            Ls["Lp"], Ls["L0"], Ls["L1"] = Lp, L0, L1

        oT = alloc(top, "oT", (128, 16, 1088), BF16)
        d_oT = [Dep() for _ in range(16)]

        def mod_block(stk_bufs, blk, lhs, psum_i):
            wbuf, d_w, mbb, d_mbb, mrow, d_mrow = stk_bufs
            S.dma("sp", mbb[:], MODB[blk:blk + 1, :].partition_broadcast(128), w=[d_mbb])
            for kc in range(16):
                S.op("pe", lambda e: e.matmul(PS[psum_i][:], lhsT=lhs[:, kc, :], rhs=wbuf[:, kc, :],
                                              start=(kc == 0), stop=(kc == 15)),
                     r=[d_w, d_Lp], w=[dPS[psum_i]], signal=(kc == 15))
            S.op("dve", lambda e: e.tensor_tensor(out=mrow[:], in0=PS[psum_i][:], in1=mbb[:], op=ALU.add),
                 r=[dPS[psum_i], d_mbb], w=[d_mrow])

        def mod_featmajor(stk_bufs, vec_i, dstv, psA, psB):
            wstream, mbb, d_mbb, mrow, d_mrow = stk_bufs
            for q in range(4):
                blk = vec_i * 4 + q
                wb, d_w = wstream.get()
                mod_block((wb, d_w, mbb, d_mbb, mrow, d_mrow), blk, Ls["Lp"], psA)
                for j in range(4):
                    S.op("pe", lambda e: e.transpose(out=PS[psB][:, j * 128:(j + 1) * 128],
                                                     in_=mrow[:, j * 128:(j + 1) * 128], identity=ident),
                         r=[d_mrow, d_cst], w=[dPS[psB]], signal=(j == 3))
                for j in range(4):
                    kc = q * 4 + j
                    S.op("dve", lambda e: e.tensor_copy(out=dstv[:, kc, :], in_=PS[psB][:, j * 128:(j + 1) * 128:64]),
                         r=[dPS[psB]], w=[d_small])

        with arena.scope() as ph:
            make_L(ph)
            wbufs = [(alloc(ph, "mw%d" % i, (128, 16, 512), BF16), Dep()) for i in range(2)]
            mbb = alloc(ph, "mbb", (128, 512)); d_mbb = Dep()
            mrow = alloc(ph, "mrow", (128, 512)); d_mrow = Dep()
            bufs = (WStream(S, wbufs, [MODW[b_] for b_ in range(0, 8)], 1), mbb, d_mbb, mrow, d_mrow)
            bufs[0].prefetch()
            mod_featmajor(bufs, 0, b1v, 0, 1)
            mod_featmajor(bufs, 1, s1v, 0, 1)
            S.op("dve", lambda e: e.tensor_scalar(out=small[:, 32:64], in0=small[:, 32:64], scalar1=1.0, scalar2=None,
                                                  op0=ALU.add), r=[d_small], w=[d_small])
            S.op("dve", lambda e: e.tensor_tensor(out=s1v, in0=s1v, in1=small[:, 0:16].unsqueeze(2).broadcast_to([128, 16, 2]),
                                                  op=ALU.mult), r=[d_small], w=[d_small])
            S.barrier()

        if stop_after == "p0":
            print("ops", S.nops, "waits", S.nwaits, "sim", S.simulate()[:2])
            return nc
        NHEADS = int(os.environ.get("KHEADS", "8"))
        d_hT = Dep()
        hTbox = {}

        def rstd_from_ss(st, d_st, n, scale):
            S.op("act", lambda e: e.activation(out=st[:, n:2 * n], in_=st[:, 0:n], func=AF.Ln, bias=EPS, scale=scale),
                 r=[d_st], w=[d_st])
            S.op("act", lambda e: e.activation(out=st[:, n:2 * n], in_=st[:, n:2 * n], func=AF.Exp, scale=-0.5),
                 r=[d_st], w=[d_st])

        def norm_to_featmajor(stack, src_rows, ntile, dst, d_dst, sv, bv, cvi, tag):
            xts = [(alloc(stack, "%sxt%d" % (tag, i), (128, 2048)), Dep()) for i in range(2)]
            xns = [(alloc(stack, "%sxn%d" % (tag, i), (128, 2048)), Dep()) for i in range(2)]
            st = alloc(stack, tag + "st", (128, 2 * ntile)); d_st = Dep()
            S.op("dve", lambda e: e.memset(st[:], 0.0), w=[d_st])
            for t in range(ntile):
                xt, d_xt = xts[t % 2]
                xn, d_xn = xns[t % 2]
                S.dma("sp", xt[:], src_rows(t), w=[d_xt])
                S.op("act", lambda e: e.activation(out=xn[:], in_=xt[:], func=AF.Square, accum_out=st[:, 2 * t:2 * t + 1]),
                     r=[d_xt], w=[d_xn, d_st])
                S.op("act", lambda e: e.activation(out=st[:, 2 * t + 1:2 * t + 2], in_=st[:, 2 * t:2 * t + 1], func=AF.Ln,
                                                   bias=EPS, scale=1.0 / D), r=[d_st], w=[d_st])
                S.op("act", lambda e: e.activation(out=st[:, 2 * t + 1:2 * t + 2], in_=st[:, 2 * t + 1:2 * t + 2],
                                                   func=AF.Exp, scale=-0.5), r=[d_st], w=[d_st])
                S.op("dve", lambda e: e.tensor_scalar(out=xn[:], in0=xt[:], scalar1=st[:, 2 * t + 1:2 * t + 2], scalar2=None,
                                                      op0=ALU.mult), r=[d_xt, d_st], w=[d_xn])
                for g in range(4):
                    for j in range(4):
                        kc = g * 4 + j
                        S.op("pe", lambda e: e.transpose(out=PS[g][:, j * 128:(j + 1) * 128],
                                                         in_=xn[:, kc * 128:(kc + 1) * 128], identity=ident),
                             r=[d_xn, d_cst], w=[dPS[g]], signal=(j == 3))
                    for j in range(4):
                        kc = g * 4 + j
                        if j % 2 == 0:
                            S.op("act", lambda e: e.activation(out=dst[:, kc, t * 128:(t + 1) * 128],
                                                               in_=PS[g][:, j * 128:(j + 1) * 128], func=AF.Identity,
                                                               bias=bv[:, kc, cvi:cvi + 1], scale=sv[:, kc, cvi:cvi + 1]),
                                 r=[dPS[g], d_small], w=[d_dst])
                        else:
                            S.op("dve", lambda e: e.tensor_scalar(out=dst[:, kc, t * 128:(t + 1) * 128],
                                                                  in0=PS[g][:, j * 128:(j + 1) * 128],
                                                                  scalar1=sv[:, kc, cvi:cvi + 1], scalar2=bv[:, kc, cvi:cvi + 1],
                                                                  op0=ALU.mult, op1=ALU.add),
                                 r=[dPS[g], d_small], w=[d_dst])

        def passes(n):
            return [(a, min(a + 512, n)) for a in range(0, n, 512)]

        def proj(wt, d_wt, c0, c1, psum_i, M=128, mo=0):
            for kc in range(16):
                S.op("pe", lambda e: e.matmul(PS[psum_i][0:M, 0:c1 - c0], lhsT=wt[:, kc, mo:mo + M], rhs=hTbox["hT"][:, kc, c0:c1],
                                              start=(kc == 0), stop=(kc == 15)),
                     r=[d_wt, d_hT], w=[dPS[psum_i]], signal=(kc == 15))

        def mixer_group(gi):
            isS = gi == 1
            ntok = 1024 if isS else 512
            ntile = ntok // 128
            mcols = 576 if isS else 512
            ocol0 = 512 if isS else 0
            seqs = [(0, 1024)] if isS else [(0, 256), (256, 512)]
            mt = 5 if isS else 4
            X = XS if isS else XP
            with arena.scope() as ph:
                norm_to_featmajor(ph, lambda t: X[t * 128:(t + 1) * 128, :], ntile, hTbox["hT"], d_hT, s1v, b1v, gi, "n1")
                S.barrier()
            if stop_after == "n1":
                raise _Stop()
            with arena.scope() as gs:
                G = alloc(gs, "G", (128, 10, ntile, 16)); d_G = Dep()
                for t in range(ntile):
                    for kc in range(16):
                        S.op("pe", lambda e: e.matmul(PS[0][:, t * 32:(t + 1) * 32], lhsT=hTbox["hT"][:, kc, t * 128:(t + 1) * 128],
                                                      rhs=wgb[:, kc, :], start=(kc == 0), stop=(kc == 15)),
                             r=[d_hT, d_wgb], w=[dPS[0]], signal=(kc == 15 and t == ntile - 1))
                pg = PS[0][:, 0:ntile * 32].rearrange("p (t c) -> p t c", c=32)
                S.op("act", lambda e: e.activation(out=G[:, 0], in_=pg[:, :, 0:16], func=AF.Exp, scale=-1.0), r=[dPS[0]], w=[d_G])
                S.op("dve", lambda e: e.tensor_scalar(out=G[:, 0], in0=G[:, 0], scalar1=1.0, scalar2=None, op0=ALU.add), r=[d_G], w=[d_G])
                S.op("dve", lambda e: e.reciprocal(out=G[:, 0], in_=G[:, 0]), r=[d_G], w=[d_G])
                S.op("dve", lambda e: e.tensor_tensor(out=G[:, 1], in0=pg[:, :, 16:32],
                                                      in1=dtbb.unsqueeze(1).broadcast_to([128, ntile, 16]), op=ALU.add),
                     r=[dPS[0], d_small], w=[d_G])
                S.op("act", lambda e: e.activation(out=G[:, 1], in_=G[:, 1], func=AF.Exp), r=[d_G], w=[d_G])
                S.op("act", lambda e: e.activation(out=G[:, 1], in_=G[:, 1], func=AF.Ln, bias=1.0), r=[d_G], w=[d_G])
                S.op("dve", lambda e: e.tensor_tensor(out=G[:, 1], in0=G[:, 1],
                                                      in1=negA.unsqueeze(1).broadcast_to([128, ntile, 16]), op=ALU.mult),
                     r=[d_G, d_small], w=[d_G])
                gflat = G[:, 1].rearrange("p t c -> p (t c)")
                n16 = ntile * 16
                for i, m in enumerate((ublk, lblk, eblk)):
                    S.op("pe", lambda e: e.matmul(PS[1][:, i * n16:(i + 1) * n16], lhsT=m, rhs=gflat, start=True, stop=True),
                         r=[d_G, d_cst], w=[dPS[1]], signal=(i == 2))
                for i, m in enumerate((e0, e1)):
                    S.op("pe", lambda e: e.matmul(PS[2][:, i * n16:(i + 1) * n16], lhsT=m, rhs=gflat, start=True, stop=True),
                         r=[d_G, d_cst], w=[dPS[2]], signal=(i == 1))
                p1 = lambda i: PS[1][:, i * n16:(i + 1) * n16].rearrange("p (t c) -> p t c", c=16)
                p2 = lambda i: PS[2][:, i * n16:(i + 1) * n16].rearrange("p (t c) -> p t c", c=16)
                S.op("dve", lambda e: e.tensor_copy(out=G[:, 2, :, 0:8], in_=p1(0)[:, :, 0:8]), r=[dPS[1]], w=[d_G])
                S.op("dve", lambda e: e.tensor_copy(out=G[:, 2, :, 8:16], in_=p1(1)[:, :, 8:16]), r=[dPS[1]], w=[d_G])
                S.op("dve", lambda e: e.tensor_copy(out=G[:, 3], in_=p1(2)), r=[dPS[1]], w=[d_G])
                S.op("act", lambda e: e.activation(out=G[:, 4], in_=G[:, 2], func=AF.Exp), r=[d_G], w=[d_G])
                S.op("dve", lambda e: e.tensor_tensor(out=G[:, 9], in0=G[:, 3], in1=G[:, 2], op=ALU.subtract), r=[d_G], w=[d_G])
                S.op("act", lambda e: e.activation(out=G[:, 5], in_=G[:, 9], func=AF.Exp), r=[d_G], w=[d_G])
                S.op("act", lambda e: e.activation(out=G[:, 6], in_=p2(0), func=AF.Exp), r=[dPS[2]], w=[d_G])
                S.op("act", lambda e: e.activation(out=G[:, 7], in_=p2(1), func=AF.Exp), r=[dPS[2]], w=[d_G])
                S.op("dve", lambda e: e.tensor_tensor(out=G[:, 8], in0=G[:, 0], in1=G[:, 4], op=ALU.mult), r=[d_G], w=[d_G])
                BETA, GG, GC, EGC, EKD, DL, BEGC = G[:, 0], G[:, 1], G[:, 2], G[:, 4], G[:, 5], (G[:, 6], G[:, 7]), G[:, 8]
                if stop_after == "gates":
                    S.barrier()
                    raise _Stop()

                wch = [(alloc(gs, "wch%d" % i, (128, 16, 128), BF16), Dep()) for i in range(6)]
                win_stream = WStream(S, wch, [WIN[h_, c_] for h_ in range(NHEADS) for c_ in range(7)], 5)
                win_stream.prefetch()

                def load_w(h, ci):
                    return win_stream.get()

                def attention_head(h):
                    S.mark("g%d h%d attn" % (gi, h))
                    with arena.scope() as hs:
                        nk = ntok + (256 if isS else 0)
                        nkt = nk // 128
                        qT = alloc(hs, "qT", (128, mcols), BF16); d_qT = Dep()
                        kT = alloc(hs, "kT", (128, nk), BF16); d_kT = Dep()
                        vaug = alloc(hs, "vaug", (128, nkt, 132), BF16); d_va = Dep()
                        S.op("pool", lambda e: e.memset(vaug[:], 1.0), w=[d_va])
                        if stop_after == "attnA":
                            load_w(h, 0)
                            S.barrier()
                            raise _Stop()
                        tmpf = [(alloc(hs, "tmpf%d" % i, (128, 512)), Dep()) for i in range(2)]
                        tmpb = [(alloc(hs, "tmpb%d" % i, (128, 512), BF16), Dep()) for i in range(2)]
                        if isS:
                            rope = alloc(hs, "rope", (128, 2, 1024)); d_rope = Dep()
                            S.dma("sp", rope[:], ROPE, w=[d_rope])
                        stage = alloc(hs, "stage", (128, 4, 128)); d_stage = Dep()

                        def qk_proj(ci, dst, d_dst, ncols, keep_f32=None):
                            wt, d_wt = load_w(h, ci)
                            for pi, (c0, c1) in enumerate(passes(ncols)):
                                n = c1 - c0
                                pb = pi % 2
                                proj(wt, d_wt, c0, c1, pb)
                                KQ = int(os.environ.get("KQ", "9"))
                                if KQ == 1:
                                    continue
                                if keep_f32 is not None and KQ >= 3:
                                    S.op("act", lambda e: e.activation(out=keep_f32[0][:, c0:c1], in_=PS[pb][:, 0:n], func=AF.Identity),
                                         r=[dPS[pb]], w=[keep_f32[1]])
                                if not isS:
                                    S.op("dve", lambda e: e.tensor_copy(out=dst[:, c0:c1], in_=PS[pb][:, 0:n]),
                                         r=[dPS[pb]], w=[d_dst])
                                else:
                                    tb, d_tb = tmpb[pi % 2]
                                    tf, d_tf = tmpf[pi % 2]
                                    S.op("act", lambda e: e.activation(out=tb[:, 0:n], in_=PS[pb][:, 0:n], func=AF.Identity), r=[dPS[pb]], w=[d_tb])
                                    S.op("dve", lambda e: e.tensor_tensor(out=tf[:, 0:n], in0=PS[pb][:, 0:n],
                                                                          in1=rope[:, 0, c0:c1], op=ALU.mult),
                                         r=[dPS[pb], d_rope], w=[d_tf])
                                    S.op("pe", lambda e: e.matmul(PS[2 + pb][:, 0:n], lhsT=propeb[:], rhs=tb[:, 0:n],
                                                                  start=True, stop=True), r=[d_tb, d_cst], w=[dPS[2 + pb]])
                                    S.op("dve", lambda e: e.tensor_tensor(out=tb[:, 0:n], in0=PS[2 + pb][:, 0:n],
                                                                          in1=rope[:, 1, c0:c1], op=ALU.mult),
                                         r=[dPS[2 + pb], d_rope], w=[d_tb])
                                    S.op("pool", lambda e: e.tensor_tensor(out=dst[:, c0:c1], in0=tf[:, 0:n], in1=tb[:, 0:n],
                                                                           op=ALU.add), r=[d_tf, d_tb], w=[d_dst])

                        kf = None
                        if not isS:
                            kf = (alloc(hs, "kf", (128, 512)), Dep())
                        qk_proj(0, qT, d_qT, mcols)
                        qk_proj(1, kT, d_kT, ntok, keep_f32=kf)
                        if stop_after == "attnB":
                            S.barrier()
                            raise _Stop()
                        avf = alloc(hs, "avf", (128, ntok)); d_avf = Dep()
                        wt, d_wt = load_w(h, 2)
                        for pi, (c0, c1) in enumerate(passes(ntok)):
                            proj(wt, d_wt, c0, c1, pi % 2)
                            S.op("act", lambda e: e.activation(out=avf[:, c0:c1], in_=PS[pi % 2][:, 0:c1 - c0], func=AF.Identity),
                                 r=[dPS[pi % 2]], w=[d_avf])
                        for t in range(ntile):
                            pb = 4 + t % 2
                            S.op("pe", lambda e: e.transpose(out=PS[pb][:, 0:128], in_=avf[:, t * 128:(t + 1) * 128], identity=ident),
                                 r=[d_avf, d_cst], w=[dPS[pb]])
                            S.op("act", lambda e: e.activation(out=vaug[:, t, 0:128], in_=PS[pb][:, 0:128], func=AF.Identity), r=[dPS[pb]], w=[d_va])
                            if not isS:
                                S.op("dve", lambda e: e.tensor_copy(out=stage[:, t, :], in_=PS[pb][:, 0:128]),
                                     r=[dPS[pb]], w=[d_stage])
                        if not isS:
                            S.dma("sp", NV[:, h, :].rearrange("(t p) d -> p t d", p=128), stage[:], r=[d_stage], evlist=out_deps)
                            kf_t, d_kf = kf
                            stage2 = alloc(hs, "stage2", (128, 4, 128)); d_stage2 = Dep()
                            for t in range(4):
                                pb = 4 + t % 2
                                S.op("pe", lambda e: e.transpose(out=PS[pb][:, 0:128], in_=kf_t[:, t * 128:(t + 1) * 128], identity=ident),
                                     r=[d_kf, d_cst], w=[dPS[pb]])
                                S.op("dve", lambda e: e.tensor_copy(out=stage2[:, t, :], in_=PS[pb][:, 0:128]),
                                     r=[dPS[pb]], w=[d_stage2])
                            S.dma("sp", NK[:, h, :].rearrange("(t p) d -> p t d", p=128), stage2[:], r=[d_stage2], evlist=out_deps)
                        else:
                            ckf = alloc(hs, "ckf", (128, 2, 128)); d_ckf = Dep()
                            S.dma("sp", ckf[:], CK[h].rearrange("(t p) d -> p t d", p=128), w=[d_ckf])
                            S.dma("pool", vaug[:, 8:10, 0:128], CV[h].rearrange("(t p) d -> p t d", p=128), w=[d_va])
                            for t in range(2):
                                pb = 4 + t
                                S.op("pe", lambda e: e.transpose(out=PS[pb][:, 0:128], in_=ckf[:, t, :], identity=ident),
                                     r=[d_ckf, d_cst], w=[dPS[pb]])
                                S.op("dve", lambda e: e.tensor_copy(out=kT[:, 1024 + t * 128:1024 + (t + 1) * 128],
                                                                    in_=PS[pb][:, 0:128]), r=[dPS[pb]], w=[d_kT])
                        if stop_after == "attnproj":
                            S.barrier()
                            raise _Stop()
                        S.mark("g%d h%d scores" % (gi, h))
                        On = alloc(hs, "On", (128, 2, mt, 128)); d_On = Dep()
                        Eall = alloc(hs, "Eall", (128, nkt, 576), BF16); d_E = Dep()
                        rden = alloc(hs, "rden", (128, 16)); d_rden = Dep()
                        ectr = 0
                        for (q0, q1) in ([(0, 576)] if isS else [(0, 256), (256, 512)]):
                            nq = q1 - q0
                            kts = list(range(nkt)) if isS else [q0 // 128, q0 // 128 + 1]
                            qtl = [(a, min(a + 128, nq)) for a in range(0, nq, 128)]
                            for m in range(2):
                                rows = slice(m * 64, m * 64 + 64)
                                for ki, kt in enumerate(kts):
                                    ectr += 1
                                    for pi, (a, b) in enumerate(passes(nq)):
                                        pb = 2 * (ectr % 2) + pi
                                        S.op("pe", lambda e: e.matmul(PS[pb][:, 0:b - a], lhsT=kT[rows, kt * 128:(kt + 1) * 128],
                                                                      rhs=qT[rows, q0 + a:q0 + b], start=True, stop=True),
                                             r=[d_kT, d_qT], w=[dPS[pb]])
                                        S.op("act", lambda e: e.activation(out=Eall[:, ki, a:b], in_=PS[pb][:, 0:b - a], func=AF.Exp,
                                                                           scale=0.125), r=[dPS[pb]], w=[d_E])
                                for qi, (a, b) in enumerate(qtl):
                                    pb = 4 + qi // 3
                                    co = (qi % 3) * 129
                                    for ki, kt in enumerate(kts):
                                        S.op("pe", lambda e: e.matmul(PS[pb][0:b - a, co:co + 129], lhsT=Eall[:, ki, a:b],
                                                                      rhs=vaug[:, kt, 0:129], start=(ki == 0), stop=(ki == len(kts) - 1)),
                                             r=[d_E, d_va], w=[dPS[pb]], signal=(ki == len(kts) - 1))
                                for qi, (a, b) in enumerate(qtl):
                                    pb = 4 + qi // 3
                                    co = (qi % 3) * 129
                                    n = b - a
                                    tq = (q0 + a) // 128
                                    S.op("dve", lambda e: e.reciprocal(out=rden[0:n, qi:qi + 1], in_=PS[pb][0:n, co + 128:co + 129]),
                                         r=[dPS[pb]], w=[d_rden])
                                    S.op("dve", lambda e: e.tensor_scalar(out=On[0:n, m, tq, :], in0=PS[pb][0:n, co:co + 128],
                                                                          scalar1=rden[0:n, qi:qi + 1], scalar2=None, op0=ALU.mult),
                                         r=[dPS[pb], d_rden], w=[d_On])
                        if stop_after == "attnpv":
                            S.barrier()
                            raise _Stop()
                        st = alloc(hs, "ast", (128, 2 * mt)); d_st = Dep()
                        S.op("dve", lambda e: e.memset(st[:], 0.0), w=[d_st])
                        for tq in range(mt):
                            n = 64 if (isS and tq == 4) else 128
                            S.op("dve", lambda e: e.scalar_tensor_tensor(out=On[0:n, 0, tq, :], in0=On[0:n, 1, tq, :], scalar=neglam[0:n, :],
                                                                         in1=On[0:n, 0, tq, :], op0=ALU.mult, op1=ALU.add),
                                 r=[d_On, d_small], w=[d_On])
                            S.op("act", lambda e: e.activation(out=On[0:n, 1, tq, :], in_=On[0:n, 0, tq, :], func=AF.Square,
                                                               accum_out=st[0:n, tq:tq + 1]), r=[d_On], w=[d_On, d_st])
                        rstd_from_ss(st, d_st, mt, 1.0 / 128)
                        for tq in range(mt):
                            n = 64 if (isS and tq == 4) else 128
                            pb = 2 + tq % 2
                            S.op("dve", lambda e: e.scalar_tensor_tensor(out=On[0:n, 1, tq, :], in0=On[0:n, 0, tq, :],
                                                                         scalar=st[0:n, mt + tq:mt + tq + 1], in1=subgs[0:n, :],
                                                                         op0=ALU.mult, op1=ALU.mult),
                                 r=[d_On, d_st, d_small], w=[d_On])
                            S.op("pe", lambda e: e.transpose(out=PS[pb][:, 0:n], in_=On[0:n, 1, tq, :], identity=cst[0:n, 0, 0:n]),
                                 r=[d_On, d_cst], w=[dPS[pb]])
                            S.op("act", lambda e: e.activation(out=oT[:, h, ocol0 + tq * 128:ocol0 + tq * 128 + n], in_=PS[pb][:, 0:n], func=AF.Identity),
                                 r=[dPS[pb]], w=[d_oT[h]])
                        if S.rec is None:
                            S.barrier()

                attention_head(0)
                for h in range(NHEADS):
                    if stop_after == "attn":
                        if h + 1 < NHEADS:
                            attention_head(h + 1)
                        continue
                    with arena.scope() as hs:
                        S.mark("g%d h%d dnproj" % (gi, h))

                        def hook(h=h):
                            if h + 1 >= NHEADS or os.environ.get("KNOILV"):
                                if h + 1 < NHEADS:
                                    return None
                                return []
                            S.rec = []
                            arena.norelease = True
                            attention_head(h + 1)
                            arena.norelease = False
                            rec, S.rec = S.rec, None
                            return rec

                        dn_head(hs, gi, h, load_w, G_=(BETA, GG, GC, EGC, EKD, DL, BEGC), d_G=d_G, interleave=hook)
                        S.barrier()
                    if os.environ.get("KNOILV") and h + 1 < NHEADS:
                        attention_head(h + 1)
                S.barrier()

        def dn_head(hs, gi, h, load_w, G_, d_G, interleave=None):
            BETA, GG, GC, EGC, EKD, DL, BEGC = G_
            isS = gi == 1
            ntok = 1024 if isS else 512
            ntile = ntok // 128
            mcols = 576 if isS else 512
            ocol0 = 512 if isS else 0
            mt = 5 if isS else 4
            seqs = [(0, 1024)] if isS else [(0, 256), (256, 512)]
            yq = alloc(hs, "yq", (128, ntok))
            yqb = alloc(hs, "yqb", (128, ntok), BF16); ykb = alloc(hs, "ykb", (128, ntok), BF16); d_yb = Dep()
            gsT = alloc(hs, "gsT", (128, mcols)); d_gs = Dep()
            dirbuf = []
            for d_ in range(2):
                ntd = mt if (isS and d_ == 0) else ntile
                dirbuf.append(dict(uw=alloc(hs, "uw%d" % d_, (128, ntd, 256), BF16), kdq=alloc(hs, "kdq%d" % d_, (128, ntd, 256), BF16), d=Dep()))
            inner_mark = arena.ptr
            yk = alloc(hs, "yk", (128, ntok)); yv = alloc(hs, "yv", (128, ntok))
            d_y = [Dep(), Dep(), Dep()]
            zrs = [(alloc(hs, "zr%d" % i_, (128, ntok)), Dep()) for i_ in range(2)]
            for which, (ci, yy) in enumerate(((3, yq), (4, yk), (5, yv))):
                zr, d_zr = zrs[which % 2]
                wt, d_wt = load_w(h, ci)
                cw = convq[:, which * 8 + h, :]
                for pi, (c0, c1) in enumerate(passes(ntok)):
                    proj(wt, d_wt, c0, c1, pi % 2)
                    S.op("act", lambda e: e.activation(out=zr[:, c0:c1], in_=PS[pi % 2][:, 0:c1 - c0], func=AF.Identity), r=[dPS[pi % 2]], w=[d_zr])
                S.op("dve", lambda e: e.tensor_scalar(out=yy[:], in0=zr[:], scalar1=cw[:, 1:2], scalar2=None, op0=ALU.mult),
                     r=[d_zr, d_convq], w=[d_y[which]])
                for (a, b) in seqs:
                    S.op("dve", lambda e: e.scalar_tensor_tensor(out=yy[:, a + 1:b], in0=zr[:, a:b - 1], scalar=cw[:, 0:1],
                                                                 in1=yy[:, a + 1:b], op0=ALU.mult, op1=ALU.add),
                         r=[d_zr, d_convq], w=[d_y[which]])
                    S.op("dve", lambda e: e.scalar_tensor_tensor(out=yy[:, a:b - 1], in0=zr[:, a + 1:b], scalar=cw[:, 2:3],
                                                                  in1=yy[:, a:b - 1], op0=ALU.mult, op1=ALU.add),
                         r=[d_zr, d_convq], w=[d_y[which]])
                S.op("act", lambda e: e.activation(out=yy[:], in_=yy[:], func=AF.Silu), r=[d_y[which]], w=[d_y[which]])
            wt, d_wt = load_w(h, 6)
            for pi, (c0, c1) in enumerate(passes(mcols)):
                proj(wt, d_wt, c0, c1, pi % 2)
                S.op("act", lambda e: e.activation(out=gsT[:, c0:c1], in_=PS[pi % 2][:, 0:c1 - c0], func=AF.Silu),
                     r=[dPS[pi % 2]], w=[d_gs])
            for which, yy in ((0, yq), (1, yk)):
                zr, d_zr = zrs[(which + 1) % 2]
                S.op("pool", lambda e: e.tensor_tensor(out=zr[:], in0=yy[:], in1=yy[:], op=ALU.mult), r=[d_y[which]], w=[d_zr])
                for pi, (c0, c1) in enumerate(passes(ntok)):
                    pb = 2 + pi % 2
                    n = c1 - c0
                    S.op("pe", lambda e: e.matmul(PS[pb][:, 0:n], lhsT=ones, rhs=zr[:, c0:c1], start=True, stop=True),
                         r=[d_zr, d_cst], w=[dPS[pb]])
                    S.op("act", lambda e: e.activation(out=zr[:, c0:c1], in_=PS[pb][:, 0:n], func=AF.Ln, bias=EPS, scale=1.0),
                         r=[dPS[pb]], w=[d_zr])
                    S.op("act", lambda e: e.activation(out=zr[:, c0:c1], in_=zr[:, c0:c1], func=AF.Exp, scale=-0.5),
                         r=[d_zr], w=[d_zr])
                sc = 128.0 ** -0.5 if which == 0 else 1.0
                S.op("dve", lambda e: e.scalar_tensor_tensor(out=yy[:], in0=yy[:], scalar=sc, in1=zr[:], op0=ALU.mult, op1=ALU.mult),
                     r=[d_zr, d_y[which]], w=[d_y[which]])
                S.op("act", lambda e: e.activation(out=(yqb if which == 0 else ykb)[:], in_=yy[:], func=AF.Identity),
                     r=[d_y[which]], w=[d_yb])
            ktok = alloc(hs, "ktok", (128, ntile, 128)); vtok = alloc(hs, "vtok", (128, ntile, 128)); d_kv = Dep()
            KK = alloc(hs, "KK", (128, ntile, 128)); QKT = alloc(hs, "QKT", (128, mt, 128)); d_KQ = Dep()
            for t in range(ntile):
                ts = slice(t * 128, (t + 1) * 128)
                pb = 4 + t % 2
                S.op("pe", lambda e: e.transpose(out=PS[pb][:, 0:128], in_=yk[:, ts], identity=ident), r=[d_y[1], d_cst], w=[dPS[pb]], signal=False)
                S.op("pe", lambda e: e.transpose(out=PS[pb][:, 128:256], in_=yv[:, ts], identity=ident), r=[d_y[2], d_cst], w=[dPS[pb]])
                S.op("act", lambda e: e.activation(out=ktok[:, t, :], in_=PS[pb][:, 0:128], func=AF.Identity), r=[dPS[pb]], w=[d_kv])
                S.op("act", lambda e: e.activation(out=vtok[:, t, :], in_=PS[pb][:, 128:256], func=AF.Identity), r=[dPS[pb]], w=[d_kv])
                pb2 = 6 + t % 2
                S.op("pe", lambda e: e.matmul(PS[pb2][:, 0:128], lhsT=ykb[:, ts], rhs=ykb[:, ts], start=True, stop=True),
                     r=[d_yb], w=[dPS[pb2]], signal=(t >= mt))
                S.op("dve", lambda e: e.tensor_copy(out=KK[:, t, :], in_=PS[pb2][:, 0:128]), r=[dPS[pb2]], w=[d_KQ]) if t >= mt else None
                if t < mt:
                    S.op("pe", lambda e: e.matmul(PS[pb2][:, 128:256], lhsT=ykb[:, ts], rhs=yqb[:, ts], start=True, stop=True),
                         r=[d_yb], w=[dPS[pb2]])
                    S.op("dve", lambda e: e.tensor_copy(out=KK[:, t, :], in_=PS[pb2][:, 0:128]), r=[dPS[pb2]], w=[d_KQ])
                    S.op("dve", lambda e: e.tensor_copy(out=QKT[:, t, :], in_=PS[pb2][:, 128:256]), r=[dPS[pb2]], w=[d_KQ])
            S.mark("g%d h%d solve" % (gi, h))
            NP = 8
            sol = [dict(A=[alloc(hs, "sA%d_%d" % (i, j), (128, 128)) for j in range(2)],
                        BW=[alloc(hs, "sBW%d_%d" % (i, j), (128, 256)) for j in range(2)],
                        x1=alloc(hs, "sx1_%d" % i, (128, 128)), x2=alloc(hs, "sx2_%d" % i, (128, 128)),
                        d=Dep()) for i in range(NP)]
            vb = [(alloc(hs, "vbk%d" % i, (128, 256)), Dep()) for i in range(2)]
            probs = [(d_, t) for d_ in range(2) for t in (range(ntile) if (d_ == 1 or not isS) else range(mt))]

            def pinfo(d_):
                return (d_ * 8 + h, (ublk, slblk, ublk) if d_ == 0 else (lblk, sublk, lblk), dirbuf[d_])

            nbatch = (len(probs) + NP - 1) // NP
            bsz = (len(probs) + nbatch - 1) // nbatch
            for _once in (0,):
                for g0 in range(0, len(probs), bsz):
                    grp = probs[g0:g0 + bsz]
                    for i, (d_, t) in enumerate(grp):
                        hd, (Mg, Ms, Mi), db = pinfo(d_)
                        sl = sol[i]
                        pb = i % 8
                        gcol = GG[:, t, hd:hd + 1]
                        S.op("pool", lambda e: e.tensor_tensor(out=sl["x1"][:], in0=Mg, in1=gcol.broadcast_to([128, 128]), op=ALU.mult),
                             r=[d_G, d_cst], w=[sl["d"]])
                        S.op("pe", lambda e: e.matmul(PS[pb][:, 0:128], lhsT=ones, rhs=sl["x1"][:], start=True, stop=True),
                             r=[sl["d"], d_cst], w=[dPS[pb]])
                        S.op("dve", lambda e: e.tensor_scalar(out=sl["x1"][:], in0=PS[pb][:, 0:128], scalar1=GC[:, t, hd:hd + 1],
                                                              scalar2=0.0, op0=ALU.subtract, op1=ALU.max),
                             r=[dPS[pb], d_G], w=[sl["d"]])
                        S.op("dve", lambda e: e.tensor_scalar(out=sl["x2"][:], in0=PS[pb][:, 0:128], scalar1=GC[:, t, hd:hd + 1],
                                                              scalar2=0.0, op0=ALU.subtract, op1=ALU.min),
                             r=[dPS[pb], d_G], w=[sl["d"]])
                        S.op("act", lambda e: e.activation(out=sl["x1"][:], in_=sl["x1"][:], func=AF.Exp, scale=-1.0),
                             r=[sl["d"]], w=[sl["d"]])
                        S.op("act", lambda e: e.activation(out=sl["x2"][:], in_=sl["x2"][:], func=AF.Exp), r=[sl["d"]], w=[sl["d"]])
                        S.op("pool", lambda e: e.tensor_tensor(out=sl["x1"][:], in0=sl["x1"][:], in1=Ms, op=ALU.mult),
                             r=[sl["d"], d_cst], w=[sl["d"]])
                        S.op("dve", lambda e: e.scalar_tensor_tensor(out=sl["A"][0][:], in0=KK[:, t, :], scalar=BETA[:, t, hd:hd + 1],
                                                                     in1=sl["x1"][:], op0=ALU.mult, op1=ALU.mult),
                             r=[d_KQ, d_G, sl["d"]], w=[sl["d"]])
                        if t < mt:
                            S.op("pool", lambda e: e.tensor_tensor(out=sl["x2"][:], in0=sl["x2"][:], in1=Mi, op=ALU.mult),
                                 r=[sl["d"], d_cst], w=[sl["d"]])
                            S.op("pool", lambda e: e.tensor_tensor(out=db["kdq"][:, t, 128:256], in0=QKT[:, t, :], in1=sl["x2"][:], op=ALU.mult),
                                 r=[d_KQ, sl["d"]], w=[db["d"]])
                    for i, (d_, t) in enumerate(grp):
                        hd, (Mg, Ms, Mi), db = pinfo(d_)
                        sl = sol[i]
                        pb = i % 8
                        S.op("pe", lambda e: e.transpose(out=PS[pb][:, 0:128], in_=sl["A"][0][:], identity=ident),
                             r=[sl["d"], d_cst], w=[dPS[pb]])
                        S.op("act", lambda e: e.activation(out=sl["BW"][0][:, 0:128], in_=PS[pb][:, 0:128], func=AF.Identity), r=[dPS[pb]], w=[sl["d"]])
                        S.op("dve", lambda e: e.tensor_tensor(out=sl["BW"][0][:, 128:256], in0=ident, in1=PS[pb][:, 0:128], op=ALU.subtract),
                             r=[dPS[pb], d_cst], w=[sl["d"]])
                    cur = 0
                    for i, (d_, t) in enumerate(grp):
                        sl = sol[i]
                        pb = i % 8
                        S.op("pe", lambda e: e.matmul(PS[pb][:, 128:256], lhsT=sl["A"][0][:], rhs=sl["BW"][0][:, 0:128], start=True, stop=True),
                             r=[sl["d"]], w=[dPS[pb]])
                        S.op("dve", lambda e: e.tensor_copy(out=sl["BW"][1][:, 0:128], in_=PS[pb][:, 128:256]), r=[dPS[pb]], w=[sl["d"]])
                        S.op("pool", lambda e: e.tensor_copy(out=sl["BW"][1][:, 128:256], in_=sl["BW"][0][:, 128:256]), r=[sl["d"]], w=[sl["d"]])
                    for i, (d_, t) in enumerate(grp):
                        sl = sol[i]
                        pb = i % 8
                        S.op("pe", lambda e: e.transpose(out=PS[pb][:, 0:128], in_=sl["BW"][1][:, 0:128], identity=ident),
                             r=[sl["d"], d_cst], w=[dPS[pb]])
                        S.op("act", lambda e: e.activation(out=sl["A"][1][:], in_=PS[pb][:, 0:128], func=AF.Identity), r=[dPS[pb]], w=[sl["d"]])
                    cur = 1
                    for lvl in range(1, 6):
                        nxt = 1 - cur
                        last_b = lvl >= 4
                        last_a = lvl >= 5
                        for i, (d_, t) in enumerate(grp):
                            sl = sol[i]
                            pb = i % 8
                            Ac, BWc, An, BWn = sl["A"][cur], sl["BW"][cur], sl["A"][nxt], sl["BW"][nxt]
                            if not last_b:
                                S.op("pe", lambda e: e.matmul(PS[pb][:, 0:256], lhsT=Ac[:], rhs=BWc[:], start=True, stop=True),
                                     r=[sl["d"]], w=[dPS[pb]])
                                S.op("act", lambda e: e.activation(out=BWn[:, 0:128], in_=PS[pb][:, 0:128], func=AF.Identity), r=[dPS[pb]], w=[sl["d"]])
                            else:
                                S.op("pe", lambda e: e.matmul(PS[pb][:, 128:256], lhsT=Ac[:], rhs=BWc[:, 128:256], start=True, stop=True),
                                     r=[sl["d"]], w=[dPS[pb]], signal=last_a)
                                if not last_a:
                                    S.op("pe", lambda e: e.matmul(PS[pb][:, 256:384], lhsT=BWc[:, 0:128], rhs=Ac[:], start=True, stop=True),
                                         r=[sl["d"]], w=[dPS[pb]])
                                    S.op("act", lambda e: e.activation(out=An[:], in_=PS[pb][:, 256:384], func=AF.Identity), r=[dPS[pb]], w=[sl["d"]])
                            S.op("dve", lambda e: e.tensor_tensor(out=BWn[:, 128:256], in0=BWc[:, 128:256], in1=PS[pb][:, 128:256], op=ALU.add),
                                 r=[dPS[pb], sl["d"]], w=[sl["d"]])
                        if not last_b:
                            for i, (d_, t) in enumerate(grp):
                                sl = sol[i]
                                pb = i % 8
                                An, BWn = sl["A"][nxt], sl["BW"][nxt]
                                S.op("pe", lambda e: e.transpose(out=PS[pb][:, 256:384], in_=BWn[:, 0:128], identity=ident),
                                     r=[sl["d"], d_cst], w=[dPS[pb]])
                                S.op("act", lambda e: e.activation(out=An[:], in_=PS[pb][:, 256:384], func=AF.Identity), r=[dPS[pb]], w=[sl["d"]])
                        cur = nxt
                    for i, (d_, t) in enumerate(grp):
                        hd, (Mg, Ms, Mi), db = pinfo(d_)
                        sl = sol[i]
                        W = sl["BW"][cur][:, 128:256]
                        vbk, d_vb = vb[i % 2]
                        pb = i % 8
                        S.op("pool", lambda e: e.tensor_tensor(out=vbk[:, 0:128], in0=vtok[:, t, :], in1=BETA[:, t, hd:hd + 1].broadcast_to([128, 128]), op=ALU.mult),
                             r=[d_kv, d_G], w=[d_vb])
                        S.op("pool", lambda e: e.tensor_tensor(out=vbk[:, 128:256], in0=ktok[:, t, :], in1=BEGC[:, t, hd:hd + 1].broadcast_to([128, 128]), op=ALU.mult),
                             r=[d_kv, d_G], w=[d_vb])
                        S.op("pool", lambda e: e.tensor_tensor(out=db["kdq"][:, t, 0:128], in0=ktok[:, t, :], in1=EKD[:, t, hd:hd + 1].broadcast_to([128, 128]), op=ALU.mult),
                             r=[d_kv, d_G], w=[db["d"]])
                        S.op("pe", lambda e: e.matmul(PS[pb][:, 0:256], lhsT=W, rhs=vbk[:], start=True, stop=True),
                             r=[sl["d"], d_vb], w=[dPS[pb]])
                        S.op("act", lambda e: e.activation(out=db["uw"][:, t, :], in_=PS[pb][:, 0:256], func=AF.Identity), r=[dPS[pb]], w=[db["d"]])
            S.barrier()
            arena.ptr = inner_mark
            S.mark("g%d h%d recur" % (gi, h))
            oacc = alloc(hs, "oacc", (128, mt, 128)); d_oacc = Dep()
            S.op("pool", lambda e: e.memset(oacc[:], 0.0), w=[d_oacc])
            junk = alloc(hs, "dnjunk", (128, 128)); d_junk = Dep()
            chains = []
            for si, (a, b) in enumerate(seqs):
                t0, t1 = a // 128, b // 128
                for d_ in range(2):
                    if d_ == 0:
                        chunks = [(t, hf) for t in range(t0, t1) for hf in (0, 1)]
                        if isS:
                            chunks = chunks[:9]
                    else:
                        chunks = [(t, hf) for t in range(t1 - 1, t0 - 1, -1) for hf in (1, 0)]
                    outs = [(t < mt and not (isS and t == 4 and hf == 1)) for (t, hf) in chunks]
                    nch = len(chunks)
                    ch = dict(St=[alloc(hs, "St%d_%d_%d" % (si, d_, k), (128, 128)) for k in range(2)],
                              MT=alloc(hs, "MT%d_%d" % (si, d_), (128, nch, 128)), CC=alloc(hs, "CC%d_%d" % (si, d_), (128, nch, 128), BF16),
                              NWQ=alloc(hs, "NWQ%d_%d" % (si, d_), (128, max(1, sum(outs)), 128), BF16),
                              Sb=[alloc(hs, "Sb%d_%d_%d" % (si, d_, k), (128, 128), BF16) for k in range(2)], dSb=[Dep(), Dep()],
                              tmp=alloc(hs, "rt%d_%d" % (si, d_), (128, 128)),
                              dS=[Dep(), Dep()], dpre=[Dep() for _ in range(nch)], dtmp=Dep(),
                              chunks=chunks, outs=outs, dir=d_, si=si, idx=len(chains))
                    if isS:
                        S.dma("sp", ch["St"][0][:], (SF0 if d_ == 0 else SB0)[h], w=[ch["dS"][0]])
                    else:
                        S.op("pool", lambda e: e.memset(ch["St"][0][:], 0.0), w=[ch["dS"][0]])
                    chains.append(ch)
            pctr = 0
            for ch in chains:
                d_ = ch["dir"]
                hd = d_ * 8 + h
                db = dirbuf[d_]
                oi = 0
                for ci, (t, hf) in enumerate(ch["chunks"]):
                    rows = slice(hf * 64, hf * 64 + 64)
                    pb = pctr % 4
                    pctr += 1
                    need_o = ch["outs"][ci]
                    nn = 256 if need_o else 128
                    S.op("pe", lambda e: e.matmul(PS[pb][:, 0:nn], lhsT=db["uw"][rows, t, 128:256], rhs=db["kdq"][rows, t, 0:nn], start=True, stop=True),
                         r=[db["d"]], w=[dPS[pb]], signal=False)
                    S.op("pe", lambda e: e.matmul(PS[pb][:, 256:384], lhsT=db["kdq"][rows, t, 0:128], rhs=db["uw"][rows, t, 0:128], start=True, stop=True),
                         r=[db["d"]], w=[dPS[pb]])
                    S.op("act", lambda e: e.activation(out=ch["MT"][:, ci, :], in_=PS[pb][:, 0:128], func=AF.Identity, scale=-1.0),
                         r=[dPS[pb]], w=[ch["dpre"][ci]])
                    S.op("dve", lambda e: e.tensor_copy(out=ch["CC"][:, ci, :], in_=PS[pb][:, 256:384]), r=[dPS[pb]], w=[ch["dpre"][ci]])
                    if need_o:
                        S.op("act", lambda e: e.activation(out=ch["NWQ"][:, oi, :], in_=PS[pb][:, 128:256], func=AF.Identity, scale=-1.0),
                             r=[dPS[pb]], w=[ch["dpre"][ci]])
                        oi += 1
            maxlen = max(len(c["chunks"]) for c in chains)
            for ch in chains:
                ch["oi"] = 0
            thunks = (interleave() if interleave is not None else None) or []
            nslots = sum(len(c["chunks"]) for c in chains)
            per = (len(thunks) + nslots - 1) // max(1, nslots)
            tpos = [0]

            def replay(n):
                for th in thunks[tpos[0]:tpos[0] + n]:
                    th()
                tpos[0] = min(len(thunks), tpos[0] + n)

            for step in range(maxlen):
                for ch in chains:
                    if step >= len(ch["chunks"]):
                        continue
                    replay(per)
                    t, hf = ch["chunks"][step]
                    d_ = ch["dir"]
                    hd = d_ * 8 + h
                    db = dirbuf[d_]
                    rows = slice(hf * 64, hf * 64 + 64)
                    Sc, Sn = ch["St"][step % 2], ch["St"][(step + 1) % 2]
                    dSc, dSn = ch["dS"][step % 2], ch["dS"][(step + 1) % 2]
                    pbs = 6
                    ts = slice(t * 128, (t + 1) * 128)
                    S.op("pe", lambda e: e.matmul(PS[pbs][:, 0:128], lhsT=identb[:], rhs=ch["CC"][:, step, :], start=True, stop=False),
                         r=[ch["dpre"][step], d_cst], w=[dPS[pbs]], signal=False)
                    S.op("pe", lambda e: e.matmul(PS[pbs][:, 0:128], lhsT=ch["MT"][:, step, :], rhs=Sc[:], start=False, stop=True),
                         r=[ch["dpre"][step], dSc], w=[dPS[pbs]])
                    S.op("dve", lambda e: e.scalar_tensor_tensor(out=Sn[:], in0=Sc[:], scalar=DL[hf][:, t, hd:hd + 1], in1=PS[pbs][:, 0:128],
                                                                 op0=ALU.mult, op1=ALU.add),
                         r=[dPS[pbs], dSc, d_G], w=[dSn])
                    if ch["outs"][step]:
                        pbo = 7
                        oi = ch["oi"]
                        ch["oi"] += 1
                        Sb, dSb = ch["Sb"][step % 2], ch["dSb"][step % 2]
                        S.op("act", lambda e: e.activation(out=Sb[:], in_=Sc[:], func=AF.Identity), r=[dSc], w=[dSb])
                        S.op("pe", lambda e: e.matmul(PS[pbo][:, 0:128], lhsT=yqb[:, ts], rhs=Sb[:], start=True, stop=True),
                             r=[d_yb, dSb], w=[dPS[pbo]], signal=False)
                        S.op("pe", lambda e: e.matmul(PS[pbo][:, 128:256], lhsT=ch["NWQ"][:, oi, :], rhs=Sb[:], start=True, stop=False),
                             r=[ch["dpre"][step], dSb], w=[dPS[pbo]], signal=False)
                        S.op("pe", lambda e: e.matmul(PS[pbo][:, 128:256], lhsT=db["kdq"][rows, t, 128:256], rhs=db["uw"][rows, t, 0:128], start=False, stop=True),
                             r=[db["d"]], w=[dPS[pbo]])
                        S.op("dve", lambda e: e.tensor_tensor(out=ch["tmp"][rows, :], in0=PS[pbo][rows, 128:256], in1=oacc[rows, t, :], op=ALU.add),
                             r=[dPS[pbo], d_oacc], w=[ch["dtmp"]])
                        S.op("dve", lambda e: e.scalar_tensor_tensor(out=oacc[rows, t, :], in0=PS[pbo][rows, 0:128], scalar=EGC[rows, t, hd:hd + 1],
                                                                     in1=ch["tmp"][rows, :], op0=ALU.mult, op1=ALU.add),
                             r=[dPS[pbo], d_G, ch["dtmp"]], w=[d_oacc])
            replay(len(thunks))
            if not isS:
                for ch in chains:
                    n = len(ch["chunks"])
                    ev = S.dma("sp", (NSF if ch["dir"] == 0 else NSB)[ch["si"], h], ch["St"][n % 2][:], r=[ch["dS"][n % 2]])
                    out_deps.append(ev)
            S.mark("g%d h%d dnout" % (gi, h))
            st = alloc(hs, "dst", (128, 2 * mt)); d_st = Dep()
            S.op("dve", lambda e: e.memset(st[:], 0.0), w=[d_st])
            for tq in range(mt):
                n = 64 if (isS and tq == 4) else 128
                S.op("act", lambda e: e.activation(out=junk[0:n, :], in_=oacc[0:n, tq, :], func=AF.Square,
                                                   accum_out=st[0:n, tq:tq + 1]), r=[d_oacc], w=[d_junk, d_st])
            rstd_from_ss(st, d_st, mt, 1.0 / 128)
            for tq in range(mt):
                n = 64 if (isS and tq == 4) else 128
                pb = 6 + tq % 2
                S.op("dve", lambda e: e.scalar_tensor_tensor(out=oacc[0:n, tq, :], in0=oacc[0:n, tq, :], scalar=st[0:n, mt + tq:mt + tq + 1],
                                                             in1=dngb[0:n, :], op0=ALU.mult, op1=ALU.mult),
                     r=[d_st, d_small], w=[d_oacc])
                S.op("pe", lambda e: e.transpose(out=PS[pb][:, 0:n], in_=oacc[0:n, tq, :], identity=cst[0:n, 0, 0:n]),
                     r=[d_oacc, d_cst], w=[dPS[pb]])
                S.op("dve", lambda e: e.tensor_tensor(out=oT[:, 8 + h, ocol0 + tq * 128:ocol0 + tq * 128 + n], in0=PS[pb][:, 0:n],
                                                      in1=gsT[:, tq * 128:tq * 128 + n], op=ALU.mult),
                     r=[dPS[pb], d_gs], w=[d_oT[8 + h]])

        try:
            with arena.scope() as ms:
                hTbox["hT"] = alloc(ms, "hT", (128, 16, 1024), BF16)
                mixer_group(0)
                if stop_after != "g0":
                    mixer_group(1)
                S.barrier()
        except _Stop:
            stop_after = "mixer"

        if stop_after in ("mixer", "attn", "g0"):
            for ev in out_deps:
                S._wait("sp", ev)
            print("ops", S.nops, "waits", S.nwaits, "sim", S.simulate()[:2])
            return nc

        S.mark("phase3")
        xmid = alloc(top, "xmid", (128, 8, 2048)); d_xm = [Dep() for _ in range(9)]
        gbc = alloc(top, "gbc", (128, 2, 2048)); d_gbc = Dep()

        def mod_rowbc(ph_bufs, vec_i, bank0=0):
            for q in range(4):
                mod_rowbc_blk(ph_bufs, vec_i, q, bank0)

        def mod_rowbc_blk(ph_bufs, vec_i, q, bank0=0):
            wstream, mbb, d_mbb, mrow, d_mrow = ph_bufs
            if True:
                blk = vec_i * 4 + q
                wb, d_w = wstream.get()
                for cvi, L in enumerate((Ls["L0"], Ls["L1"])):
                    mod_block((wb, d_w, mbb, d_mbb, mrow, d_mrow), blk, L, bank0 + cvi)
                    S.op("act", lambda e: e.activation(out=gbc[:, cvi, q * 512:(q + 1) * 512], in_=mrow[:], func=AF.Identity), r=[d_mrow], w=[d_gbc])

        with arena.scope() as ph:
            make_L(ph)
            wbufs = [(alloc(ph, "mw%d" % i, (128, 16, 512), BF16), Dep()) for i in range(2)]
            mbb = alloc(ph, "mbb", (128, 512)); d_mbb = Dep()
            mrow = alloc(ph, "mrow", (128, 512)); d_mrow = Dep()
            bufs = (WStream(S, wbufs, [MODW[b_] for b_ in range(8, 20)], 1), mbb, d_mbb, mrow, d_mrow)
            bufs[0].prefetch()
            mod_rowbc(bufs, 2)
            mod_featmajor(bufs, 3, b2v, 2, 3)
            mod_featmajor(bufs, 4, s2v, 2, 3)
            S.op("dve", lambda e: e.tensor_scalar(out=small[:, 96:128], in0=small[:, 96:128], scalar1=1.0, scalar2=None,
                                                  op0=ALU.add), r=[d_small], w=[d_small])
            S.op("dve", lambda e: e.tensor_tensor(out=s2v, in0=s2v, in1=small[:, 16:32].unsqueeze(2).broadcast_to([128, 16, 2]),
                                                  op=ALU.mult), r=[d_small], w=[d_small])
            S.barrier()

        S.mark("phase4")
        NTM = 9
        with arena.scope() as ph4:
          xh = alloc(ph4, "xh", (128, 2048))
          xm = lambda t: (xmid[:, t, :] if t < 8 else xh)
          with arena.scope() as ph:
            wob = [(alloc(ph, "wo%d" % i, (128, 16, 512), BF16), Dep()) for i in range(2)]
            xin = [(alloc(ph, "xin%d" % i, (128, 512)), Dep()) for i in range(3)]
            tmpm = [(alloc(ph, "tmpm%d" % i, (128, 512)), Dep()) for i in range(2)]
            ctr = 0
            wo_stream = WStream(S, wob, [WOUT[b_] for b_ in range(4)], 1)
            wo_stream.prefetch()
            for nb in range(4):
                wb, d_w = wo_stream.get()
                for t in range(NTM):
                    n = 64 if t == 8 else 128
                    cvi = 0 if t < 4 else 1
                    pb = ctr % 4
                    xi, d_xi = xin[ctr % 3]
                    tm, d_tm = tmpm[ctr % 2]
                    ctr += 1
                    src = XP[t * 128:t * 128 + n, nb * 512:(nb + 1) * 512] if t < 4 else XS[(t - 4) * 128:(t - 4) * 128 + n, nb * 512:(nb + 1) * 512]
                    S.dma("sp", xi[0:n, :], src, w=[d_xi])
                    for kc in range(16):
                        S.op("pe", lambda e: e.matmul(PS[pb][0:n, :], lhsT=oT[:, kc, t * 128:t * 128 + n], rhs=wb[:, kc, :],
                                                      start=(kc == 0), stop=(kc == 15)),
                             r=[d_oT[kc], d_w], w=[dPS[pb]], signal=(kc == 15))
                    S.op("dve", lambda e: e.tensor_tensor(out=tm[0:n, :], in0=PS[pb][0:n, :], in1=gbc[0:n, cvi, nb * 512:(nb + 1) * 512], op=ALU.mult),
                         r=[dPS[pb], d_gbc], w=[d_tm])
                    S.op("pool", lambda e: e.tensor_tensor(out=xm(t)[0:n, nb * 512:(nb + 1) * 512], in0=tm[0:n, :], in1=xi[0:n, :], op=ALU.add),
                         r=[d_tm, d_xi], w=[d_xm[t]])
            S.barrier()
          h2T = oT; d_h2 = Dep()
          with arena.scope() as ph:
            xns = [(alloc(ph, "n2xn%d" % i, (128, 2048)), Dep()) for i in range(2)]
            st = alloc(ph, "n2st", (128, 2 * NTM)); d_st = Dep()
            S.op("dve", lambda e: e.memset(st[:], 0.0), w=[d_st])
            make_L(ph)
            wbufs5 = [(alloc(ph, "mw5_%d" % i, (128, 16, 512), BF16), Dep()) for i in range(2)]
            mbb5 = alloc(ph, "mbb5", (128, 512)); d_mbb5 = Dep()
            mrow5 = alloc(ph, "mrow5", (128, 512)); d_mrow5 = Dep()
            ws5 = WStream(S, wbufs5, [MODW[b_] for b_ in range(20, 24)], 1)
            ws5.prefetch()
            for t in range(NTM):
                if t in (1, 3, 5, 7):
                    mod_rowbc_blk((ws5, mbb5, d_mbb5, mrow5, d_mrow5), 5, (t - 1) // 2, bank0=4)
                n = 64 if t == 8 else 128
                cvi = 0 if t < 4 else 1
                xn, d_xn = xns[t % 2]
                S.op("act", lambda e: e.activation(out=xn[0:n, :], in_=xm(t)[0:n, :], func=AF.Square, accum_out=st[0:n, 2 * t:2 * t + 1]),
                     r=[d_xm[t]], w=[d_xn, d_st])
                S.op("act", lambda e: e.activation(out=st[0:n, 2 * t + 1:2 * t + 2], in_=st[0:n, 2 * t:2 * t + 1], func=AF.Ln, bias=EPS, scale=1.0 / D),
                     r=[d_st], w=[d_st])
                S.op("act", lambda e: e.activation(out=st[0:n, 2 * t + 1:2 * t + 2], in_=st[0:n, 2 * t + 1:2 * t + 2], func=AF.Exp, scale=-0.5),
                     r=[d_st], w=[d_st])
                S.op("dve", lambda e: e.tensor_scalar(out=xn[0:n, :], in0=xm(t)[0:n, :], scalar1=st[0:n, 2 * t + 1:2 * t + 2], scalar2=None, op0=ALU.mult),
                     r=[d_xm[t], d_st], w=[d_xn])
                for g in range(4):
                    for j in range(4):
                        kc = g * 4 + j
                        S.op("pe", lambda e: e.transpose(out=PS[g][:, j * 128:j * 128 + n], in_=xn[0:n, kc * 128:(kc + 1) * 128], identity=cst[0:n, 0, 0:n]),
                             r=[d_xn, d_cst], w=[dPS[g]], signal=(j == 3))
                    for j in range(4):
                        kc = g * 4 + j
                        if j % 2 == 0:
                            S.op("act", lambda e: e.activation(out=h2T[:, kc, t * 128:t * 128 + n], in_=PS[g][:, j * 128:j * 128 + n], func=AF.Identity,
                                                               bias=b2v[:, kc, cvi:cvi + 1], scale=s2v[:, kc, cvi:cvi + 1]),
                                 r=[dPS[g], d_small], w=[d_h2])
                        else:
                            S.op("dve", lambda e: e.tensor_scalar(out=h2T[:, kc, t * 128:t * 128 + n], in0=PS[g][:, j * 128:j * 128 + n],
                                                                  scalar1=s2v[:, kc, cvi:cvi + 1], scalar2=b2v[:, kc, cvi:cvi + 1], op0=ALU.mult, op1=ALU.add),
                                 r=[dPS[g], d_small], w=[d_h2])
            S.barrier()

        S.mark("phase5")
        with arena.scope() as ph:
            convf = alloc(ph, "convf", (128, 88, 3)); d_cf = Dep()
            S.dma("sp", convf[:], CONVF, w=[d_cf])
            wub = [(alloc(ph, "wu%d" % i, (128, 16, 512), BF16), Dep()) for i in range(2)]
            wdb = [(alloc(ph, "wd%d" % i, (128, 4, 512), BF16), Dep()) for i in range(2)]
            aTg = alloc(ph, "aT", (128, 4, 1024), BF16); d_aT = [Dep() for _ in range(4)]
            ur = [(alloc(ph, "ur%d" % i, (128, 1032)), Dep()) for i in range(2)]
            yc = [[(alloc(ph, "yc%d_%d" % (i, k), (128, 1024)), Dep()) for k in range(2)] for i in range(2)]
            tmpc = alloc(ph, "tmpc", (128, 1024)); d_tc = Dep()
            tmpd = [(alloc(ph, "tmpd%d" % i, (128, 512)), Dep()) for i in range(2)]
            segs = [(0, 256), (256, 512), (512, 1024)]
            FP = [(0, 512), (512, 1024), (1024, 1025)]
            dctr = [0]
            wu_cur = [None]
            wu_stream = WStream(S, wub, [WUP[r] for r in range(22)], 1)
            wd_stream = WStream(S, wdb, [WDOWN[g_, n_] for g_ in range(11) for n_ in range(4)], 1)
            wu_stream.prefetch()

            def ffn_up(j):
                r_, cc = j // 2, j % 2
                if cc == 0:
                    wu_cur[0] = wu_stream.get()
                if j % 4 == 2:
                    wd_stream.prefetch(0)
                if j % 4 == 3:
                    wd_stream.prefetch(1)
                wu, d_wu = wu_cur[0]
                for gv in range(2):
                    co = gv * 256 + cc * 128
                    cidx = gv * 44 + j
                    u_, d_u = ur[gv]
                    y_, d_yc = yc[gv][j % 2]
                    for pi, (c0, c1) in enumerate(FP):
                        pb = pi
                        n = c1 - c0
                        for kc in range(16):
                            S.op("pe", lambda e: e.matmul(PS[pb][:, 0:n], lhsT=wu[:, kc, co:co + 128], rhs=h2T[:, kc, c0:c1],
                                                          start=(kc == 0), stop=(kc == 15)),
                                 r=[d_wu, d_h2], w=[dPS[pb]], signal=(kc == 15))
                        S.op("act", lambda e: e.activation(out=u_[:, c0:c1], in_=PS[pb][:, 0:n], func=AF.Identity), r=[dPS[pb]], w=[d_u])
                    cw = convf[:, cidx, :]
                    S.op("dve", lambda e: e.tensor_scalar(out=y_[:], in0=u_[:, 0:1024], scalar1=cw[:, 1:2], scalar2=None, op0=ALU.mult),
                         r=[d_u, d_cf], w=[d_yc])
                    S.op("act", lambda e: e.activation(out=tmpc[:], in_=u_[:, 1:1025], func=AF.Identity, scale=cw[:, 2:3]),
                         r=[d_u, d_cf], w=[d_tc])
                    for (a_, b_) in segs:
                        S.op("dve", lambda e: e.scalar_tensor_tensor(out=y_[:, a_ + 1:b_], in0=u_[:, a_:b_ - 1], scalar=cw[:, 0:1],
                                                                     in1=y_[:, a_ + 1:b_], op0=ALU.mult, op1=ALU.add),
                             r=[d_u, d_cf], w=[d_yc])
                    for (a_, b_) in segs:
                        b2 = b_ - 1 if b_ < 1024 else 1024
                        S.op("pool", lambda e: e.tensor_tensor(out=y_[:, a_:b2], in0=y_[:, a_:b2], in1=tmpc[:, a_:b2], op=ALU.add),
                             r=[d_tc], w=[d_yc])

            def ffn_fin(j):
                yg, d_g = yc[0][j % 2]
                yv_, d_v = yc[1][j % 2]
                S.op("act", lambda e: e.activation(out=yg[:], in_=yg[:], func=AF.Silu), r=[d_g], w=[d_g])
                S.op("dve", lambda e: e.tensor_tensor(out=aTg[:, j % 4, :], in0=yg[:], in1=yv_[:], op=ALU.mult),
                     r=[d_g, d_v], w=[d_aT[j % 4]])

            def ffn_down(grp):
                for nb in range(4):
                    wd, d_wd = wd_stream.get()
                    for t in range(8):
                        cvi = 0 if t < 4 else 1
                        pb = 3 + dctr[0] % 5
                        tm, d_tm = tmpd[dctr[0] % 2]
                        dctr[0] += 1
                        for jj in range(4):
                            S.op("pe", lambda e: e.matmul(PS[pb][:], lhsT=aTg[:, jj, t * 128:(t + 1) * 128], rhs=wd[:, jj, :],
                                                          start=(jj == 0), stop=(jj == 3)),
                                 r=[d_aT[jj], d_wd], w=[dPS[pb]], signal=(jj == 3))
                        S.op("dve", lambda e: e.tensor_tensor(out=tm[:], in0=PS[pb][:], in1=gbc[:, cvi, nb * 512:(nb + 1) * 512], op=ALU.mult),
                             r=[dPS[pb], d_gbc], w=[d_tm])
                        S.op("pool", lambda e: e.tensor_tensor(out=xmid[:, t, nb * 512:(nb + 1) * 512], in0=xmid[:, t, nb * 512:(nb + 1) * 512], in1=tm[:], op=ALU.add),
                             r=[d_tm], w=[d_xm[t]])

            for j in range(44):
                ffn_up(j)
                if j >= 1:
                    ffn_fin(j - 1)
                    if (j - 1) % 4 == 3:
                        ffn_down((j - 1) // 4)
            ffn_fin(43)
            ffn_down(10)
            S.barrier()

        S.mark("phase6")
        with arena.scope() as ph:
            fg = alloc(ph, "fg", (128, 2048)); d_fg = Dep()
            S.dma("sp", fg[:], FINALG.partition_broadcast(128), w=[d_fg])
            yo = [(alloc(ph, "yo%d" % i, (128, 2048)), Dep()) for i in range(2)]
            st = alloc(ph, "fst", (128, 16)); d_st = Dep()
            S.op("dve", lambda e: e.memset(st[:], 0.0), w=[d_st])
            for t in range(8):
                y_, d_yo = yo[t % 2]
                S.op("act", lambda e: e.activation(out=y_[:], in_=xmid[:, t, :], func=AF.Square, accum_out=st[:, 2 * t:2 * t + 1]),
                     r=[d_xm[t]], w=[d_yo, d_st])
                S.op("act", lambda e: e.activation(out=st[:, 2 * t + 1:2 * t + 2], in_=st[:, 2 * t:2 * t + 1], func=AF.Ln, bias=EPS, scale=1.0 / D),
                     r=[d_st], w=[d_st])
                S.op("act", lambda e: e.activation(out=st[:, 2 * t + 1:2 * t + 2], in_=st[:, 2 * t + 1:2 * t + 2], func=AF.Exp, scale=-0.5),
                     r=[d_st], w=[d_st])
                S.op("dve", lambda e: e.scalar_tensor_tensor(out=y_[:], in0=xmid[:, t, :], scalar=st[:, 2 * t + 1:2 * t + 2], in1=fg[:],
                                                             op0=ALU.mult, op1=ALU.mult), r=[d_xm[t], d_st, d_fg], w=[d_yo])
                dst = YP[t * 128:(t + 1) * 128, :] if t < 4 else YS[(t - 4) * 128:(t - 3) * 128, :]
                ev = S.dma("sp", dst, y_[:], r=[d_yo])
                out_deps.append(ev)
        for ev in out_deps:
            S._wait("sp", ev)
        S.mark("end")
        print("ops", S.nops, "waits", S.nwaits, "sim", S.simulate()[:2])
        if os.environ.get("KMARKS"):
            import json
            json.dump(S.marks, open(os.environ["KMARKS"], "w"))
    return nc


_NC_CACHE = {}


def kernel(**inputs):
    maps = _prep(inputs)
    stop = os.environ.get("KSTOP")
    if stop not in _NC_CACHE:
        _NC_CACHE[stop] = build(stop)
    nc = _NC_CACHE[stop]
    ncr = int(os.environ.get("KCORES", str(NCORES)))
    res = run_bass_kernel_spmd(nc, maps[:ncr], core_ids=list(range(ncr)))
    R = list(res.results) + [res.results[0]] * (NCORES - ncr)
    y_prompt = np.zeros((16, 256, 2048), np.float32)
    y_sample = np.zeros((4, 1024, 2048), np.float32)
    nk = np.zeros((16, 1, 256, 8, 128), np.float32)
    nv = np.zeros((16, 1, 256, 8, 128), np.float32)
    nsf = np.zeros((16, 1, 8, 128, 128), np.float32)
    nsb = np.zeros((16, 1, 8, 128, 128), np.float32)
    for c in range(NCORES):
        b, par = c // 2, c % 2
        r = R[c]
        yp = r["YP"].reshape(2, 256, 2048)
        ys = r["YS"]
        k = r["NK"].reshape(2, 256, 8, 128)
        v = r["NV"].reshape(2, 256, 8, 128)
        f, bw = r["NSF"], r["NSB"]
        if par:
            yp, k, v = yp[:, ::-1], k[:, ::-1], v[:, ::-1]
            ys = ys[::-1]
            f, bw = bw, f
            y_sample[b, 512:1024] = ys
        else:
            y_sample[b, 0:512] = ys
        y_prompt[2 * c:2 * c + 2] = yp
        nk[2 * c:2 * c + 2, 0] = k
        nv[2 * c:2 * c + 2, 0] = v
        nsf[2 * c:2 * c + 2, 0] = f
        nsb[2 * c:2 * c + 2, 0] = bw
    return (y_prompt, y_sample, nk, nv, nsf, nsb)
```

```python
import os
import math
import numpy as np
from contextlib import ExitStack
import concourse.bass as bass
import concourse.mybir as mybir
from concourse.bass_utils import run_bass_kernel_spmd

F32 = mybir.dt.float32
BF16 = mybir.dt.bfloat16
ALU = mybir.AluOpType
AF = mybir.ActivationFunctionType

D = 2048
NCORES = 8
EPS = 1e-6
DFF = 5632
LAM_INIT = 0.8 - 0.6 * math.exp(0.0)


class _RecEngine:
    def __getattr__(self, name):
        return lambda *a, **k: (name, a, k)


class Dep:
    __slots__ = ("w", "r", "excl")

    def __init__(self, excl=False):
        self.w = None
        self.r = {}
        self.excl = excl


class Sched:
    NDS = 24

    def __init__(self, nc, stack):
        self.nc = nc
        self.engs = {"pe": nc.tensor, "act": nc.scalar, "dve": nc.vector, "pool": nc.gpsimd, "sp": nc.sync}
        self.sem = {k: stack.enter_context(nc.semaphore("s_" + k)) for k in self.engs}
        self.cnt = {k: 0 for k in self.engs}
        self.seen = {k: {} for k in self.engs}
        self.dsem = [stack.enter_context(nc.semaphore("d%d" % i)) for i in range(self.NDS)]
        self.dval = [0] * self.NDS
        self.dnext = 0
        self.dnext_sw = 0
        self.nops = {k: 0 for k in self.engs}
        self.nwaits = 0
        self.trace = {k: [] for k in self.engs}
        self.marks = []
        self.rec = None

    def _wait(self, eng, ev):
        if ev is None:
            return
        key, sem, val = ev
        if self.seen[eng].get(key, 0) >= val:
            return
        if key == eng and val > self.cnt[eng]:
            return
        self.engs[eng].wait_ge(sem, val)
        self.trace[eng].append(("wait", key, val))
        self.nwaits += 1
        self.seen[eng][key] = val

    def _deps(self, eng, r, w):
        for d in r:
            self._wait(eng, d.w)
        for d in w:
            self._wait(eng, d.w)
            for ev in list(d.r.values()):
                self._wait(eng, ev)

    def _record(self, ev, r, w):
        for d in r:
            d.r[ev[0]] = ev
        for d in w:
            d.w = ev
            d.r = {}

    def op(self, eng, fn, r=(), w=(), signal=True):
        if self.rec is not None:
            name, a, k = fn(_RecEngine())
            self.rec.append(lambda: self.op(eng, lambda e: getattr(e, name)(*a, **k), r, w, signal))
            return None
        if any(d.excl for d in r):
            w = list(w) + [d for d in r if d.excl]
            r = [d for d in r if not d.excl]
        self._deps(eng, r, w)
        ins = fn(self.engs[eng])
        self.nops[eng] += 1
        if signal:
            self.cnt[eng] += 1
            ins.then_inc(self.sem[eng], 1)
            self.trace[eng].append(("inc", eng, 1))
            ev = (eng, self.sem[eng], self.cnt[eng])
        else:
            ev = (eng, self.sem[eng], self.cnt[eng] + 1)
        self._record(ev, r, w)
        return ins

    def dma(self, q, out, in_, r=(), w=(), evlist=None):
        if self.rec is not None:
            self.rec.append(lambda: self.dma(q, out, in_, r, w, evlist))
            return None
        half = self.NDS // 2
        if q == "pool":
            i = half + self.dnext_sw
            self.dnext_sw = (self.dnext_sw + 1) % half
        else:
            i = self.dnext
            self.dnext = (i + 1) % half
        key = "d%d" % i
        if self.dval[i] > 0:
            self._wait(q, (key, self.dsem[i], self.dval[i]))
        self._deps(q, r, w)
        ins = self.engs[q].dma_start(out=out, in_=in_)
        self.nops[q] += 1
        self.dval[i] += 16
        ins.then_inc(self.dsem[i], 16)
        self.trace[q].append(("inc", key, 16))
        ev = (key, self.dsem[i], self.dval[i])
        self._record(ev, r, w)
        if evlist is not None:
            evlist.append(ev)
        return ev

    def mark(self, label):
        if self.rec is not None:
            self.rec.append(lambda: self.mark(label))
            return
        self.marks.append((label, dict(self.nops)))

    def simulate(self):
        val = {}
        pc = {k: 0 for k in self.engs}
        progress = True
        while progress:
            progress = False
            for k in self.engs:
                tr = self.trace[k]
                while pc[k] < len(tr):
                    kind, key, v = tr[pc[k]]
                    if kind == "wait":
                        if val.get(key, 0) >= v:
                            pc[k] += 1
                            progress = True
                        else:
                            break
                    else:
                        val[key] = val.get(key, 0) + v
                        pc[k] += 1
                        progress = True
        stuck = {k: (pc[k], len(self.trace[k]), self.trace[k][pc[k]] if pc[k] < len(self.trace[k]) else None) for k in self.engs}
        ok = all(pc[k] == len(self.trace[k]) for k in self.engs)
        return ok, stuck, val

    def barrier(self):
        if self.rec is not None:
            self.rec.append(self.barrier)
            return
        self._barrier()

    def _barrier(self):
        evs = [(k, self.sem[k], self.cnt[k]) for k in self.engs if self.cnt[k] > 0]
        evs += [("d%d" % i, self.dsem[i], self.dval[i]) for i in range(self.NDS) if self.dval[i] > 0]
        for eng in self.engs:
            for ev in evs:
                self._wait(eng, ev)


class _Stop(Exception):
    pass


class WStream:
    def __init__(self, S, bufs, srcs, depth):
        self.S, self.bufs, self.srcs, self.depth = S, bufs, srcs, min(depth, len(bufs) - 1)
        self.i_issue = 0
        self.i_use = 0

    def prefetch(self, ahead=None):
        ahead = self.depth if ahead is None else min(ahead, len(self.bufs) - 1)
        while self.i_issue < min(len(self.srcs), self.i_use + ahead + 1):
            buf, d = self.bufs[self.i_issue % len(self.bufs)]
            self.S.dma("pool", buf[:], self.srcs[self.i_issue], w=[d])
            self.i_issue += 1

    def get(self):
        self.prefetch()
        buf = self.bufs[self.i_use % len(self.bufs)]
        self.i_use += 1
        return buf


class Arena:
    WORDS = 53200

    def __init__(self, nc, stack):
        self.t = stack.enter_context(nc.sbuf_tensor("arena", [128, self.WORDS], F32))
        self.ptr = 0
        self.peak = 0
        self.norelease = False

    def alloc(self, shape, dt=F32):
        n = 1
        for x in shape[1:]:
            n *= int(x)
        words = n if dt == F32 else (n + 1) // 2
        words = (words + 7) // 8 * 8
        off = self.ptr
        self.ptr += words
        self.peak = max(self.peak, self.ptr)
        assert self.ptr <= self.WORDS, ("SBUF arena overflow", self.ptr)
        ap = self.t[:, off:off + words]
        if dt != F32:
            ap = ap.bitcast(dt)
        ap = ap[:, 0:n]
        if len(shape) == 3:
            ap = ap.rearrange("p (a b) -> p a b", b=int(shape[2]))
        elif len(shape) == 4:
            ap = ap.rearrange("p (a b c) -> p a b c", b=int(shape[2]), c=int(shape[3]))
        return ap

    def scope(self):
        arena = self

        class _Scope:
            def __enter__(self_):
                self_.mark = arena.ptr
                return self_

            def __exit__(self_, *a):
                if not arena.norelease:
                    arena.ptr = self_.mark
                return False
        return _Scope()


def _tile_w(w, ncol_blk):
    K, N = w.shape
    return np.ascontiguousarray(w.reshape(K // 128, 128, N // ncol_blk, ncol_blk).transpose(2, 1, 0, 3))


def _consts():
    i = np.arange(128)
    blk = (i[:, None] // 64) == (i[None, :] // 64)
    c = {}
    c["ident"] = np.eye(128)
    c["ones"] = np.ones((128, 128))
    c["ublk"] = blk & (i[:, None] <= i[None, :])
    c["lblk"] = blk & (i[:, None] >= i[None, :])
    c["slblk"] = blk & (i[:, None] > i[None, :])
    c["sublk"] = blk & (i[:, None] < i[None, :])
    c["eblk"] = blk
    c["e0"] = np.broadcast_to((i[:, None] < 64), (128, 128))
    c["e1"] = np.broadcast_to((i[:, None] >= 64), (128, 128))
    d = i % 64
    ii = d % 32
    partner = np.where(ii < 16, i + 16, i - 16)
    prope = np.zeros((128, 128))
    prope[partner, i] = 1.0
    c["prope"] = prope
    names = ["ident", "ones", "ublk", "lblk", "slblk", "sublk", "eblk", "e0", "e1", "prope"]
    return np.ascontiguousarray(np.stack([c[n].astype(np.float32) for n in names], axis=1)), names


def _rope_tables(flip):
    t = np.arange(1024)
    if flip:
        t = t[::-1]
    rows = (t // 64).astype(np.float64)
    cols = (t % 64).astype(np.float64)
    inv = 10000.0 ** (-np.arange(0, 32, 2, dtype=np.float64) / 32.0)
    p = np.arange(128)
    d = p % 64
    half = d // 32
    ii = d % 32
    f = ii % 16
    pos = np.where(half[:, None] == 0, rows[None, :], cols[None, :])
    ang = pos * inv[f][:, None]
    cos = np.cos(ang)
    sin = np.sin(ang) * np.where(ii < 16, -1.0, 1.0)[:, None]
    return np.ascontiguousarray(np.stack([cos, sin], axis=1).astype(np.float32))


def _prep(inp):
    f32 = lambda a: np.ascontiguousarray(np.asarray(a, dtype=np.float32))
    w_in = f32(inp["w_in"])[0]
    heads = []
    for h in range(8):
        cols = np.concatenate([np.arange(o + h * 128, o + (h + 1) * 128)
                               for o in (0, 1024, 2048, 3072, 4096, 5120, 6144)])
        heads.append(_tile_w(w_in[:, cols], 128))
    WIN = np.ascontiguousarray(np.stack(heads, 0))
    MODW = _tile_w(f32(inp["mod_w"])[0], 512)
    MODB = f32(inp["mod_b"]).reshape(24, 512)
    WOUT = _tile_w(f32(inp["w_out"])[0], 512)
    w_up = f32(inp["w_up"])[0]
    upcols = np.concatenate([np.concatenate([np.arange(2 * r * 128, (2 * r + 2) * 128),
                                             np.arange(DFF + 2 * r * 128, DFF + (2 * r + 2) * 128)])
                             for r in range(22)])
    WUP = _tile_w(w_up[:, upcols], 512)
    WDOWN = np.ascontiguousarray(f32(inp["w_down"])[0].reshape(11, 4, 128, 4, 512).transpose(0, 3, 2, 1, 4))
    fm = lambda v: np.ascontiguousarray(f32(v).reshape(16, 128).T)
    GMIX, GFFN = fm(inp["norm_mix_g"]), fm(inp["norm_ffn_g"])
    FINALG = f32(inp["final_g"]).reshape(1, 2048)
    LAMV = np.concatenate([f32(inp[k]).reshape(-1) for k in ("lambda_q1", "lambda_k1", "lambda_q2", "lambda_k2")]).reshape(1, 256)
    SUBG = f32(inp["subln_g"]).reshape(1, 128)
    DNG = f32(inp["dn_norm_g"]).reshape(1, 128)
    cq = f32(inp["conv_qkv_w"])[0].reshape(3, 24, 128).transpose(2, 1, 0)
    cf = f32(inp["conv_ffn_w"])[0].reshape(3, 88, 128).transpose(2, 1, 0)
    wg = w_in[:, 7168:7200]
    alog = f32(inp["a_log"])[0].reshape(16)
    dtb = f32(inp["dt_bias"])[0].reshape(16)
    consts, _ = _consts()
    xp, xs = f32(inp["x_prompt"]), f32(inp["x_sample"])
    ck, cvv = f32(inp["cache_k"]), f32(inp["cache_v"])
    sf, sbw = f32(inp["state_fwd"]), f32(inp["state_bwd"])
    cvec, cctx = f32(inp["c"]), f32(inp["c_ctx"])
    swap = np.concatenate([np.arange(8, 16), np.arange(0, 8)])
    maps = []
    shared = {}
    for par in (0, 1):
        wgp = wg if par == 0 else wg[:, np.concatenate([swap, 16 + swap])]
        shared[par] = dict(
            WG=np.ascontiguousarray(wgp.reshape(16, 128, 32).transpose(1, 0, 2)),
            ALOG=np.ascontiguousarray((alog if par == 0 else alog[swap]).reshape(1, 16)),
            DTB=np.ascontiguousarray((dtb if par == 0 else dtb[swap]).reshape(1, 16)),
            CONVQ=np.ascontiguousarray(cq if par == 0 else cq[:, :, ::-1]),
            CONVF=np.ascontiguousarray(cf if par == 0 else cf[:, :, ::-1]),
            ROPE=_rope_tables(par == 1),
        )
    for c in range(NCORES):
        b, par = c // 2, c % 2
        fl = (lambda a, ax: a[(slice(None),) * ax + (slice(None, None, -1),)]) if par else (lambda a, ax: a)
        m = dict(
            XP=np.ascontiguousarray(fl(xp[2 * c:2 * c + 2], 1)).reshape(512, 2048),
            XS=np.ascontiguousarray(fl(xs[b], 0)),
            CK=np.ascontiguousarray(ck[b, 0].transpose(1, 0, 2)),
            CV=np.ascontiguousarray(cvv[b, 0].transpose(1, 0, 2)),
            SF0=np.ascontiguousarray((sbw if par else sf)[b, 0]),
            SB0=np.ascontiguousarray((sf if par else sbw)[b, 0]),
            CVEC=np.ascontiguousarray(np.stack([cctx, cvec[b]], 0).reshape(2, 16, 128).transpose(2, 1, 0)),
            MODW=MODW, MODB=MODB, WIN=WIN, WOUT=WOUT, WUP=WUP, WDOWN=WDOWN, GMIX=GMIX, GFFN=GFFN, FINALG=FINALG,
            LAMV=LAMV, SUBG=SUBG, DNG=DNG, CONSTS=consts,
        )
        m.update(shared[par])
        maps.append(m)
    return maps


def build(stop_after=None):
    nc = bass.Bass("TRN2", target_bir_lowering=False)
    di = lambda n, s: nc.dram_tensor(n, list(s), F32, kind="ExternalInput").ap()
    do = lambda n, s: nc.dram_tensor(n, list(s), F32, kind="ExternalOutput").ap()
    XP, XS = di("XP", (512, 2048)), di("XS", (1024, 2048))
    CK, CV = di("CK", (8, 256, 128)), di("CV", (8, 256, 128))
    SF0, SB0 = di("SF0", (8, 128, 128)), di("SB0", (8, 128, 128))
    CVEC = di("CVEC", (128, 16, 2))
    MODW, MODB = di("MODW", (24, 128, 16, 512)), di("MODB", (24, 512))
    WIN = di("WIN", (8, 7, 128, 16, 128))
    WOUT, WUP, WDOWN = di("WOUT", (4, 128, 16, 512)), di("WUP", (22, 128, 16, 512)), di("WDOWN", (11, 4, 128, 4, 512))
    GMIX, GFFN, FINALG = di("GMIX", (128, 16)), di("GFFN", (128, 16)), di("FINALG", (1, 2048))
    LAMV, SUBG, DNG = di("LAMV", (1, 256)), di("SUBG", (1, 128)), di("DNG", (1, 128))
    CONSTS = di("CONSTS", (128, 10, 128))
    WG, ALOG, DTB = di("WG", (128, 16, 32)), di("ALOG", (1, 16)), di("DTB", (1, 16))
    CONVQ, CONVF, ROPE = di("CONVQ", (128, 24, 3)), di("CONVF", (128, 88, 3)), di("ROPE", (128, 2, 1024))
    YP, YS = do("YP", (512, 2048)), do("YS", (512, 2048))
    NK, NV = do("NK", (512, 8, 128)), do("NV", (512, 8, 128))
    NSF, NSB = do("NSF", (2, 8, 128, 128)), do("NSB", (2, 8, 128, 128))

    with ExitStack() as top:
        S = Sched(nc, top)
        out_deps = []

        arena = Arena(nc, top)

        def alloc(stack, name, shape, dt=F32):
            return arena.alloc(shape, dt)

        PS = [top.enter_context(nc.psum_tensor("ps%d" % i, [128, 512], F32)) for i in range(8)]
        dPS = [Dep(excl=True) for _ in range(8)]

        cst = alloc(top, "cst", (128, 10, 128)); d_cst = Dep()
        S.dma("sp", cst[:], CONSTS, w=[d_cst])
        ident, ones, ublk, lblk, slblk, sublk, eblk, e0, e1 = [cst[:, i, :] for i in range(9)]
        propeb = alloc(top, "propeb", (128, 128), BF16)
        identb = alloc(top, "identb", (128, 128), BF16)
        S.op("dve", lambda e: e.tensor_copy(out=identb[:], in_=cst[:, 0, :]), r=[d_cst], w=[d_cst])
        S.op("dve", lambda e: e.tensor_copy(out=propeb[:], in_=cst[:, 9, :]), r=[d_cst], w=[d_cst])
        small = alloc(top, "small", (128, 1024)); d_small = Dep()
        S.dma("sp", small[:, 0:16], GMIX, w=[d_small])
        S.dma("sp", small[:, 16:32], GFFN, w=[d_small])
        S.dma("sp", small[:, 160:176], ALOG.partition_broadcast(128), w=[d_small])
        S.dma("sp", small[:, 176:192], DTB.partition_broadcast(128), w=[d_small])
        S.dma("sp", small[:, 192:448], LAMV.partition_broadcast(128), w=[d_small])
        S.dma("sp", small[:, 448:576], SUBG.partition_broadcast(128), w=[d_small])
        S.dma("sp", small[:, 576:704], DNG.partition_broadcast(128), w=[d_small])
        s1v = small[:, 32:64].rearrange("p (k c) -> p k c", c=2)
        b1v = small[:, 64:96].rearrange("p (k c) -> p k c", c=2)
        s2v = small[:, 96:128].rearrange("p (k c) -> p k c", c=2)
        b2v = small[:, 128:160].rearrange("p (k c) -> p k c", c=2)
        negA = small[:, 160:176]
        dtbb = small[:, 176:192]
        subgs = small[:, 448:576]
        dngb = small[:, 576:704]
        neglam = small[:, 704:705]
        S.op("act", lambda e: e.activation(out=negA, in_=negA, func=AF.Exp), r=[d_small], w=[d_small])
        S.op("dve", lambda e: e.tensor_scalar(out=negA, in0=negA, scalar1=-1.0, scalar2=None, op0=ALU.mult),
             r=[d_small], w=[d_small])
        S.op("dve", lambda e: e.tensor_scalar(out=subgs, in0=subgs, scalar1=1.0 - LAM_INIT, scalar2=None, op0=ALU.mult),
             r=[d_small], w=[d_small])
        S.op("dve", lambda e: e.tensor_tensor(out=small[:, 192:256], in0=small[:, 192:256], in1=small[:, 256:320], op=ALU.mult),
             r=[d_small], w=[d_small])
        S.op("dve", lambda e: e.tensor_tensor(out=small[:, 320:384], in0=small[:, 320:384], in1=small[:, 384:448], op=ALU.mult),
             r=[d_small], w=[d_small])
        S.op("dve", lambda e: e.reduce_sum(out=small[:, 705:706], in_=small[:, 192:256], axis=mybir.AxisListType.X),
             r=[d_small], w=[d_small])
        S.op("dve", lambda e: e.reduce_sum(out=small[:, 706:707], in_=small[:, 320:384], axis=mybir.AxisListType.X),
             r=[d_small], w=[d_small])
        S.op("act", lambda e: e.activation(out=small[:, 705:707], in_=small[:, 705:707], func=AF.Exp), r=[d_small], w=[d_small])
        S.op("dve", lambda e: e.tensor_tensor(out=neglam, in0=small[:, 706:707], in1=small[:, 705:706], op=ALU.subtract),
             r=[d_small], w=[d_small])
        S.op("dve", lambda e: e.tensor_scalar(out=neglam, in0=neglam, scalar1=-LAM_INIT, scalar2=None, op0=ALU.add),
             r=[d_small], w=[d_small])

        convq = alloc(top, "convq", (128, 24, 3)); d_convq = Dep()
        S.dma("sp", convq[:], CONVQ, w=[d_convq])
        wgb = alloc(top, "wgb", (128, 16, 32), BF16); d_wgb = Dep()
        S.dma("pool", wgb[:], WG, w=[d_wgb])

        cvs = alloc(top, "cvs", (128, 16, 2)); d_cvs = Dep()
        S.dma("sp", cvs[:], CVEC, w=[d_cvs])
        S.op("act", lambda e: e.activation(out=cvs[:], in_=cvs[:], func=AF.Silu), r=[d_cvs], w=[d_cvs])
        d_Lp = Dep()
        Ls = {}

        def make_L(scope):
            Lp = alloc(scope, "Lp", (128, 16, 128), BF16)
            L0 = alloc(scope, "L0", (128, 16, 128), BF16)
            L1 = alloc(scope, "L1", (128, 16, 128), BF16)
            for (dst, c0, c1, j) in ((Lp, 0, 64, 0), (Lp, 64, 128, 1), (L0, 0, 128, 0), (L1, 0, 128, 1)):
                S.op("dve", lambda e: e.tensor_copy(out=dst[:, :, c0:c1],
                                                    in_=cvs[:, :, j:j + 1].broadcast_to([128, 16, c1 - c0])),
                     r=[d_cvs], w=[d_Lp])
            Ls["Lp"], Ls["L0"], Ls["L1"] = Lp, L0, L1

        oT = alloc(top, "oT", (128, 16, 1088), BF16)
        d_oT = [Dep() for _ in range(16)]

        def mod_block(stk_bufs, blk, lhs, psum_i):
            wbuf, d_w, mbb, d_mbb, mrow, d_mrow = stk_bufs
            S.dma("sp", mbb[:], MODB[blk:blk + 1, :].partition_broadcast(128), w=[d_mbb])
            for kc in range(16):
                S.op("pe", lambda e: e.matmul(PS[psum_i][:], lhsT=lhs[:, kc, :], rhs=wbuf[:, kc, :],
                                              start=(kc == 0), stop=(kc == 15)),
                     r=[d_w, d_Lp], w=[dPS[psum_i]], signal=(kc == 15))
            S.op("dve", lambda e: e.tensor_tensor(out=mrow[:], in0=PS[psum_i][:], in1=mbb[:], op=ALU.add),
                 r=[dPS[psum_i], d_mbb], w=[d_mrow])

        def mod_featmajor(stk_bufs, vec_i, dstv, psA, psB):
            wstream, mbb, d_mbb, mrow, d_mrow = stk_bufs
            for q in range(4):
                blk = vec_i * 4 + q
                wb, d_w = wstream.get()
                mod_block((wb, d_w, mbb, d_mbb, mrow, d_mrow), blk, Ls["Lp"], psA)
                for j in range(4):
                    S.op("pe", lambda e: e.transpose(out=PS[psB][:, j * 128:(j + 1) * 128],
                                                     in_=mrow[:, j * 128:(j + 1) * 128], identity=ident),
                         r=[d_mrow, d_cst], w=[dPS[psB]], signal=(j == 3))
                for j in range(4):
                    kc = q * 4 + j
                    S.op("dve", lambda e: e.tensor_copy(out=dstv[:, kc, :], in_=PS[psB][:, j * 128:(j + 1) * 128:64]),
                         r=[dPS[psB]], w=[d_small])

        with arena.scope() as ph:
            make_L(ph)
            wbufs = [(alloc(ph, "mw%d" % i, (128, 16, 512), BF16), Dep()) for i in range(2)]
            mbb = alloc(ph, "mbb", (128, 512)); d_mbb = Dep()
            mrow = alloc(ph, "mrow", (128, 512)); d_mrow = Dep()
            bufs = (WStream(S, wbufs, [MODW[b_] for b_ in range(0, 8)], 1), mbb, d_mbb, mrow, d_mrow)
            bufs[0].prefetch()
            mod_featmajor(bufs, 0, b1v, 0, 1)
            mod_featmajor(bufs, 1, s1v, 0, 1)
            S.op("dve", lambda e: e.tensor_scalar(out=small[:, 32:64], in0=small[:, 32:64], scalar1=1.0, scalar2=None,
                                                  op0=ALU.add), r=[d_small], w=[d_small])
            S.op("dve", lambda e: e.tensor_tensor(out=s1v, in0=s1v, in1=small[:, 0:16].unsqueeze(2).broadcast_to([128, 16, 2]),
                                                  op=ALU.mult), r=[d_small], w=[d_small])
            S.barrier()

        if stop_after == "p0":
            print("ops", S.nops, "waits", S.nwaits, "sim", S.simulate()[:2])
            return nc
        NHEADS = int(os.environ.get("KHEADS", "8"))
        d_hT = Dep()
        hTbox = {}

        def rstd_from_ss(st, d_st, n, scale):
            S.op("act", lambda e: e.activation(out=st[:, n:2 * n], in_=st[:, 0:n], func=AF.Ln, bias=EPS, scale=scale),
                 r=[d_st], w=[d_st])
            S.op("act", lambda e: e.activation(out=st[:, n:2 * n], in_=st[:, n:2 * n], func=AF.Exp, scale=-0.5),
                 r=[d_st], w=[d_st])

        def norm_to_featmajor(stack, src_rows, ntile, dst, d_dst, sv, bv, cvi, tag):
            xts = [(alloc(stack, "%sxt%d" % (tag, i), (128, 2048)), Dep()) for i in range(2)]
            xns = [(alloc(stack, "%sxn%d" % (tag, i), (128, 2048)), Dep()) for i in range(2)]
            st = alloc(stack, tag + "st", (128, 2 * ntile)); d_st = Dep()
            S.op("dve", lambda e: e.memset(st[:], 0.0), w=[d_st])
            for t in range(ntile):
                xt, d_xt = xts[t % 2]
                xn, d_xn = xns[t % 2]
                S.dma("sp", xt[:], src_rows(t), w=[d_xt])
                S.op("act", lambda e: e.activation(out=xn[:], in_=xt[:], func=AF.Square, accum_out=st[:, 2 * t:2 * t + 1]),
                     r=[d_xt], w=[d_xn, d_st])
                S.op("act", lambda e: e.activation(out=st[:, 2 * t + 1:2 * t + 2], in_=st[:, 2 * t:2 * t + 1], func=AF.Ln,
                                                   bias=EPS, scale=1.0 / D), r=[d_st], w=[d_st])
                S.op("act", lambda e: e.activation(out=st[:, 2 * t + 1:2 * t + 2], in_=st[:, 2 * t + 1:2 * t + 2],
                                                   func=AF.Exp, scale=-0.5), r=[d_st], w=[d_st])
                S.op("dve", lambda e: e.tensor_scalar(out=xn[:], in0=xt[:], scalar1=st[:, 2 * t + 1:2 * t + 2], scalar2=None,
                                                      op0=ALU.mult), r=[d_xt, d_st], w=[d_xn])
                for g in range(4):
                    for j in range(4):
                        kc = g * 4 + j
                        S.op("pe", lambda e: e.transpose(out=PS[g][:, j * 128:(j + 1) * 128],
                                                         in_=xn[:, kc * 128:(kc + 1) * 128], identity=ident),
                             r=[d_xn, d_cst], w=[dPS[g]], signal=(j == 3))
                    for j in range(4):
                        kc = g * 4 + j
                        if g % 2 == 0:
                            S.op("act", lambda e: e.activation(out=dst[:, kc, t * 128:(t + 1) * 128],
                                                               in_=PS[g][:, j * 128:(j + 1) * 128], func=AF.Identity,
                                                               bias=bv[:, kc, cvi:cvi + 1], scale=sv[:, kc, cvi:cvi + 1]),
                                 r=[dPS[g], d_small], w=[d_dst])
                        else:
                            S.op("dve", lambda e: e.tensor_scalar(out=dst[:, kc, t * 128:(t + 1) * 128],
                                                                  in0=PS[g][:, j * 128:(j + 1) * 128],
                                                                  scalar1=sv[:, kc, cvi:cvi + 1], scalar2=bv[:, kc, cvi:cvi + 1],
                                                                  op0=ALU.mult, op1=ALU.add),
                                 r=[dPS[g], d_small], w=[d_dst])

        def passes(n):
            return [(a, min(a + 512, n)) for a in range(0, n, 512)]

        def proj(wt, d_wt, c0, c1, psum_i, M=128, mo=0):
            for kc in range(16):
                S.op("pe", lambda e: e.matmul(PS[psum_i][0:M, 0:c1 - c0], lhsT=wt[:, kc, mo:mo + M], rhs=hTbox["hT"][:, kc, c0:c1],
                                              start=(kc == 0), stop=(kc == 15)),
                     r=[d_wt, d_hT], w=[dPS[psum_i]], signal=(kc == 15))

        def mixer_group(gi):
            isS = gi == 1
            ntok = 1024 if isS else 512
            ntile = ntok // 128
            mcols = 576 if isS else 512
            ocol0 = 512 if isS else 0
            seqs = [(0, 1024)] if isS else [(0, 256), (256, 512)]
            mt = 5 if isS else 4
            X = XS if isS else XP
            with arena.scope() as ph:
                norm_to_featmajor(ph, lambda t: X[t * 128:(t + 1) * 128, :], ntile, hTbox["hT"], d_hT, s1v, b1v, gi, "n1")
                S.barrier()
            if stop_after == "n1":
                raise _Stop()
            with arena.scope() as gs:
                G = alloc(gs, "G", (128, 10, ntile, 16)); d_G = Dep()
                for t in range(ntile):
                    for kc in range(16):
                        S.op("pe", lambda e: e.matmul(PS[0][:, t * 32:(t + 1) * 32], lhsT=hTbox["hT"][:, kc, t * 128:(t + 1) * 128],
                                                      rhs=wgb[:, kc, :], start=(kc == 0), stop=(kc == 15)),
                             r=[d_hT, d_wgb], w=[dPS[0]], signal=(kc == 15 and t == ntile - 1))
                pg = PS[0][:, 0:ntile * 32].rearrange("p (t c) -> p t c", c=32)
                S.op("act", lambda e: e.activation(out=G[:, 0], in_=pg[:, :, 0:16], func=AF.Exp, scale=-1.0), r=[dPS[0]], w=[d_G])
                S.op("dve", lambda e: e.tensor_scalar(out=G[:, 0], in0=G[:, 0], scalar1=1.0, scalar2=None, op0=ALU.add), r=[d_G], w=[d_G])
                S.op("dve", lambda e: e.reciprocal(out=G[:, 0], in_=G[:, 0]), r=[d_G], w=[d_G])
                S.op("dve", lambda e: e.tensor_tensor(out=G[:, 1], in0=pg[:, :, 16:32],
                                                      in1=dtbb.unsqueeze(1).broadcast_to([128, ntile, 16]), op=ALU.add),
                     r=[dPS[0], d_small], w=[d_G])
                S.op("act", lambda e: e.activation(out=G[:, 1], in_=G[:, 1], func=AF.Exp), r=[d_G], w=[d_G])
                S.op("act", lambda e: e.activation(out=G[:, 1], in_=G[:, 1], func=AF.Ln, bias=1.0), r=[d_G], w=[d_G])
                S.op("dve", lambda e: e.tensor_tensor(out=G[:, 1], in0=G[:, 1],
                                                      in1=negA.unsqueeze(1).broadcast_to([128, ntile, 16]), op=ALU.mult),
                     r=[d_G, d_small], w=[d_G])
                gflat = G[:, 1].rearrange("p t c -> p (t c)")
                n16 = ntile * 16
                for i, m in enumerate((ublk, lblk, eblk)):
                    S.op("pe", lambda e: e.matmul(PS[1][:, i * n16:(i + 1) * n16], lhsT=m, rhs=gflat, start=True, stop=True),
                         r=[d_G, d_cst], w=[dPS[1]], signal=(i == 2))
                for i, m in enumerate((e0, e1)):
                    S.op("pe", lambda e: e.matmul(PS[2][:, i * n16:(i + 1) * n16], lhsT=m, rhs=gflat, start=True, stop=True),
                         r=[d_G, d_cst], w=[dPS[2]], signal=(i == 1))
                p1 = lambda i: PS[1][:, i * n16:(i + 1) * n16].rearrange("p (t c) -> p t c", c=16)
                p2 = lambda i: PS[2][:, i * n16:(i + 1) * n16].rearrange("p (t c) -> p t c", c=16)
                S.op("dve", lambda e: e.tensor_copy(out=G[:, 2, :, 0:8], in_=p1(0)[:, :, 0:8]), r=[dPS[1]], w=[d_G])
                S.op("dve", lambda e: e.tensor_copy(out=G[:, 2, :, 8:16], in_=p1(1)[:, :, 8:16]), r=[dPS[1]], w=[d_G])
                S.op("dve", lambda e: e.tensor_copy(out=G[:, 3], in_=p1(2)), r=[dPS[1]], w=[d_G])
                S.op("act", lambda e: e.activation(out=G[:, 4], in_=G[:, 2], func=AF.Exp), r=[d_G], w=[d_G])
                S.op("dve", lambda e: e.tensor_tensor(out=G[:, 9], in0=G[:, 3], in1=G[:, 2], op=ALU.subtract), r=[d_G], w=[d_G])
                S.op("act", lambda e: e.activation(out=G[:, 5], in_=G[:, 9], func=AF.Exp), r=[d_G], w=[d_G])
                S.op("act", lambda e: e.activation(out=G[:, 6], in_=p2(0), func=AF.Exp), r=[dPS[2]], w=[d_G])
                S.op("act", lambda e: e.activation(out=G[:, 7], in_=p2(1), func=AF.Exp), r=[dPS[2]], w=[d_G])
                S.op("dve", lambda e: e.tensor_tensor(out=G[:, 8], in0=G[:, 0], in1=G[:, 4], op=ALU.mult), r=[d_G], w=[d_G])
                BETA, GG, GC, EGC, EKD, DL, BEGC = G[:, 0], G[:, 1], G[:, 2], G[:, 4], G[:, 5], (G[:, 6], G[:, 7]), G[:, 8]
                if stop_after == "gates":
                    S.barrier()
                    raise _Stop()

                wch = [(alloc(gs, "wch%d" % i, (128, 16, 128), BF16), Dep()) for i in range(6)]
                win_stream = WStream(S, wch, [WIN[h_, c_] for h_ in range(NHEADS) for c_ in range(7)], 5)
                win_stream.prefetch()

                def load_w(h, ci):
                    return win_stream.get()

                def attention_head(h):
                    S.mark("g%d h%d attn" % (gi, h))
                    with arena.scope() as hs:
                        nk = ntok + (256 if isS else 0)
                        nkt = nk // 128
                        qT = alloc(hs, "qT", (128, mcols), BF16); d_qT = Dep()
                        kT = alloc(hs, "kT", (128, nk), BF16); d_kT = Dep()
                        vaug = alloc(hs, "vaug", (128, nkt, 132), BF16); d_va = Dep()
                        S.op("pool", lambda e: e.memset(vaug[:], 1.0), w=[d_va])
                        if stop_after == "attnA":
                            load_w(h, 0)
                            S.barrier()
                            raise _Stop()
                        tmpf = [(alloc(hs, "tmpf%d" % i, (128, 512)), Dep()) for i in range(2)]
                        tmpb = [(alloc(hs, "tmpb%d" % i, (128, 512), BF16), Dep()) for i in range(2)]
                        if isS:
                            rope = alloc(hs, "rope", (128, 2, 1024)); d_rope = Dep()
                            S.dma("sp", rope[:], ROPE, w=[d_rope])
                        stage = alloc(hs, "stage", (128, 4, 128)); d_stage = Dep()

                        def qk_proj(ci, dst, d_dst, ncols, keep_f32=None):
                            wt, d_wt = load_w(h, ci)
                            for pi, (c0, c1) in enumerate(passes(ncols)):
                                n = c1 - c0
                                pb = pi % 2
                                proj(wt, d_wt, c0, c1, pb)
                                KQ = int(os.environ.get("KQ", "9"))
                                if KQ == 1:
                                    continue
                                if keep_f32 is not None and KQ >= 3:
                                    S.op("act", lambda e: e.activation(out=keep_f32[0][:, c0:c1], in_=PS[pb][:, 0:n], func=AF.Identity),
                                         r=[dPS[pb]], w=[keep_f32[1]])
                                if not isS:
                                    S.op("dve", lambda e: e.tensor_copy(out=dst[:, c0:c1], in_=PS[pb][:, 0:n]),
                                         r=[dPS[pb]], w=[d_dst])
                                else:
                                    tb, d_tb = tmpb[pi % 2]
                                    tf, d_tf = tmpf[pi % 2]
                                    S.op("act", lambda e: e.activation(out=tb[:, 0:n], in_=PS[pb][:, 0:n], func=AF.Identity), r=[dPS[pb]], w=[d_tb])
                                    S.op("dve", lambda e: e.tensor_tensor(out=tf[:, 0:n], in0=PS[pb][:, 0:n],
                                                                          in1=rope[:, 0, c0:c1], op=ALU.mult),
                                         r=[dPS[pb], d_rope], w=[d_tf])
                                    S.op("pe", lambda e: e.matmul(PS[2 + pb][:, 0:n], lhsT=propeb[:], rhs=tb[:, 0:n],
                                                                  start=True, stop=True), r=[d_tb, d_cst], w=[dPS[2 + pb]])
                                    S.op("dve", lambda e: e.tensor_tensor(out=tb[:, 0:n], in0=PS[2 + pb][:, 0:n],
                                                                          in1=rope[:, 1, c0:c1], op=ALU.mult),
                                         r=[dPS[2 + pb], d_rope], w=[d_tb])
                                    S.op("pool", lambda e: e.tensor_tensor(out=dst[:, c0:c1], in0=tf[:, 0:n], in1=tb[:, 0:n],
                                                                           op=ALU.add), r=[d_tf, d_tb], w=[d_dst])

                        kf = None
                        if not isS:
                            kf = (alloc(hs, "kf", (128, 512)), Dep())
                        qk_proj(0, qT, d_qT, mcols)
                        qk_proj(1, kT, d_kT, ntok, keep_f32=kf)
                        if stop_after == "attnB":
                            S.barrier()
                            raise _Stop()
                        avf = alloc(hs, "avf", (128, ntok)); d_avf = Dep()
                        wt, d_wt = load_w(h, 2)
                        for pi, (c0, c1) in enumerate(passes(ntok)):
                            proj(wt, d_wt, c0, c1, pi % 2)
                            S.op("act", lambda e: e.activation(out=avf[:, c0:c1], in_=PS[pi % 2][:, 0:c1 - c0], func=AF.Identity),
                                 r=[dPS[pi % 2]], w=[d_avf])
                        for t in range(ntile):
                            pb = 4 + t % 2
                            S.op("pe", lambda e: e.transpose(out=PS[pb][:, 0:128], in_=avf[:, t * 128:(t + 1) * 128], identity=ident),
                                 r=[d_avf, d_cst], w=[dPS[pb]])
                            S.op("act", lambda e: e.activation(out=vaug[:, t, 0:128], in_=PS[pb][:, 0:128], func=AF.Identity), r=[dPS[pb]], w=[d_va])
                            if not isS:
                                S.op("dve", lambda e: e.tensor_copy(out=stage[:, t, :], in_=PS[pb][:, 0:128]),
                                     r=[dPS[pb]], w=[d_stage])
                        if not isS:
                            S.dma("sp", NV[:, h, :].rearrange("(t p) d -> p t d", p=128), stage[:], r=[d_stage], evlist=out_deps)
                            kf_t, d_kf = kf
                            stage2 = alloc(hs, "stage2", (128, 4, 128)); d_stage2 = Dep()
                            for t in range(4):
                                pb = 4 + t % 2
                                S.op("pe", lambda e: e.transpose(out=PS[pb][:, 0:128], in_=kf_t[:, t * 128:(t + 1) * 128], identity=ident),
                                     r=[d_kf, d_cst], w=[dPS[pb]])
                                S.op("dve", lambda e: e.tensor_copy(out=stage2[:, t, :], in_=PS[pb][:, 0:128]),
                                     r=[dPS[pb]], w=[d_stage2])
                            S.dma("sp", NK[:, h, :].rearrange("(t p) d -> p t d", p=128), stage2[:], r=[d_stage2], evlist=out_deps)
                        else:
                            ckf = alloc(hs, "ckf", (128, 2, 128)); d_ckf = Dep()
                            S.dma("sp", ckf[:], CK[h].rearrange("(t p) d -> p t d", p=128), w=[d_ckf])
                            S.dma("pool", vaug[:, 8:10, 0:128], CV[h].rearrange("(t p) d -> p t d", p=128), w=[d_va])
                            for t in range(2):
                                pb = 4 + t
                                S.op("pe", lambda e: e.transpose(out=PS[pb][:, 0:128], in_=ckf[:, t, :], identity=ident),
                                     r=[d_ckf, d_cst], w=[dPS[pb]])
                                S.op("dve", lambda e: e.tensor_copy(out=kT[:, 1024 + t * 128:1024 + (t + 1) * 128],
                                                                    in_=PS[pb][:, 0:128]), r=[dPS[pb]], w=[d_kT])
                        if stop_after == "attnproj":
                            S.barrier()
                            raise _Stop()
                        S.mark("g%d h%d scores" % (gi, h))
                        On = alloc(hs, "On", (128, 2, mt, 128)); d_On = Dep()
                        Eall = alloc(hs, "Eall", (128, nkt, 576), BF16); d_E = Dep()
                        rden = alloc(hs, "rden", (128, 16)); d_rden = Dep()
                        ectr = 0
                        for (q0, q1) in ([(0, 576)] if isS else [(0, 256), (256, 512)]):
                            nq = q1 - q0
                            kts = list(range(nkt)) if isS else [q0 // 128, q0 // 128 + 1]
                            qtl = [(a, min(a + 128, nq)) for a in range(0, nq, 128)]
                            for m in range(2):
                                rows = slice(m * 64, m * 64 + 64)
                                for ki, kt in enumerate(kts):
                                    ectr += 1
                                    for pi, (a, b) in enumerate(passes(nq)):
                                        pb = 2 * (ectr % 2) + pi
                                        S.op("pe", lambda e: e.matmul(PS[pb][:, 0:b - a], lhsT=kT[rows, kt * 128:(kt + 1) * 128],
                                                                      rhs=qT[rows, q0 + a:q0 + b], start=True, stop=True),
                                             r=[d_kT, d_qT], w=[dPS[pb]])
                                        S.op("act", lambda e: e.activation(out=Eall[:, ki, a:b], in_=PS[pb][:, 0:b - a], func=AF.Exp,
                                                                           scale=0.125), r=[dPS[pb]], w=[d_E])
                                for qi, (a, b) in enumerate(qtl):
                                    pb = 4 + qi // 3
                                    co = (qi % 3) * 129
                                    for ki, kt in enumerate(kts):
                                        S.op("pe", lambda e: e.matmul(PS[pb][0:b - a, co:co + 129], lhsT=Eall[:, ki, a:b],
                                                                      rhs=vaug[:, kt, 0:129], start=(ki == 0), stop=(ki == len(kts) - 1)),
                                             r=[d_E, d_va], w=[dPS[pb]], signal=(ki == len(kts) - 1))
                                for qi, (a, b) in enumerate(qtl):
                                    pb = 4 + qi // 3
                                    co = (qi % 3) * 129
                                    n = b - a
                                    tq = (q0 + a) // 128
                                    S.op("dve", lambda e: e.reciprocal(out=rden[0:n, qi:qi + 1], in_=PS[pb][0:n, co + 128:co + 129]),
                                         r=[dPS[pb]], w=[d_rden])
                                    S.op("dve", lambda e: e.tensor_scalar(out=On[0:n, m, tq, :], in0=PS[pb][0:n, co:co + 128],
                                                                          scalar1=rden[0:n, qi:qi + 1], scalar2=None, op0=ALU.mult),
                                         r=[dPS[pb], d_rden], w=[d_On])
                        if stop_after == "attnpv":
                            S.barrier()
                            raise _Stop()
                        st = alloc(hs, "ast", (128, 2 * mt)); d_st = Dep()
                        S.op("dve", lambda e: e.memset(st[:], 0.0), w=[d_st])
                        for tq in range(mt):
                            n = 64 if (isS and tq == 4) else 128
                            S.op("dve", lambda e: e.scalar_tensor_tensor(out=On[0:n, 0, tq, :], in0=On[0:n, 1, tq, :], scalar=neglam[0:n, :],
                                                                         in1=On[0:n, 0, tq, :], op0=ALU.mult, op1=ALU.add),
                                 r=[d_On, d_small], w=[d_On])
                            S.op("act", lambda e: e.activation(out=On[0:n, 1, tq, :], in_=On[0:n, 0, tq, :], func=AF.Square,
                                                               accum_out=st[0:n, tq:tq + 1]), r=[d_On], w=[d_On, d_st])
                        rstd_from_ss(st, d_st, mt, 1.0 / 128)
                        for tq in range(mt):
                            n = 64 if (isS and tq == 4) else 128
                            pb = 2 + tq % 2
                            S.op("dve", lambda e: e.scalar_tensor_tensor(out=On[0:n, 1, tq, :], in0=On[0:n, 0, tq, :],
                                                                         scalar=st[0:n, mt + tq:mt + tq + 1], in1=subgs[0:n, :],
                                                                         op0=ALU.mult, op1=ALU.mult),
                                 r=[d_On, d_st, d_small], w=[d_On])
                            S.op("pe", lambda e: e.transpose(out=PS[pb][:, 0:n], in_=On[0:n, 1, tq, :], identity=cst[0:n, 0, 0:n]),
                                 r=[d_On, d_cst], w=[dPS[pb]])
                            S.op("act", lambda e: e.activation(out=oT[:, h, ocol0 + tq * 128:ocol0 + tq * 128 + n], in_=PS[pb][:, 0:n], func=AF.Identity),
                                 r=[dPS[pb]], w=[d_oT[h]])
                        if S.rec is None:
                            S.barrier()

                attention_head(0)
                for h in range(NHEADS):
                    if stop_after == "attn":
                        if h + 1 < NHEADS:
                            attention_head(h + 1)
                        continue
                    with arena.scope() as hs:
                        S.mark("g%d h%d dnproj" % (gi, h))

                        def hook(h=h):
                            if h + 1 >= NHEADS or os.environ.get("KNOILV"):
                                if h + 1 < NHEADS:
                                    return None
                                return []
                            S.rec = []
                            arena.norelease = True
                            attention_head(h + 1)
                            arena.norelease = False
                            rec, S.rec = S.rec, None
                            return rec

                        dn_head(hs, gi, h, load_w, G_=(BETA, GG, GC, EGC, EKD, DL, BEGC), d_G=d_G, interleave=hook)
                        S.barrier()
                    if os.environ.get("KNOILV") and h + 1 < NHEADS:
                        attention_head(h + 1)
                S.barrier()

        def dn_head(hs, gi, h, load_w, G_, d_G, interleave=None):
            BETA, GG, GC, EGC, EKD, DL, BEGC = G_
            isS = gi == 1
            ntok = 1024 if isS else 512
            ntile = ntok // 128
            mcols = 576 if isS else 512
            ocol0 = 512 if isS else 0
            mt = 5 if isS else 4
            seqs = [(0, 1024)] if isS else [(0, 256), (256, 512)]
            yq = alloc(hs, "yq", (128, ntok))
            yqb = alloc(hs, "yqb", (128, ntok), BF16); ykb = alloc(hs, "ykb", (128, ntok), BF16); d_yb = Dep()
            gsT = alloc(hs, "gsT", (128, mcols)); d_gs = Dep()
            dirbuf = []
            for d_ in range(2):
                ntd = mt if (isS and d_ == 0) else ntile
                dirbuf.append(dict(uw=alloc(hs, "uw%d" % d_, (128, ntd, 256), BF16), kdq=alloc(hs, "kdq%d" % d_, (128, ntd, 256), BF16), d=Dep()))
            inner_mark = arena.ptr
            yk = alloc(hs, "yk", (128, ntok)); yv = alloc(hs, "yv", (128, ntok))
            d_y = [Dep(), Dep(), Dep()]
            zrs = [(alloc(hs, "zr%d" % i_, (128, ntok)), Dep()) for i_ in range(2)]
            for which, (ci, yy) in enumerate(((3, yq), (4, yk), (5, yv))):
                zr, d_zr = zrs[which % 2]
                wt, d_wt = load_w(h, ci)
                cw = convq[:, which * 8 + h, :]
                for pi, (c0, c1) in enumerate(passes(ntok)):
                    proj(wt, d_wt, c0, c1, pi % 2)
                    S.op("act", lambda e: e.activation(out=zr[:, c0:c1], in_=PS[pi % 2][:, 0:c1 - c0], func=AF.Identity), r=[dPS[pi % 2]], w=[d_zr])
                S.op("dve", lambda e: e.tensor_scalar(out=yy[:], in0=zr[:], scalar1=cw[:, 1:2], scalar2=None, op0=ALU.mult),
                     r=[d_zr, d_convq], w=[d_y[which]])
                for (a, b) in seqs:
                    S.op("dve", lambda e: e.scalar_tensor_tensor(out=yy[:, a + 1:b], in0=zr[:, a:b - 1], scalar=cw[:, 0:1],
                                                                 in1=yy[:, a + 1:b], op0=ALU.mult, op1=ALU.add),
                         r=[d_zr, d_convq], w=[d_y[which]])
                    S.op("dve", lambda e: e.scalar_tensor_tensor(out=yy[:, a:b - 1], in0=zr[:, a + 1:b], scalar=cw[:, 2:3],
                                                                  in1=yy[:, a:b - 1], op0=ALU.mult, op1=ALU.add),
                         r=[d_zr, d_convq], w=[d_y[which]])
                S.op("act", lambda e: e.activation(out=yy[:], in_=yy[:], func=AF.Silu), r=[d_y[which]], w=[d_y[which]])
            wt, d_wt = load_w(h, 6)
            for pi, (c0, c1) in enumerate(passes(mcols)):
                proj(wt, d_wt, c0, c1, pi % 2)
                S.op("act", lambda e: e.activation(out=gsT[:, c0:c1], in_=PS[pi % 2][:, 0:c1 - c0], func=AF.Silu),
                     r=[dPS[pi % 2]], w=[d_gs])
            for which, yy in ((0, yq), (1, yk)):
                zr, d_zr = zrs[(which + 1) % 2]
                S.op("pool", lambda e: e.tensor_tensor(out=zr[:], in0=yy[:], in1=yy[:], op=ALU.mult), r=[d_y[which]], w=[d_zr])
                for pi, (c0, c1) in enumerate(passes(ntok)):
                    pb = 2 + pi % 2
                    n = c1 - c0
                    S.op("pe", lambda e: e.matmul(PS[pb][:, 0:n], lhsT=ones, rhs=zr[:, c0:c1], start=True, stop=True),
                         r=[d_zr, d_cst], w=[dPS[pb]])
                    S.op("act", lambda e: e.activation(out=zr[:, c0:c1], in_=PS[pb][:, 0:n], func=AF.Ln, bias=EPS, scale=1.0),
                         r=[dPS[pb]], w=[d_zr])
                    S.op("act", lambda e: e.activation(out=zr[:, c0:c1], in_=zr[:, c0:c1], func=AF.Exp, scale=-0.5),
                         r=[d_zr], w=[d_zr])
                sc = 128.0 ** -0.5 if which == 0 else 1.0
                S.op("dve", lambda e: e.scalar_tensor_tensor(out=yy[:], in0=yy[:], scalar=sc, in1=zr[:], op0=ALU.mult, op1=ALU.mult),
                     r=[d_zr, d_y[which]], w=[d_y[which]])
                S.op("act", lambda e: e.activation(out=(yqb if which == 0 else ykb)[:], in_=yy[:], func=AF.Identity),
                     r=[d_y[which]], w=[d_yb])
            ktok = alloc(hs, "ktok", (128, ntile, 128)); vtok = alloc(hs, "vtok", (128, ntile, 128)); d_kv = Dep()
            KK = alloc(hs, "KK", (128, ntile, 128)); QKT = alloc(hs, "QKT", (128, mt, 128)); d_KQ = Dep()
            for t in range(ntile):
                ts = slice(t * 128, (t + 1) * 128)
                pb = 4 + t % 2
                S.op("pe", lambda e: e.transpose(out=PS[pb][:, 0:128], in_=yk[:, ts], identity=ident), r=[d_y[1], d_cst], w=[dPS[pb]], signal=False)
                S.op("pe", lambda e: e.transpose(out=PS[pb][:, 128:256], in_=yv[:, ts], identity=ident), r=[d_y[2], d_cst], w=[dPS[pb]])
                S.op("act", lambda e: e.activation(out=ktok[:, t, :], in_=PS[pb][:, 0:128], func=AF.Identity), r=[dPS[pb]], w=[d_kv])
                S.op("act", lambda e: e.activation(out=vtok[:, t, :], in_=PS[pb][:, 128:256], func=AF.Identity), r=[dPS[pb]], w=[d_kv])
                pb2 = 6 + t % 2
                S.op("pe", lambda e: e.matmul(PS[pb2][:, 0:128], lhsT=ykb[:, ts], rhs=ykb[:, ts], start=True, stop=True),
                     r=[d_yb], w=[dPS[pb2]], signal=(t >= mt))
                S.op("dve", lambda e: e.tensor_copy(out=KK[:, t, :], in_=PS[pb2][:, 0:128]), r=[dPS[pb2]], w=[d_KQ]) if t >= mt else None
                if t < mt:
                    S.op("pe", lambda e: e.matmul(PS[pb2][:, 128:256], lhsT=ykb[:, ts], rhs=yqb[:, ts], start=True, stop=True),
                         r=[d_yb], w=[dPS[pb2]])
                    S.op("dve", lambda e: e.tensor_copy(out=KK[:, t, :], in_=PS[pb2][:, 0:128]), r=[dPS[pb2]], w=[d_KQ])
                    S.op("dve", lambda e: e.tensor_copy(out=QKT[:, t, :], in_=PS[pb2][:, 128:256]), r=[dPS[pb2]], w=[d_KQ])
            S.mark("g%d h%d solve" % (gi, h))
            NP = 8
            sol = [dict(A=[alloc(hs, "sA%d_%d" % (i, j), (128, 128)) for j in range(2)],
                        BW=[alloc(hs, "sBW%d_%d" % (i, j), (128, 256)) for j in range(2)],
                        x1=alloc(hs, "sx1_%d" % i, (128, 128)), x2=alloc(hs, "sx2_%d" % i, (128, 128)),
                        d=Dep()) for i in range(NP)]
            vb = [(alloc(hs, "vbk%d" % i, (128, 256)), Dep()) for i in range(2)]
            probs = [(d_, t) for d_ in range(2) for t in (range(ntile) if (d_ == 1 or not isS) else range(mt))]

            def pinfo(d_):
                return (d_ * 8 + h, (ublk, slblk, ublk) if d_ == 0 else (lblk, sublk, lblk), dirbuf[d_])

            for _once in (0,):
                for g0 in range(0, len(probs), NP):
                    grp = probs[g0:g0 + NP]
                    for i, (d_, t) in enumerate(grp):
                        hd, (Mg, Ms, Mi), db = pinfo(d_)
                        sl = sol[i]
                        pb = i % 8
                        gcol = GG[:, t, hd:hd + 1]
                        S.op("pool", lambda e: e.tensor_tensor(out=sl["x1"][:], in0=Mg, in1=gcol.broadcast_to([128, 128]), op=ALU.mult),
                             r=[d_G, d_cst], w=[sl["d"]])
                        S.op("pe", lambda e: e.matmul(PS[pb][:, 0:128], lhsT=ones, rhs=sl["x1"][:], start=True, stop=True),
                             r=[sl["d"], d_cst], w=[dPS[pb]])
                        S.op("dve", lambda e: e.tensor_scalar(out=sl["x1"][:], in0=PS[pb][:, 0:128], scalar1=GC[:, t, hd:hd + 1],
                                                              scalar2=0.0, op0=ALU.subtract, op1=ALU.max),
                             r=[dPS[pb], d_G], w=[sl["d"]])
                        S.op("dve", lambda e: e.tensor_scalar(out=sl["x2"][:], in0=PS[pb][:, 0:128], scalar1=GC[:, t, hd:hd + 1],
                                                              scalar2=0.0, op0=ALU.subtract, op1=ALU.min),
                             r=[dPS[pb], d_G], w=[sl["d"]])
                        S.op("act", lambda e: e.activation(out=sl["x1"][:], in_=sl["x1"][:], func=AF.Exp, scale=-1.0),
                             r=[sl["d"]], w=[sl["d"]])
                        S.op("act", lambda e: e.activation(out=sl["x2"][:], in_=sl["x2"][:], func=AF.Exp), r=[sl["d"]], w=[sl["d"]])
                        S.op("pool", lambda e: e.tensor_tensor(out=sl["x1"][:], in0=sl["x1"][:], in1=Ms, op=ALU.mult),
                             r=[sl["d"], d_cst], w=[sl["d"]])
                        S.op("dve", lambda e: e.scalar_tensor_tensor(out=sl["A"][0][:], in0=KK[:, t, :], scalar=BETA[:, t, hd:hd + 1],
                                                                     in1=sl["x1"][:], op0=ALU.mult, op1=ALU.mult),
                             r=[d_KQ, d_G, sl["d"]], w=[sl["d"]])
                        if t < mt:
                            S.op("pool", lambda e: e.tensor_tensor(out=sl["x2"][:], in0=sl["x2"][:], in1=Mi, op=ALU.mult),
                                 r=[sl["d"], d_cst], w=[sl["d"]])
                            S.op("pool", lambda e: e.tensor_tensor(out=db["kdq"][:, t, 128:256], in0=QKT[:, t, :], in1=sl["x2"][:], op=ALU.mult),
                                 r=[d_KQ, sl["d"]], w=[db["d"]])
                    for i, (d_, t) in enumerate(grp):
                        hd, (Mg, Ms, Mi), db = pinfo(d_)
                        sl = sol[i]
                        pb = i % 8
                        S.op("pe", lambda e: e.transpose(out=PS[pb][:, 0:128], in_=sl["A"][0][:], identity=ident),
                             r=[sl["d"], d_cst], w=[dPS[pb]])
                        S.op("act", lambda e: e.activation(out=sl["BW"][0][:, 0:128], in_=PS[pb][:, 0:128], func=AF.Identity), r=[dPS[pb]], w=[sl["d"]])
                        S.op("dve", lambda e: e.tensor_tensor(out=sl["BW"][0][:, 128:256], in0=ident, in1=PS[pb][:, 0:128], op=ALU.subtract),
                             r=[dPS[pb], d_cst], w=[sl["d"]])
                    cur = 0
                    for i, (d_, t) in enumerate(grp):
                        sl = sol[i]
                        pb = i % 8
                        S.op("pe", lambda e: e.matmul(PS[pb][:, 128:256], lhsT=sl["A"][0][:], rhs=sl["BW"][0][:, 0:128], start=True, stop=True),
                             r=[sl["d"]], w=[dPS[pb]])
                        S.op("dve", lambda e: e.tensor_copy(out=sl["BW"][1][:, 0:128], in_=PS[pb][:, 128:256]), r=[dPS[pb]], w=[sl["d"]])
                        S.op("pool", lambda e: e.tensor_copy(out=sl["BW"][1][:, 128:256], in_=sl["BW"][0][:, 128:256]), r=[sl["d"]], w=[sl["d"]])
                    for i, (d_, t) in enumerate(grp):
                        sl = sol[i]
                        pb = i % 8
                        S.op("pe", lambda e: e.transpose(out=PS[pb][:, 0:128], in_=sl["BW"][1][:, 0:128], identity=ident),
                             r=[sl["d"], d_cst], w=[dPS[pb]])
                        S.op("act", lambda e: e.activation(out=sl["A"][1][:], in_=PS[pb][:, 0:128], func=AF.Identity), r=[dPS[pb]], w=[sl["d"]])
                    cur = 1
                    for lvl in range(1, 6):
                        nxt = 1 - cur
                        last_b = lvl >= 4
                        last_a = lvl >= 5
                        for i, (d_, t) in enumerate(grp):
                            sl = sol[i]
                            pb = i % 8
                            Ac, BWc, An, BWn = sl["A"][cur], sl["BW"][cur], sl["A"][nxt], sl["BW"][nxt]
                            if not last_b:
                                S.op("pe", lambda e: e.matmul(PS[pb][:, 0:256], lhsT=Ac[:], rhs=BWc[:], start=True, stop=True),
                                     r=[sl["d"]], w=[dPS[pb]])
                                S.op("act", lambda e: e.activation(out=BWn[:, 0:128], in_=PS[pb][:, 0:128], func=AF.Identity), r=[dPS[pb]], w=[sl["d"]])
                            else:
                                S.op("pe", lambda e: e.matmul(PS[pb][:, 128:256], lhsT=Ac[:], rhs=BWc[:, 128:256], start=True, stop=True),
                                     r=[sl["d"]], w=[dPS[pb]], signal=last_a)
                                if not last_a:
                                    S.op("pe", lambda e: e.matmul(PS[pb][:, 256:384], lhsT=BWc[:, 0:128], rhs=Ac[:], start=True, stop=True),
                                         r=[sl["d"]], w=[dPS[pb]])
                                    S.op("act", lambda e: e.activation(out=An[:], in_=PS[pb][:, 256:384], func=AF.Identity), r=[dPS[pb]], w=[sl["d"]])
                            S.op("dve", lambda e: e.tensor_tensor(out=BWn[:, 128:256], in0=BWc[:, 128:256], in1=PS[pb][:, 128:256], op=ALU.add),
                                 r=[dPS[pb], sl["d"]], w=[sl["d"]])
                        if not last_b:
                            for i, (d_, t) in enumerate(grp):
                                sl = sol[i]
                                pb = i % 8
                                An, BWn = sl["A"][nxt], sl["BW"][nxt]
                                S.op("pe", lambda e: e.transpose(out=PS[pb][:, 256:384], in_=BWn[:, 0:128], identity=ident),
                                     r=[sl["d"], d_cst], w=[dPS[pb]])
                                S.op("act", lambda e: e.activation(out=An[:], in_=PS[pb][:, 256:384], func=AF.Identity), r=[dPS[pb]], w=[sl["d"]])
                        cur = nxt
                    for i, (d_, t) in enumerate(grp):
                        hd, (Mg, Ms, Mi), db = pinfo(d_)
                        sl = sol[i]
                        W = sl["BW"][cur][:, 128:256]
                        vbk, d_vb = vb[i % 2]
                        pb = i % 8
                        S.op("pool", lambda e: e.tensor_tensor(out=vbk[:, 0:128], in0=vtok[:, t, :], in1=BETA[:, t, hd:hd + 1].broadcast_to([128, 128]), op=ALU.mult),
                             r=[d_kv, d_G], w=[d_vb])
                        S.op("pool", lambda e: e.tensor_tensor(out=vbk[:, 128:256], in0=ktok[:, t, :], in1=BEGC[:, t, hd:hd + 1].broadcast_to([128, 128]), op=ALU.mult),
                             r=[d_kv, d_G], w=[d_vb])
                        S.op("pool", lambda e: e.tensor_tensor(out=db["kdq"][:, t, 0:128], in0=ktok[:, t, :], in1=EKD[:, t, hd:hd + 1].broadcast_to([128, 128]), op=ALU.mult),
                             r=[d_kv, d_G], w=[db["d"]])
                        S.op("pe", lambda e: e.matmul(PS[pb][:, 0:256], lhsT=W, rhs=vbk[:], start=True, stop=True),
                             r=[sl["d"], d_vb], w=[dPS[pb]])
                        S.op("act", lambda e: e.activation(out=db["uw"][:, t, :], in_=PS[pb][:, 0:256], func=AF.Identity), r=[dPS[pb]], w=[db["d"]])
            S.barrier()
            arena.ptr = inner_mark
            S.mark("g%d h%d recur" % (gi, h))
            oacc = alloc(hs, "oacc", (128, mt, 128)); d_oacc = Dep()
            S.op("pool", lambda e: e.memset(oacc[:], 0.0), w=[d_oacc])
            junk = alloc(hs, "dnjunk", (128, 128)); d_junk = Dep()
            chains = []
            for si, (a, b) in enumerate(seqs):
                t0, t1 = a // 128, b // 128
                for d_ in range(2):
                    if d_ == 0:
                        chunks = [(t, hf) for t in range(t0, t1) for hf in (0, 1)]
                        if isS:
                            chunks = chunks[:9]
                    else:
                        chunks = [(t, hf) for t in range(t1 - 1, t0 - 1, -1) for hf in (1, 0)]
                    outs = [(t < mt and not (isS and t == 4 and hf == 1)) for (t, hf) in chunks]
                    nch = len(chunks)
                    ch = dict(St=[alloc(hs, "St%d_%d_%d" % (si, d_, k), (128, 128)) for k in range(2)],
                              MT=alloc(hs, "MT%d_%d" % (si, d_), (128, nch, 128)), CC=alloc(hs, "CC%d_%d" % (si, d_), (128, nch, 128), BF16),
                              NWQ=alloc(hs, "NWQ%d_%d" % (si, d_), (128, max(1, sum(outs)), 128), BF16),
                              Sb=[alloc(hs, "Sb%d_%d_%d" % (si, d_, k), (128, 128), BF16) for k in range(2)], dSb=[Dep(), Dep()],
                              tmp=alloc(hs, "rt%d_%d" % (si, d_), (128, 128)),
                              dS=[Dep(), Dep()], dpre=[Dep() for _ in range(nch)], dtmp=Dep(),
                              chunks=chunks, outs=outs, dir=d_, si=si, idx=len(chains))
                    if isS:
                        S.dma("sp", ch["St"][0][:], (SF0 if d_ == 0 else SB0)[h], w=[ch["dS"][0]])
                    else:
                        S.op("pool", lambda e: e.memset(ch["St"][0][:], 0.0), w=[ch["dS"][0]])
                    chains.append(ch)
            pctr = 0
            for ch in chains:
                d_ = ch["dir"]
                hd = d_ * 8 + h
                db = dirbuf[d_]
                oi = 0
                for ci, (t, hf) in enumerate(ch["chunks"]):
                    rows = slice(hf * 64, hf * 64 + 64)
                    pb = pctr % 4
                    pctr += 1
                    need_o = ch["outs"][ci]
                    nn = 256 if need_o else 128
                    S.op("pe", lambda e: e.matmul(PS[pb][:, 0:nn], lhsT=db["uw"][rows, t, 128:256], rhs=db["kdq"][rows, t, 0:nn], start=True, stop=True),
                         r=[db["d"]], w=[dPS[pb]], signal=False)
                    S.op("pe", lambda e: e.matmul(PS[pb][:, 256:384], lhsT=db["kdq"][rows, t, 0:128], rhs=db["uw"][rows, t, 0:128], start=True, stop=True),
                         r=[db["d"]], w=[dPS[pb]])
                    S.op("act", lambda e: e.activation(out=ch["MT"][:, ci, :], in_=PS[pb][:, 0:128], func=AF.Identity, scale=-1.0),
                         r=[dPS[pb]], w=[ch["dpre"][ci]])
                    S.op("dve", lambda e: e.tensor_copy(out=ch["CC"][:, ci, :], in_=PS[pb][:, 256:384]), r=[dPS[pb]], w=[ch["dpre"][ci]])
                    if need_o:
                        S.op("act", lambda e: e.activation(out=ch["NWQ"][:, oi, :], in_=PS[pb][:, 128:256], func=AF.Identity, scale=-1.0),
                             r=[dPS[pb]], w=[ch["dpre"][ci]])
                        oi += 1
            maxlen = max(len(c["chunks"]) for c in chains)
            for ch in chains:
                ch["oi"] = 0
            thunks = (interleave() if interleave is not None else None) or []
            nslots = sum(len(c["chunks"]) for c in chains)
            per = (len(thunks) + nslots - 1) // max(1, nslots)
            tpos = [0]

            def replay(n):
                for th in thunks[tpos[0]:tpos[0] + n]:
                    th()
                tpos[0] = min(len(thunks), tpos[0] + n)

            for step in range(maxlen):
                for ch in chains:
                    if step >= len(ch["chunks"]):
                        continue
                    replay(per)
                    t, hf = ch["chunks"][step]
                    d_ = ch["dir"]
                    hd = d_ * 8 + h
                    db = dirbuf[d_]
                    rows = slice(hf * 64, hf * 64 + 64)
                    Sc, Sn = ch["St"][step % 2], ch["St"][(step + 1) % 2]
                    dSc, dSn = ch["dS"][step % 2], ch["dS"][(step + 1) % 2]
                    pbs = 6
                    ts = slice(t * 128, (t + 1) * 128)
                    S.op("pe", lambda e: e.matmul(PS[pbs][:, 0:128], lhsT=identb[:], rhs=ch["CC"][:, step, :], start=True, stop=False),
                         r=[ch["dpre"][step], d_cst], w=[dPS[pbs]], signal=False)
                    S.op("pe", lambda e: e.matmul(PS[pbs][:, 0:128], lhsT=ch["MT"][:, step, :], rhs=Sc[:], start=False, stop=True),
                         r=[ch["dpre"][step], dSc], w=[dPS[pbs]])
                    S.op("dve", lambda e: e.scalar_tensor_tensor(out=Sn[:], in0=Sc[:], scalar=DL[hf][:, t, hd:hd + 1], in1=PS[pbs][:, 0:128],
                                                                 op0=ALU.mult, op1=ALU.add),
                         r=[dPS[pbs], dSc, d_G], w=[dSn])
                    if ch["outs"][step]:
                        pbo = 7
                        oi = ch["oi"]
                        ch["oi"] += 1
                        Sb, dSb = ch["Sb"][step % 2], ch["dSb"][step % 2]
                        S.op("act", lambda e: e.activation(out=Sb[:], in_=Sc[:], func=AF.Identity), r=[dSc], w=[dSb])
                        S.op("pe", lambda e: e.matmul(PS[pbo][:, 0:128], lhsT=yqb[:, ts], rhs=Sb[:], start=True, stop=True),
                             r=[d_yb, dSb], w=[dPS[pbo]], signal=False)
                        S.op("pe", lambda e: e.matmul(PS[pbo][:, 128:256], lhsT=ch["NWQ"][:, oi, :], rhs=Sb[:], start=True, stop=False),
                             r=[ch["dpre"][step], dSb], w=[dPS[pbo]], signal=False)
                        S.op("pe", lambda e: e.matmul(PS[pbo][:, 128:256], lhsT=db["kdq"][rows, t, 128:256], rhs=db["uw"][rows, t, 0:128], start=False, stop=True),
                             r=[db["d"]], w=[dPS[pbo]])
                        S.op("dve", lambda e: e.tensor_tensor(out=ch["tmp"][rows, :], in0=PS[pbo][rows, 128:256], in1=oacc[rows, t, :], op=ALU.add),
                             r=[dPS[pbo], d_oacc], w=[ch["dtmp"]])
                        S.op("dve", lambda e: e.scalar_tensor_tensor(out=oacc[rows, t, :], in0=PS[pbo][rows, 0:128], scalar=EGC[rows, t, hd:hd + 1],
                                                                     in1=ch["tmp"][rows, :], op0=ALU.mult, op1=ALU.add),
                             r=[dPS[pbo], d_G, ch["dtmp"]], w=[d_oacc])
            replay(len(thunks))
            if not isS:
                for ch in chains:
                    n = len(ch["chunks"])
                    ev = S.dma("sp", (NSF if ch["dir"] == 0 else NSB)[ch["si"], h], ch["St"][n % 2][:], r=[ch["dS"][n % 2]])
                    out_deps.append(ev)
            S.mark("g%d h%d dnout" % (gi, h))
            st = alloc(hs, "dst", (128, 2 * mt)); d_st = Dep()
            S.op("dve", lambda e: e.memset(st[:], 0.0), w=[d_st])
            for tq in range(mt):
                n = 64 if (isS and tq == 4) else 128
                S.op("act", lambda e: e.activation(out=junk[0:n, :], in_=oacc[0:n, tq, :], func=AF.Square,
                                                   accum_out=st[0:n, tq:tq + 1]), r=[d_oacc], w=[d_junk, d_st])
            rstd_from_ss(st, d_st, mt, 1.0 / 128)
            for tq in range(mt):
                n = 64 if (isS and tq == 4) else 128
                pb = 6 + tq % 2
                S.op("dve", lambda e: e.scalar_tensor_tensor(out=oacc[0:n, tq, :], in0=oacc[0:n, tq, :], scalar=st[0:n, mt + tq:mt + tq + 1],
                                                             in1=dngb[0:n, :], op0=ALU.mult, op1=ALU.mult),
                     r=[d_st, d_small], w=[d_oacc])
                S.op("pe", lambda e: e.transpose(out=PS[pb][:, 0:n], in_=oacc[0:n, tq, :], identity=cst[0:n, 0, 0:n]),
                     r=[d_oacc, d_cst], w=[dPS[pb]])
                S.op("dve", lambda e: e.tensor_tensor(out=oT[:, 8 + h, ocol0 + tq * 128:ocol0 + tq * 128 + n], in0=PS[pb][:, 0:n],
                                                      in1=gsT[:, tq * 128:tq * 128 + n], op=ALU.mult),
                     r=[dPS[pb], d_gs], w=[d_oT[8 + h]])

        try:
            with arena.scope() as ms:
                hTbox["hT"] = alloc(ms, "hT", (128, 16, 1024), BF16)
                mixer_group(0)
                if stop_after != "g0":
                    mixer_group(1)
                S.barrier()
        except _Stop:
            stop_after = "mixer"

        if stop_after in ("mixer", "attn", "g0"):
            for ev in out_deps:
                S._wait("sp", ev)
            print("ops", S.nops, "waits", S.nwaits, "sim", S.simulate()[:2])
            return nc

        S.mark("phase3")
        xmid = alloc(top, "xmid", (128, 8, 2048)); d_xm = [Dep() for _ in range(9)]
        gbc = alloc(top, "gbc", (128, 2, 2048)); d_gbc = Dep()

        def mod_rowbc(ph_bufs, vec_i, bank0=0):
            for q in range(4):
                mod_rowbc_blk(ph_bufs, vec_i, q, bank0)

        def mod_rowbc_blk(ph_bufs, vec_i, q, bank0=0):
            wstream, mbb, d_mbb, mrow, d_mrow = ph_bufs
            if True:
                blk = vec_i * 4 + q
                wb, d_w = wstream.get()
                for cvi, L in enumerate((Ls["L0"], Ls["L1"])):
                    mod_block((wb, d_w, mbb, d_mbb, mrow, d_mrow), blk, L, bank0 + cvi)
                    S.op("act", lambda e: e.activation(out=gbc[:, cvi, q * 512:(q + 1) * 512], in_=mrow[:], func=AF.Identity), r=[d_mrow], w=[d_gbc])

        with arena.scope() as ph:
            make_L(ph)
            wbufs = [(alloc(ph, "mw%d" % i, (128, 16, 512), BF16), Dep()) for i in range(2)]
            mbb = alloc(ph, "mbb", (128, 512)); d_mbb = Dep()
            mrow = alloc(ph, "mrow", (128, 512)); d_mrow = Dep()
            bufs = (WStream(S, wbufs, [MODW[b_] for b_ in range(8, 20)], 1), mbb, d_mbb, mrow, d_mrow)
            bufs[0].prefetch()
            mod_rowbc(bufs, 2)
            mod_featmajor(bufs, 3, b2v, 2, 3)
            mod_featmajor(bufs, 4, s2v, 2, 3)
            S.op("dve", lambda e: e.tensor_scalar(out=small[:, 96:128], in0=small[:, 96:128], scalar1=1.0, scalar2=None,
                                                  op0=ALU.add), r=[d_small], w=[d_small])
            S.op("dve", lambda e: e.tensor_tensor(out=s2v, in0=s2v, in1=small[:, 16:32].unsqueeze(2).broadcast_to([128, 16, 2]),
                                                  op=ALU.mult), r=[d_small], w=[d_small])
            S.barrier()

        S.mark("phase4")
        NTM = 9
        with arena.scope() as ph4:
          xh = alloc(ph4, "xh", (128, 2048))
          xm = lambda t: (xmid[:, t, :] if t < 8 else xh)
          with arena.scope() as ph:
            wob = [(alloc(ph, "wo%d" % i, (128, 16, 512), BF16), Dep()) for i in range(2)]
            xin = [(alloc(ph, "xin%d" % i, (128, 512)), Dep()) for i in range(3)]
            tmpm = [(alloc(ph, "tmpm%d" % i, (128, 512)), Dep()) for i in range(2)]
            ctr = 0
            wo_stream = WStream(S, wob, [WOUT[b_] for b_ in range(4)], 1)
            wo_stream.prefetch()
            for nb in range(4):
                wb, d_w = wo_stream.get()
                for t in range(NTM):
                    n = 64 if t == 8 else 128
                    cvi = 0 if t < 4 else 1
                    pb = ctr % 4
                    xi, d_xi = xin[ctr % 3]
                    tm, d_tm = tmpm[ctr % 2]
                    ctr += 1
                    src = XP[t * 128:t * 128 + n, nb * 512:(nb + 1) * 512] if t < 4 else XS[(t - 4) * 128:(t - 4) * 128 + n, nb * 512:(nb + 1) * 512]
                    S.dma("sp", xi[0:n, :], src, w=[d_xi])
                    for kc in range(16):
                        S.op("pe", lambda e: e.matmul(PS[pb][0:n, :], lhsT=oT[:, kc, t * 128:t * 128 + n], rhs=wb[:, kc, :],
                                                      start=(kc == 0), stop=(kc == 15)),
                             r=[d_oT[kc], d_w], w=[dPS[pb]], signal=(kc == 15))
                    S.op("dve", lambda e: e.tensor_tensor(out=tm[0:n, :], in0=PS[pb][0:n, :], in1=gbc[0:n, cvi, nb * 512:(nb + 1) * 512], op=ALU.mult),
                         r=[dPS[pb], d_gbc], w=[d_tm])
                    S.op("pool", lambda e: e.tensor_tensor(out=xm(t)[0:n, nb * 512:(nb + 1) * 512], in0=tm[0:n, :], in1=xi[0:n, :], op=ALU.add),
                         r=[d_tm, d_xi], w=[d_xm[t]])
            S.barrier()
          h2T = oT; d_h2 = Dep()
          with arena.scope() as ph:
            xns = [(alloc(ph, "n2xn%d" % i, (128, 2048)), Dep()) for i in range(2)]
            st = alloc(ph, "n2st", (128, 2 * NTM)); d_st = Dep()
            S.op("dve", lambda e: e.memset(st[:], 0.0), w=[d_st])
            make_L(ph)
            wbufs5 = [(alloc(ph, "mw5_%d" % i, (128, 16, 512), BF16), Dep()) for i in range(2)]
            mbb5 = alloc(ph, "mbb5", (128, 512)); d_mbb5 = Dep()
            mrow5 = alloc(ph, "mrow5", (128, 512)); d_mrow5 = Dep()
            ws5 = WStream(S, wbufs5, [MODW[b_] for b_ in range(20, 24)], 1)
            ws5.prefetch()
            for t in range(NTM):
                if t in (1, 3, 5, 7):
                    mod_rowbc_blk((ws5, mbb5, d_mbb5, mrow5, d_mrow5), 5, (t - 1) // 2, bank0=4)
                n = 64 if t == 8 else 128
                cvi = 0 if t < 4 else 1
                xn, d_xn = xns[t % 2]
                S.op("act", lambda e: e.activation(out=xn[0:n, :], in_=xm(t)[0:n, :], func=AF.Square, accum_out=st[0:n, 2 * t:2 * t + 1]),
                     r=[d_xm[t]], w=[d_xn, d_st])
                S.op("act", lambda e: e.activation(out=st[0:n, 2 * t + 1:2 * t + 2], in_=st[0:n, 2 * t:2 * t + 1], func=AF.Ln, bias=EPS, scale=1.0 / D),
                     r=[d_st], w=[d_st])
                S.op("act", lambda e: e.activation(out=st[0:n, 2 * t + 1:2 * t + 2], in_=st[0:n, 2 * t + 1:2 * t + 2], func=AF.Exp, scale=-0.5),
                     r=[d_st], w=[d_st])
                S.op("dve", lambda e: e.tensor_scalar(out=xn[0:n, :], in0=xm(t)[0:n, :], scalar1=st[0:n, 2 * t + 1:2 * t + 2], scalar2=None, op0=ALU.mult),
                     r=[d_xm[t], d_st], w=[d_xn])
                for g in range(4):
                    for j in range(4):
                        kc = g * 4 + j
                        S.op("pe", lambda e: e.transpose(out=PS[g][:, j * 128:j * 128 + n], in_=xn[0:n, kc * 128:(kc + 1) * 128], identity=cst[0:n, 0, 0:n]),
                             r=[d_xn, d_cst], w=[dPS[g]], signal=(j == 3))
                    for j in range(4):
                        kc = g * 4 + j
                        if g % 2 == 0:
                            S.op("act", lambda e: e.activation(out=h2T[:, kc, t * 128:t * 128 + n], in_=PS[g][:, j * 128:j * 128 + n], func=AF.Identity,
                                                               bias=b2v[:, kc, cvi:cvi + 1], scale=s2v[:, kc, cvi:cvi + 1]),
                                 r=[dPS[g], d_small], w=[d_h2])
                        else:
                            S.op("dve", lambda e: e.tensor_scalar(out=h2T[:, kc, t * 128:t * 128 + n], in0=PS[g][:, j * 128:j * 128 + n],
                                                                  scalar1=s2v[:, kc, cvi:cvi + 1], scalar2=b2v[:, kc, cvi:cvi + 1], op0=ALU.mult, op1=ALU.add),
                                 r=[dPS[g], d_small], w=[d_h2])
            S.barrier()

        S.mark("phase5")
        with arena.scope() as ph:
            convf = alloc(ph, "convf", (128, 88, 3)); d_cf = Dep()
            S.dma("sp", convf[:], CONVF, w=[d_cf])
            wub = [(alloc(ph, "wu%d" % i, (128, 16, 512), BF16), Dep()) for i in range(2)]
            wdb = [(alloc(ph, "wd%d" % i, (128, 4, 512), BF16), Dep()) for i in range(2)]
            aTg = alloc(ph, "aT", (128, 4, 1024), BF16); d_aT = [Dep() for _ in range(4)]
            ur = [(alloc(ph, "ur%d" % i, (128, 1032)), Dep()) for i in range(2)]
            yc = [[(alloc(ph, "yc%d_%d" % (i, k), (128, 1024)), Dep()) for k in range(2)] for i in range(2)]
            tmpc = alloc(ph, "tmpc", (128, 1024)); d_tc = Dep()
            tmpd = [(alloc(ph, "tmpd%d" % i, (128, 512)), Dep()) for i in range(2)]
            segs = [(0, 256), (256, 512), (512, 1024)]
            FP = [(0, 512), (512, 1024), (1024, 1025)]
            dctr = [0]
            wu_cur = [None]
            wu_stream = WStream(S, wub, [WUP[r] for r in range(22)], 1)
            wd_stream = WStream(S, wdb, [WDOWN[g_, n_] for g_ in range(11) for n_ in range(4)], 1)
            wu_stream.prefetch()

            def ffn_up(j):
                r_, cc = j // 2, j % 2
                if cc == 0:
                    wu_cur[0] = wu_stream.get()
                if j % 4 == 2:
                    wd_stream.prefetch(0)
                if j % 4 == 3:
                    wd_stream.prefetch(1)
                wu, d_wu = wu_cur[0]
                for gv in range(2):
                    co = gv * 256 + cc * 128
                    cidx = gv * 44 + j
                    u_, d_u = ur[gv]
                    y_, d_yc = yc[gv][j % 2]
                    for pi, (c0, c1) in enumerate(FP):
                        pb = pi
                        n = c1 - c0
                        for kc in range(16):
                            S.op("pe", lambda e: e.matmul(PS[pb][:, 0:n], lhsT=wu[:, kc, co:co + 128], rhs=h2T[:, kc, c0:c1],
                                                          start=(kc == 0), stop=(kc == 15)),
                                 r=[d_wu, d_h2], w=[dPS[pb]], signal=(kc == 15))
                        S.op("act", lambda e: e.activation(out=u_[:, c0:c1], in_=PS[pb][:, 0:n], func=AF.Identity), r=[dPS[pb]], w=[d_u])
                    cw = convf[:, cidx, :]
                    S.op("dve", lambda e: e.tensor_scalar(out=y_[:], in0=u_[:, 0:1024], scalar1=cw[:, 1:2], scalar2=None, op0=ALU.mult),
                         r=[d_u, d_cf], w=[d_yc])
                    S.op("act", lambda e: e.activation(out=tmpc[:], in_=u_[:, 1:1025], func=AF.Identity, scale=cw[:, 2:3]),
                         r=[d_u, d_cf], w=[d_tc])
                    for (a_, b_) in segs:
                        S.op("dve", lambda e: e.scalar_tensor_tensor(out=y_[:, a_ + 1:b_], in0=u_[:, a_:b_ - 1], scalar=cw[:, 0:1],
                                                                     in1=y_[:, a_ + 1:b_], op0=ALU.mult, op1=ALU.add),
                             r=[d_u, d_cf], w=[d_yc])
                    for (a_, b_) in segs:
                        b2 = b_ - 1 if b_ < 1024 else 1024
                        S.op("pool", lambda e: e.tensor_tensor(out=y_[:, a_:b2], in0=y_[:, a_:b2], in1=tmpc[:, a_:b2], op=ALU.add),
                             r=[d_tc], w=[d_yc])

            def ffn_fin(j):
                yg, d_g = yc[0][j % 2]
                yv_, d_v = yc[1][j % 2]
                S.op("act", lambda e: e.activation(out=yg[:], in_=yg[:], func=AF.Silu), r=[d_g], w=[d_g])
                S.op("dve", lambda e: e.tensor_tensor(out=aTg[:, j % 4, :], in0=yg[:], in1=yv_[:], op=ALU.mult),
                     r=[d_g, d_v], w=[d_aT[j % 4]])

            def ffn_down(grp):
                for nb in range(4):
                    wd, d_wd = wd_stream.get()
                    for t in range(8):
                        cvi = 0 if t < 4 else 1
                        pb = 3 + dctr[0] % 5
                        tm, d_tm = tmpd[dctr[0] % 2]
                        dctr[0] += 1
                        for jj in range(4):
                            S.op("pe", lambda e: e.matmul(PS[pb][:], lhsT=aTg[:, jj, t * 128:(t + 1) * 128], rhs=wd[:, jj, :],
                                                          start=(jj == 0), stop=(jj == 3)),
                                 r=[d_aT[jj], d_wd], w=[dPS[pb]], signal=(jj == 3))
                        S.op("dve", lambda e: e.tensor_tensor(out=tm[:], in0=PS[pb][:], in1=gbc[:, cvi, nb * 512:(nb + 1) * 512], op=ALU.mult),
                             r=[dPS[pb], d_gbc], w=[d_tm])
                        S.op("pool", lambda e: e.tensor_tensor(out=xmid[:, t, nb * 512:(nb + 1) * 512], in0=xmid[:, t, nb * 512:(nb + 1) * 512], in1=tm[:], op=ALU.add),
                             r=[d_tm], w=[d_xm[t]])

            for j in range(44):
                ffn_up(j)
                if j >= 1:
                    ffn_fin(j - 1)
                    if (j - 1) % 4 == 3:
                        ffn_down((j - 1) // 4)
            ffn_fin(43)
            ffn_down(10)
            S.barrier()

        S.mark("phase6")
        with arena.scope() as ph:
            fg = alloc(ph, "fg", (128, 2048)); d_fg = Dep()
            S.dma("sp", fg[:], FINALG.partition_broadcast(128), w=[d_fg])
            yo = [(alloc(ph, "yo%d" % i, (128, 2048)), Dep()) for i in range(2)]
            st = alloc(ph, "fst", (128, 16)); d_st = Dep()
            S.op("dve", lambda e: e.memset(st[:], 0.0), w=[d_st])
            for t in range(8):
                y_, d_yo = yo[t % 2]
                S.op("act", lambda e: e.activation(out=y_[:], in_=xmid[:, t, :], func=AF.Square, accum_out=st[:, 2 * t:2 * t + 1]),
                     r=[d_xm[t]], w=[d_yo, d_st])
                S.op("act", lambda e: e.activation(out=st[:, 2 * t + 1:2 * t + 2], in_=st[:, 2 * t:2 * t + 1], func=AF.Ln, bias=EPS, scale=1.0 / D),
                     r=[d_st], w=[d_st])
                S.op("act", lambda e: e.activation(out=st[:, 2 * t + 1:2 * t + 2], in_=st[:, 2 * t + 1:2 * t + 2], func=AF.Exp, scale=-0.5),
                     r=[d_st], w=[d_st])
                S.op("dve", lambda e: e.scalar_tensor_tensor(out=y_[:], in0=xmid[:, t, :], scalar=st[:, 2 * t + 1:2 * t + 2], in1=fg[:],
                                                             op0=ALU.mult, op1=ALU.mult), r=[d_xm[t], d_st, d_fg], w=[d_yo])
                dst = YP[t * 128:(t + 1) * 128, :] if t < 4 else YS[(t - 4) * 128:(t - 3) * 128, :]
                ev = S.dma("sp", dst, y_[:], r=[d_yo])
                out_deps.append(ev)
        for ev in out_deps:
            S._wait("sp", ev)
        S.mark("end")
        print("ops", S.nops, "waits", S.nwaits, "sim", S.simulate()[:2])
        if os.environ.get("KMARKS"):
            import json
            json.dump(S.marks, open(os.environ["KMARKS"], "w"))
    return nc


_NC_CACHE = {}


def kernel(**inputs):
    maps = _prep(inputs)
    stop = os.environ.get("KSTOP")
    if stop not in _NC_CACHE:
        _NC_CACHE[stop] = build(stop)
    nc = _NC_CACHE[stop]
    ncr = int(os.environ.get("KCORES", str(NCORES)))
    res = run_bass_kernel_spmd(nc, maps[:ncr], core_ids=list(range(ncr)))
    R = list(res.results) + [res.results[0]] * (NCORES - ncr)
    y_prompt = np.zeros((16, 256, 2048), np.float32)
    y_sample = np.zeros((4, 1024, 2048), np.float32)
    nk = np.zeros((16, 1, 256, 8, 128), np.float32)
    nv = np.zeros((16, 1, 256, 8, 128), np.float32)
    nsf = np.zeros((16, 1, 8, 128, 128), np.float32)
    nsb = np.zeros((16, 1, 8, 128, 128), np.float32)
    for c in range(NCORES):
        b, par = c // 2, c % 2
        r = R[c]
        yp = r["YP"].reshape(2, 256, 2048)
        ys = r["YS"]
        k = r["NK"].reshape(2, 256, 8, 128)
        v = r["NV"].reshape(2, 256, 8, 128)
        f, bw = r["NSF"], r["NSB"]
        if par:
            yp, k, v = yp[:, ::-1], k[:, ::-1], v[:, ::-1]
            ys = ys[::-1]
            f, bw = bw, f
            y_sample[b, 512:1024] = ys
        else:
            y_sample[b, 0:512] = ys
        y_prompt[2 * c:2 * c + 2] = yp
        nk[2 * c:2 * c + 2, 0] = k
        nv[2 * c:2 * c + 2, 0] = v
        nsf[2 * c:2 * c + 2, 0] = f
        nsb[2 * c:2 * c + 2, 0] = bw
    return (y_prompt, y_sample, nk, nv, nsf, nsb)
```
